# Optimizing a Trainium2 kernel written in Bass

```python
import jax, jax.numpy as jnp
from jax import lax
import numpy as np

D_MODEL = 1024
BATCH = 8
SEQ = 2048
DEPTH = 1

N_MEM = 256
D_MIX = D_MODEL
D_HGRN = D_MIX // 2
D_POOL = D_MIX - D_HGRN
HGRN_HEADS = 4
HGRN_HEAD_DIM = D_HGRN // HGRN_HEADS
CHUNK = 64
POOL_WINDOWS = (2, 4, 8, 16)
POOL_GROUPS = len(POOL_WINDOWS)
POOL_GROUP_DIM = D_POOL // POOL_GROUPS
D_IN_PROJ = 4 * D_HGRN + D_POOL
D_FF = ((8 * D_MODEL // 3 + 255) // 256) * 256
XA_HEADS = 4
XA_HEAD_DIM = D_MODEL // XA_HEADS
ALPHA = (2.0 * DEPTH) ** 0.25
BETA = (8.0 * DEPTH) ** -0.25
LN_EPS = 1e-5
RMS_EPS = 1e-6

kernel_name = "hymba_hgrn2_pool_macaron_deepnorm"


def _layernorm(x, g, b):
    xf = x.astype(jnp.float32)
    mu = jnp.mean(xf, axis=-1, keepdims=True)
    var = jnp.mean(jnp.square(xf - mu), axis=-1, keepdims=True)
    return ((xf - mu) * lax.rsqrt(var + LN_EPS) * g.astype(jnp.float32) + b.astype(jnp.float32)).astype(x.dtype)


def _swiglu(h, w_in, w_out):
    gate, up = jnp.split(h @ w_in, 2, axis=-1)
    return (jax.nn.silu(gate) * up) @ w_out


def _hgrn2_chunk_scan(q, k, v, log_f):
    b_, s_, h_, dk = q.shape
    dv = v.shape[-1]
    nc = s_ // CHUNK

    def to_chunks(t):
        return t.reshape(b_, nc, CHUNK, h_, t.shape[-1]).transpose(1, 0, 3, 2, 4)

    qc, kc, vc, gc = to_chunks(q), to_chunks(k), to_chunks(v), to_chunks(log_f)
    causal = jnp.tril(jnp.ones((CHUNK, CHUNK), dtype=bool))[:, :, None]

    def step(state, inp):
        qi, ki, vi, gi = inp
        cum = jnp.cumsum(gi, axis=2)
        diff = cum[:, :, :, None, :] - cum[:, :, None, :, :]
        decay = jnp.exp(jnp.where(causal, diff, -jnp.inf))
        scores = jnp.einsum('bhtk,bhsk,bhtsk->bhts', qi, ki, decay)
        o = (jnp.einsum('bhts,bhsv->bhtv', scores, vi)
             + jnp.einsum('bhtk,bhkv->bhtv', qi * jnp.exp(cum), state))
        last = cum[:, :, -1:, :]
        state = (jnp.exp(last[:, :, 0, :])[..., None] * state
                 + jnp.einsum('bhsk,bhsv->bhkv', ki * jnp.exp(last - cum), vi))
        return state, o

    init = jnp.zeros((b_, h_, dk, dv), jnp.float32)
    _, oc = lax.scan(step, init, (qc, kc, vc, gc))
    return oc.transpose(1, 0, 3, 2, 4).reshape(b_, s_, h_, dv)


def _causal_multiscale_pool(v, pool_w, pool_scale):
    b_, s_, _ = v.shape
    vf = v.astype(jnp.float32)
    cs = jnp.cumsum(vf, axis=1)
    cs_pad = jnp.concatenate([jnp.zeros((b_, 1, D_POOL), jnp.float32), cs], axis=1)
    pos = jnp.arange(1, s_ + 1, dtype=jnp.int32)
    outs = []
    for g, w in enumerate(POOL_WINDOWS):
        sl = slice(g * POOL_GROUP_DIM, (g + 1) * POOL_GROUP_DIM)
        upper = cs_pad[:, 1:, sl]
        lower = jnp.concatenate([jnp.zeros((b_, w - 1, POOL_GROUP_DIM), jnp.float32),
                                 cs_pad[:, :s_ - w + 1, sl]], axis=1)
        count = jnp.minimum(pos, w).astype(jnp.float32)[None, :, None]
        outs.append((upper - lower) / count - vf[:, :, sl])
    pooled = jnp.stack(outs, axis=2)
    mixed = jnp.einsum('bsgc,gcd->bsgd', pooled, pool_w.astype(jnp.float32))
    return mixed.reshape(b_, s_, D_POOL) * pool_scale.astype(jnp.float32)


def _parallel_mixer(h, w_in, lb, gnorm, pool_w, pool_scale, w_out):
    b_, s_, _ = h.shape
    proj = h @ w_in
    q, f, i, g, v = jnp.split(proj, [D_HGRN, 2 * D_HGRN, 3 * D_HGRN, 4 * D_HGRN], axis=-1)
    q = jax.nn.silu(q.astype(jnp.float32))
    forget = lb + (1.0 - lb) * jax.nn.sigmoid(f.astype(jnp.float32))
    key = 1.0 - forget
    log_f = jnp.log(forget)
    heads = lambda t: t.reshape(b_, s_, HGRN_HEADS, HGRN_HEAD_DIM)
    o = _hgrn2_chunk_scan(heads(q), heads(key), heads(i.astype(jnp.float32)), heads(log_f))
    o = o * lax.rsqrt(jnp.mean(jnp.square(o), axis=-1, keepdims=True) + RMS_EPS) * gnorm.astype(jnp.float32)
    o = o.reshape(b_, s_, D_HGRN) * jax.nn.silu(g.astype(jnp.float32))
    p = _causal_multiscale_pool(v, pool_w, pool_scale)
    merged = jnp.concatenate([o, p], axis=-1).astype(h.dtype)
    return merged @ w_out


def _memory_attention(h, mem, wq, wk, wv, wo):
    b_, s_, _ = h.shape
    m_ = mem.shape[1]
    q = (h @ wq).reshape(b_, s_, XA_HEADS, XA_HEAD_DIM)
    k = (mem @ wk).reshape(b_, m_, XA_HEADS, XA_HEAD_DIM)
    v = (mem @ wv).reshape(b_, m_, XA_HEADS, XA_HEAD_DIM)
    s = jnp.einsum('bshd,bmhd->bhsm', q, k).astype(jnp.float32) * (XA_HEAD_DIM ** -0.5)
    p = jax.nn.softmax(s, axis=-1).astype(v.dtype)
    o = jnp.einsum('bhsm,bmhd->bshd', p, v).reshape(b_, s_, D_MODEL)
    return o @ wo


def setup_inputs(seed: int = 0) -> dict:
    key = jax.random.key(seed)
    ks = jax.random.split(key, 26)
    L = DEPTH
    nrm = lambda k, shape, scale: jax.random.normal(k, shape, jnp.float32) * scale
    return {
        "x": nrm(ks[0], (BATCH, SEQ, D_MODEL), 1.0),
        "mem": nrm(ks[1], (BATCH, N_MEM, D_MODEL), 1.0),
        "w_ffn1_in": nrm(ks[2], (L, D_MODEL, 2 * D_FF), D_MODEL ** -0.5),
        "w_ffn1_out": nrm(ks[3], (L, D_FF, D_MODEL), BETA * D_FF ** -0.5),
        "ln1_g": 1.0 + nrm(ks[4], (L, D_MODEL), 0.02),
        "ln1_b": nrm(ks[5], (L, D_MODEL), 0.02),
        "w_mix_in": nrm(ks[6], (L, D_MODEL, D_IN_PROJ), D_MODEL ** -0.5),
        "hgrn_lb": nrm(ks[7], (L + 1, D_HGRN), 0.5),
        "hgrn_gnorm": 1.0 + nrm(ks[8], (L, HGRN_HEAD_DIM), 0.02),
        "pool_w": nrm(ks[9], (L, POOL_GROUPS, POOL_GROUP_DIM, POOL_GROUP_DIM), POOL_GROUP_DIM ** -0.5),
        "pool_scale": 1.0 + nrm(ks[10], (L, D_POOL), 0.1),
        "w_mix_out": nrm(ks[11], (L, D_MIX, D_MODEL), BETA * D_MIX ** -0.5),
        "ln2_g": 1.0 + nrm(ks[12], (L, D_MODEL), 0.02),
        "ln2_b": nrm(ks[13], (L, D_MODEL), 0.02),
        "xa_wq": nrm(ks[14], (L, D_MODEL, D_MODEL), D_MODEL ** -0.5),
        "xa_wk": nrm(ks[15], (L, D_MODEL, D_MODEL), D_MODEL ** -0.5),
        "xa_wv": nrm(ks[16], (L, D_MODEL, D_MODEL), BETA * D_MODEL ** -0.5),
        "xa_wo": nrm(ks[17], (L, D_MODEL, D_MODEL), BETA * D_MODEL ** -0.5),
        "ln3_g": 1.0 + nrm(ks[18], (L, D_MODEL), 0.02),
        "ln3_b": nrm(ks[19], (L, D_MODEL), 0.02),
        "w_ffn2_in": nrm(ks[20], (L, D_MODEL, 2 * D_FF), D_MODEL ** -0.5),
        "w_ffn2_out": nrm(ks[21], (L, D_FF, D_MODEL), BETA * D_FF ** -0.5),
        "ln4_g": 1.0 + nrm(ks[22], (L, D_MODEL), 0.02),
        "ln4_b": nrm(ks[23], (L, D_MODEL), 0.02),
    }


def reference(x, mem, w_ffn1_in, w_ffn1_out, ln1_g, ln1_b, w_mix_in, hgrn_lb, hgrn_gnorm,
              pool_w, pool_scale, w_mix_out, ln2_g, ln2_b, xa_wq, xa_wk, xa_wv, xa_wo,
              ln3_g, ln3_b, w_ffn2_in, w_ffn2_out, ln4_g, ln4_b):
    lower_bounds = jnp.cumsum(jax.nn.softmax(hgrn_lb.astype(jnp.float32), axis=0), axis=0)
    h = x
    for l in range(DEPTH):
        h = _layernorm(ALPHA * h + 0.5 * _swiglu(h, w_ffn1_in[l], w_ffn1_out[l]), ln1_g[l], ln1_b[l])
        h = _layernorm(ALPHA * h + _parallel_mixer(h, w_mix_in[l], lower_bounds[l], hgrn_gnorm[l],
                                                  pool_w[l], pool_scale[l], w_mix_out[l]),
                       ln2_g[l], ln2_b[l])
        h = _layernorm(ALPHA * h + _memory_attention(h, mem, xa_wq[l], xa_wk[l], xa_wv[l], xa_wo[l]),
                       ln3_g[l], ln3_b[l])
        h = _layernorm(ALPHA * h + 0.5 * _swiglu(h, w_ffn2_in[l], w_ffn2_out[l]), ln4_g[l], ln4_b[l])
    return h
```

```python
from contextlib import ExitStack
import numpy as np
import concourse.bass as bass
import concourse.mybir as mybir
from concourse.bass_utils import run_bass_kernel_spmd

F32 = mybir.dt.float32
BF16 = mybir.dt.bfloat16
I32 = mybir.dt.int32
AF = mybir.ActivationFunctionType
ALU = mybir.AluOpType

D = 1024
KC = 8
DFF = 2816
JC = 22
NMEM = 256
ALPHA = 2.0 ** 0.25
LN_EPS = 1e-5
RMS_EPS = 1e-6
POOL_W = (2, 4, 8, 16)
NCOLS = 80
C_LN = 0
C_PSC = 64
C_GN = 68
C_LBA = 69
C_LBB = 73


class Prog:
    ENGS = ("pe", "act", "dve", "pool", "sp")

    def __init__(self, nc):
        self.nc = nc
        self.ops = {e: [] for e in self.ENGS}
        self.count = {e: 0 for e in self.ENGS}
        self.sem = {}
        self.dsem = {}
        self.last_w = {}
        self.readers = {}
        self.waited = {e: {} for e in self.ENGS}
        self.guard = []
        self.seen = set()
        self.n_waits = 0

    def _yield(self):
        il = getattr(self, "_il", None)
        if il is None:
            return
        import threading
        i = il["tl"].__dict__.get("idx")
        if i is None:
            return
        il["left"][i] -= 1
        if il["left"][i] > 0:
            return
        il["main"].release()
        il["sems"][i].acquire()

    def interleave(self, fns, weights=None):
        import threading
        fns = [f for f in fns if f is not None]
        if not fns:
            return
        n = len(fns)
        weights = list(weights or [1] * n)
        il = dict(tl=threading.local(), sems=[threading.Semaphore(0) for _ in range(n)],
                  main=threading.Semaphore(0), left=[0] * n, alive=[True] * n, err=[])
        self._il = il

        def runner(i):
            il["tl"].idx = i
            il["sems"][i].acquire()
            try:
                fns[i]()
            except BaseException as ex:
                il["err"].append(ex)
            il["alive"][i] = False
            il["tl"].idx = None
            il["main"].release()

        ths = [threading.Thread(target=runner, args=(i,), daemon=True) for i in range(n)]
        for t in ths:
            t.start()
        while any(il["alive"]):
            for i in range(n):
                if il["alive"][i]:
                    il["left"][i] = weights[i]
                    il["sems"][i].release()
                    il["main"].acquire()
                    if il["err"]:
                        self._il = None
                        raise il["err"][0]
        for t in ths:
            t.join()
        self._il = None

    def _sem_for(self, tok):
        if tok[0] == "eng":
            return self.sem[tok[1]], tok[2]
        return self.dsem[tok[1]][0], tok[2]

    def new_phase(self):
        g = {}
        for d in (self.last_w,):
            for t in d.values():
                n = t[0] + ":" + t[1]
                if n not in g or g[n][2] < t[2]:
                    g[n] = t
        for lst in self.readers.values():
            for t in lst:
                n = t[0] + ":" + t[1]
                if n not in g or g[n][2] < t[2]:
                    g[n] = t
        self.guard = list(g.values())
        self.seen = set()

    def _collect(self, eng, reads, writes):
        deps = []
        for r in reads:
            t = self.last_w.get(r)
            if t is not None:
                deps.append(t)
        for w in writes:
            t = self.last_w.get(w)
            if t is not None:
                deps.append(t)
            deps.extend(self.readers.get(w, ()))
            if w not in self.seen:
                self.seen.add(w)
                deps.extend(self.guard)
        need = {}
        for t in deps:
            if t[0] == "eng" and t[1] == eng and eng == "pe":
                continue
            name = t[0] + ":" + t[1]
            v = t[2]
            if self.waited[eng].get(name, 0) >= v:
                continue
            if name not in need or need[name][2] < v:
                need[name] = t
        for name, t in need.items():
            self.waited[eng][name] = t[2]
        return list(need.values())

    def _commit(self, tok, reads, writes):
        for r in reads:
            self.readers.setdefault(r, []).append(tok)
        for w in writes:
            self.last_w[w] = tok
            self.readers[w] = []

    def op(self, eng, fn, reads=(), writes=()):
        waits = self._collect(eng, reads, writes)
        self.count[eng] += 1
        seq = self.count[eng]
        self.n_waits += len(waits)

        def emit(e, fn=fn, waits=waits, eng=eng):
            for t in waits:
                e.wait_ge(*self._sem_for(t))
            ins = fn(e)
            ins.then_inc(self.sem[eng], 1)

        self.ops[eng].append(emit)
        tok = ("eng", eng, seq)
        self._commit(tok, reads, writes)
        self._yield()
        return tok

    def dma(self, eng, key, out, in_, reads=(), writes=(), **kw):
        waits = self._collect(eng, reads, writes)
        if key not in self.dsem:
            self.dsem[key] = [None, 0]
        self.dsem[key][1] += 16
        cnt = self.dsem[key][1]
        self.n_waits += len(waits)

        def emit(e, waits=waits, key=key, out=out, in_=in_, kw=kw):
            for t in waits:
                e.wait_ge(*self._sem_for(t))
            e.dma_start(out=out, in_=in_, **kw).then_inc(self.dsem[key][0], 16)

        self.ops[eng].append(emit)
        tok = ("dma", key, cnt)
        self._commit(tok, reads, writes)
        self._yield()
        return tok

    def final_wait(self, eng, toks):
        def emit(e, toks=list(toks)):
            for t in toks:
                e.wait_ge(*self._sem_for(t))
        self.ops[eng].append(emit)

    def run(self, stack):
        nc = self.nc
        for e in self.ENGS:
            self.sem[e] = stack.enter_context(nc.semaphore("prog_" + e))
        for k in self.dsem:
            self.dsem[k][0] = stack.enter_context(nc.semaphore("dma_" + k))
        block = stack.enter_context(nc.Block())

        @block.tensor
        def _(e):
            for f in self.ops["pe"]:
                f(e)

        @block.scalar
        def _(e):
            for f in self.ops["act"]:
                f(e)

        @block.vector
        def _(e):
            for f in self.ops["dve"]:
                f(e)

        @block.gpsimd
        def _(e):
            for f in self.ops["pool"]:
                f(e)

        @block.sync
        def _(e):
            for f in self.ops["sp"]:
                f(e)


def build(S=2048, TG=512, U=256, stop_after=4):
    assert S % (2 * TG) == 0 and TG % U == 0 and U % 128 == 0
    nc = bass.Bass("TRN2", target_bir_lowering=False)
    NT = S // 128
    HALF = S // 2
    NTGH = HALF // TG

    import os
    _u = "u" in os.environ.get("KDBG", "")
    _names = []

    def din(name, shape):
        if _u and stop_after == 0 and name not in ("x", "cols"):
            return None
        _names.append(name)
        return nc.dram_tensor(name, list(shape), F32, kind="ExternalInput").ap()

    x_d = din("x", [S, D])
    mem_d = din("mem", [NMEM, D])
    wf_in_d = [din("wf1_in", [11, 128, KC * 512]), din("wf2_in", [11, 128, KC * 512])]
    wf_out_d = [din("wf1_out", [8, 128, JC * 128]), din("wf2_out", [8, 128, JC * 128])]
    wmi_d = din("wmi", [5, 128, KC * 512])
    wmo_d = din("wmo", [2, 128, KC * 512])
    wq_d = din("wq", [2, 128, KC * 512])
    wk_d = din("wk", [2, 128, KC * 512])
    wv_d = din("wv", [2, 128, KC * 512])
    wo_d = din("wo", [2, 128, KC * 512])
    poolw_d = din("poolw", [128, 512])
    cols_d = din("cols", [128, NCOLS])
    out_d = nc.dram_tensor("out", [S, D], F32, kind="ExternalOutput").ap()

    st = ExitStack()
    with st:
        AW = 53200
        arena = st.enter_context(nc.sbuf_tensor("arena", [128, AW], F32))
        psb = [st.enter_context(nc.psum_tensor(f"ps{i}", [128, 512], F32))[:] for i in range(8)]
        P = Prog(nc)

        class Alloc:
            def __init__(self, base):
                self.off = base

            def f32(self, *shape):
                n = int(np.prod(shape))
                v = arena[:, self.off:self.off + n]
                self.off += n
                assert self.off <= AW, f"arena overflow {self.off}"
                if len(shape) == 2:
                    return v.rearrange("p (a b) -> p a b", a=shape[0])
                if len(shape) == 3:
                    return v.rearrange("p (a b c) -> p a b c", a=shape[0], b=shape[1])
                return v

            def bf16(self, *shape):
                n = int(np.prod(shape))
                assert n % 2 == 0
                v = arena[:, self.off:self.off + n // 2].bitcast(BF16)
                self.off += n // 2
                assert self.off <= AW, f"arena overflow {self.off}"
                if len(shape) == 2:
                    return v.rearrange("p (a b) -> p a b", a=shape[0])
                if len(shape) == 3:
                    return v.rearrange("p (a b c) -> p a b c", a=shape[0], b=shape[1])
                return v

        A = Alloc(0)
        RT = A.f32(KC, S)
        XT = A.bf16(KC, S)
        identF = A.f32(128)
        onesD = A.bf16(128)
        onesV = A.bf16(128)
        ones1 = A.bf16(128)
        mask2 = A.f32(128)
        cols = A.f32(NCOLS)
        lbc = A.f32(4)
        omlc = A.f32(4)
        rcfix = A.f32(4, 16)
        iot = A.f32(16)
        ioti = arena[:, A.off:A.off + 16].bitcast(I32)
        A.off += 16
        scanmask = A.f32(U)
        mean_off = A.off
        mean = A.f32(TG)
        m2 = A.f32(TG)
        rstd = A.f32(TG)
        zsq_off = A.off
        zsq = A.bf16(KC, TG)
        PH = A.off

        def RK(name, cs, t0, n):
            return [(name, c, tt) for c in cs for tt in range(t0 // 128, (t0 + n + 127) // 128)]

        ALLC = list(range(KC))
        bank_rr = [0]

        def nextbank():
            b = bank_rr[0]
            bank_rr[0] = (b + 1) % 4
            return b

        def mmgroup(out, pairs):
            def fn(e, out=out, pairs=pairs):
                n = len(pairs)
                ins = None
                for i, (l, r) in enumerate(pairs):
                    ins = e.matmul(out, l, r, start=(i == 0), stop=(i == n - 1))
                return ins
            return fn

        def wload(key, dst, src, reskey, rows=None):
            P.dma("pool", key, dst, src, writes=[reskey])

        P.dma("sp", "cols", cols, cols_d, writes=["cols"])
        P.op("pool", lambda e: e.memset(identF, 0.0), writes=["identF"])
        P.op("pool", lambda e: e.affine_select(out=identF, in_=identF, pattern=[[-1, 128]], compare_op=ALU.not_equal,
                                               fill=1.0, base=0, channel_multiplier=1),
             reads=["identF"], writes=["identF"])
        import os
        DBG = os.environ.get("KDBG", "")
        P.op("pool", lambda e: e.memset(onesD, 1.0 / 1024.0), writes=["onesD"])
        P.op("pool", lambda e: e.memset(onesV, 1.0 / 128.0), writes=["onesV"])
        P.op("pool", lambda e: e.memset(ones1, 1.0), writes=["ones1"])
        if "m" not in DBG:
            P.op("pool", lambda e: e.memset(mask2, 1.0), writes=["mask2"])
            P.op("pool", lambda e: e.affine_select(out=mask2, in_=mask2, pattern=[[1, 128]], compare_op=ALU.is_ge,
                                                   fill=0.0, base=0, channel_multiplier=-1),
                 reads=["mask2"], writes=["mask2"])
            P.op("pool", lambda e: e.memset(mask2[0:64, 64:128], 0.0), reads=["mask2"], writes=["mask2"])
        if "s" not in DBG:
            P.op("pool", lambda e: e.memset(scanmask, 1.0), writes=["scanmask"])
            P.op("pool", lambda e: e.memset(scanmask.rearrange("p (a b) -> p a b", b=64)[:, :, 0:1], 0.0),
                 reads=["scanmask"], writes=["scanmask"])
        if "i" not in DBG:
            P.op("pool", lambda e: e.iota(ioti, pattern=[[1, 16]], base=1, channel_multiplier=0), writes=["ioti"])
            P.op("pool", lambda e: e.tensor_copy(iot, ioti), reads=["ioti"], writes=["iot"])
            for g, w in enumerate(POOL_W):
                P.op("dve", lambda e, g=g, w=w: e.tensor_scalar_min(rcfix[:, g, :], iot, float(w)),
                     reads=["iot"], writes=[("rcfix", g)])
                P.op("dve", lambda e, g=g: e.reciprocal(rcfix[:, g, :], rcfix[:, g, :]),
                     reads=[("rcfix", g)], writes=[("rcfix", g)])
        if "l" not in DBG:
            P.op("dve", lambda e: e.tensor_tensor(lbc, cols[:, C_LBA:C_LBA + 4], cols[:, C_LBB:C_LBB + 4], ALU.subtract),
                 reads=["cols"], writes=["lbc"])
            P.op("act", lambda e: e.activation(lbc, lbc, AF.Sigmoid), reads=["lbc"], writes=["lbc"])
            P.op("dve", lambda e: e.tensor_scalar(omlc, lbc, -1.0, 1.0, ALU.mult, ALU.add), reads=["lbc"], writes=["omlc"])

        zsq_flat = zsq.rearrange("p c t -> p (c t)")

        def layernorm(t0, n, gcol, bcol, cast_eng="dve"):
            zsq = zsq_flat[:, :KC * n].rearrange("p (c t) -> p c t", c=KC)
            tok = slice(t0, t0 + n)
            xk = RK("XT", ALLC, t0, n)
            HV = ([0, 1, 2, 3], [4, 5, 6, 7])
            for hi, cs in enumerate(HV):
                c0, c1 = cs[0], cs[-1] + 1
                P.op(cast_eng, lambda e, c0=c0, c1=c1: e.tensor_copy(XT[:, c0:c1, tok], RT[:, c0:c1, tok]),
                     reads=RK("RT", cs, t0, n), writes=RK("XT", cs, t0, n))
                P.op("act", lambda e, c0=c0, c1=c1: e.activation(zsq[:, c0:c1, :n], RT[:, c0:c1, tok], AF.Square),
                     reads=RK("RT", cs, t0, n), writes=[("zsq", hi)])
            P.op("pe", mmgroup(psb[6][:, :n], [(onesD, XT[:, c, tok]) for c in ALLC]),
                 reads=xk + ["onesD"], writes=[("ps", 6)])
            if n <= 256:
                msq_ps, msq_key = psb[6][:, 256:256 + n], ("ps", 6)
            else:
                msq_ps, msq_key = psb[7][:, :n], ("ps", 7)
            P.op("pe", mmgroup(msq_ps, [(onesD, zsq[:, c, :n]) for c in ALLC]),
                 reads=[("zsq", 0), ("zsq", 1), "onesD"], writes=[msq_key])
            P.op("act", lambda e: e.activation(mean[:, :n], psb[6][:, :n], AF.Identity),
                 reads=[("ps", 6)], writes=["mean"])
            P.op("dve", lambda e: e.tensor_tensor(m2[:, :n], mean[:, :n], mean[:, :n], ALU.mult),
                 reads=["mean"], writes=["m2"])
            P.op("dve", lambda e: e.tensor_tensor(m2[:, :n], msq_ps, m2[:, :n], ALU.subtract),
                 reads=[msq_key, "m2"], writes=["m2"])
            P.op("act", lambda e: e.activation(m2[:, :n], m2[:, :n], AF.Ln, bias=LN_EPS, scale=1.0),
                 reads=["m2"], writes=["m2"])
            P.op("act", lambda e: e.activation(rstd[:, :n], m2[:, :n], AF.Exp, scale=-0.5),
                 reads=["m2"], writes=["rstd"])
            for hi, cs in enumerate(HV):
                c0, c1 = cs[0], cs[-1] + 1
                rk = RK("RT", cs, t0, n)
                P.op("dve", lambda e, c0=c0, c1=c1: e.tensor_tensor(RT[:, c0:c1, tok], RT[:, c0:c1, tok],
                                                                    mean[:, :n].unsqueeze(1).to_broadcast([128, 4, n]), ALU.subtract),
                     reads=rk + ["mean"], writes=rk)
                P.op("dve", lambda e, c0=c0, c1=c1: e.tensor_tensor(RT[:, c0:c1, tok], RT[:, c0:c1, tok],
                                                                    rstd[:, :n].unsqueeze(1).to_broadcast([128, 4, n]), ALU.mult),
                     reads=rk + ["rstd"], writes=rk)
            for hi, cs in enumerate(HV):
                c0, c1 = cs[0], cs[-1] + 1
                for c in cs:
                    P.op("act", lambda e, c=c: e.activation(RT[:, c, tok], RT[:, c, tok], AF.Identity,
                                                            scale=cols[:, gcol + c:gcol + c + 1],
                                                            bias=cols[:, bcol + c:bcol + c + 1]),
                         reads=RK("RT", [c], t0, n) + ["cols"], writes=RK("RT", [c], t0, n))
                P.op(cast_eng, lambda e, c0=c0, c1=c1: e.tensor_copy(XT[:, c0:c1, tok], RT[:, c0:c1, tok]),
                     reads=RK("RT", cs, t0, n), writes=RK("XT", cs, t0, n))

        A.off = PH
        xstage = [A.f32(D), A.f32(D)]

        def load_x(tiles):
          for tt in tiles:
              sl = tt % 2
              P.dma("sp", f"xs{sl}", xstage[sl], x_d[tt * 128:(tt + 1) * 128, :], writes=[("xs", sl)])
              for hb in range(2):
                  b = 6 + hb

                  def trf(e, sl=sl, hb=hb, b=b):
                      ins = None
                      for q in range(4):
                          c = hb * 4 + q
                          ins = e.transpose(psb[b][:, q * 128:(q + 1) * 128], xstage[sl][:, c * 128:(c + 1) * 128], identF)
                      return ins
                  P.op("pe", trf, reads=[("xs", sl), "identF"], writes=[("ps", b)])
                  cs = list(range(hb * 4, hb * 4 + 4))
                  pv = psb[b].rearrange("p (a b) -> p a b", a=4)
                  P.op("dve", lambda e, hb=hb, tt=tt, pv=pv: e.tensor_copy(RT[:, hb * 4:hb * 4 + 4, tt * 128:(tt + 1) * 128], pv),
                       reads=[("ps", b)], writes=RK("RT", cs, tt * 128, 128))
                  if "A" not in DBG:
                      P.op("act", lambda e, hb=hb, tt=tt: e.activation(XT[:, hb * 4:hb * 4 + 4, tt * 128:(tt + 1) * 128],
                                                                       RT[:, hb * 4:hb * 4 + 4, tt * 128:(tt + 1) * 128], AF.Identity),
                           reads=RK("RT", cs, tt * 128, 128), writes=RK("XT", cs, tt * 128, 128))


        load_x(range(NT // 2))

        def ffn(idx, gcol, bcol, pre=None, newphase=False):
            if newphase:
                P.new_phase()
            A.off = PH + 2 * D
            H = A.bf16(JC, HALF)
            wi = [A.bf16(KC, 512), A.bf16(KC, 512)]
            wo = [A.bf16(JC, 128), A.bf16(JC, 128)]
            sg = [A.f32(TG), A.f32(TG)]
            w_in_d, w_out_d = wf_in_d[idx], wf_out_d[idx]
            cnt = [0, 0, 0]

            def Aph(half):
                t0h = half * HALF
                for jj in range(JC // 2):
                    sl = cnt[0] % 2
                    cnt[0] += 1
                    wload(f"wi{sl}", wi[sl].rearrange("p k n -> p (k n)").rearrange("p (r e) -> p r e", e=2048),
                          w_in_d[jj].rearrange("p (r e) -> p r e", e=2048), ("wi", sl))
                    for jl in range(2):
                        j = 2 * jj + jl
                        for tg in range(NTGH):
                            t0 = t0h + tg * TG
                            tok = slice(t0, t0 + TG)
                            i = cnt[1] % 2
                            cnt[1] += 1
                            gb, ub = i, 2 + i
                            xk = RK("XT", ALLC, t0, TG)
                            P.op("pe", mmgroup(psb[gb][:, :TG], [(wi[sl][:, k, jl * 128:(jl + 1) * 128], XT[:, k, tok]) for k in ALLC]),
                                 reads=[("wi", sl)] + xk, writes=[("ps", gb)])
                            P.op("pe", mmgroup(psb[ub][:, :TG], [(wi[sl][:, k, 256 + jl * 128:256 + (jl + 1) * 128], XT[:, k, tok]) for k in ALLC]),
                                 reads=[("wi", sl)] + xk, writes=[("ps", ub)])
                            P.op("act", lambda e, i=i, gb=gb: e.activation(sg[i], psb[gb][:, :TG], AF.Silu),
                                 reads=[("ps", gb)], writes=[("sg", i)])
                            P.op("dve", lambda e, i=i, ub=ub, j=j, tg=tg: e.scalar_tensor_tensor(
                                H[:, j, tg * TG:(tg + 1) * TG], sg[i], 0.5, psb[ub][:, :TG], ALU.mult, ALU.mult),
                                 reads=[("sg", i), ("ps", ub)], writes=[("H", j, tg)])

            def Bph(half):
                t0h = half * HALF
                for m in range(KC):
                    sl = cnt[2] % 2
                    cnt[2] += 1
                    wload(f"wo{sl}", wo[sl].rearrange("p j n -> p (j n)").rearrange("p (r e) -> p r e", e=1408),
                          w_out_d[m].rearrange("p (r e) -> p r e", e=1408), ("wo", sl))
                    for tg in range(NTGH):
                        t0 = t0h + tg * TG
                        tok = slice(t0, t0 + TG)
                        ob = 4 + (m * NTGH + tg) % 2
                        P.op("pe", mmgroup(psb[ob][:, :TG], [(wo[sl][:, j, :], H[:, j, tg * TG:(tg + 1) * TG]) for j in range(JC)]),
                             reads=[("wo", sl)] + [("H", j, tg) for j in range(JC)], writes=[("ps", ob)])
                        rk = RK("RT", [m], t0, TG)
                        P.op("dve", lambda e, m=m, tok=tok, ob=ob: e.scalar_tensor_tensor(
                            RT[:, m, tok], RT[:, m, tok], ALPHA, psb[ob][:, :TG], ALU.mult, ALU.add),
                             reads=[("ps", ob)] + rk, writes=rk)

            def LNs(half):
                for tg in range(NTGH):
                    layernorm(half * HALF + tg * TG, TG, gcol, bcol)

            if pre is not None:
                P.interleave([pre, lambda: Aph(0)], weights=[1, 6])
            else:
                Aph(0)
            Bph(0)
            P.interleave([lambda: LNs(0), lambda: Aph(1)], weights=[1, 6])
            Bph(1)
            return lambda: LNs(1)

        tail = None
        if stop_after >= 1:
            tail = ffn(0, C_LN + 0, C_LN + 8, pre=lambda: load_x(range(NT // 2, NT)))
        else:
            load_x(range(NT // 2, NT))

        def mixer(gcol, bcol, pre=None):
            P.new_phase()
            A.off = PH
            NTU = U // 128
            NCH = U // 64
            NUU = S // U
            wmi = A.bf16(KC, 2560)
            wmo = A.bf16(KC, D)
            poolw = A.bf16(512)
            vtok = [A.bf16(NTU, 512), A.bf16(NTU, 512)]
            MT = A.bf16(KC, U)
            fl_off = A.off
            F_, L_, C_, E_ = A.f32(U), A.f32(U), A.f32(U), A.f32(U)
            PAIR = (TG >= 2 * U)
            if PAIR:
                F2 = arena[:, fl_off:fl_off + 2 * U].rearrange("p (a b) -> p a b", a=2)
                L2 = arena[:, fl_off + 2 * U:fl_off + 4 * U].rearrange("p (a b) -> p a b", a=2)
                zu = zsq_off + KC * U // 2
                C2 = arena[:, zu:zu + 2 * U].rearrange("p (a b) -> p a b", a=2)
                E2 = arena[:, zu + 2 * U:zu + 4 * U].rearrange("p (a b) -> p a b", a=2)
                sgt2 = arena[:, mean_off:mean_off + 2 * TG].rearrange("p (a b) -> p a b", a=2)[:, :, U:2 * U]
            qd4 = [A.bf16(4, U), A.bf16(4, U)]
            kd4 = [A.bf16(4, U), A.bf16(4, U)]
            kltok4 = [A.bf16(4, U), A.bf16(4, U)]
            EL = [A.f32(4, NCH), A.f32(4, NCH)]
            sgt_default = mean[:, U:2 * U] if TG >= 2 * U else A.f32(U)
            Sf = A.f32(4, 128)
            Sb = A.bf16(4, 128)
            sT4 = A.bf16(4, 128)
            osq2 = A.bf16(2, U)
            rs2 = A.f32(2, U)
            sgg2 = A.f32(2, U)
            VP = [A.f32(16 + U), A.f32(16 + U), A.f32(16 + U)]
            halo = A.f32(4, 16)
            pooled = A.bf16(U)
            pfix = A.f32(16)
            OB = (3, 5)
            rrA = [0]
            rrC = [0]

            def nbA():
                return 0

            def nbC():
                rrC[0] ^= 1
                return (2, 7)[rrC[0]]

            def prologue():
                for b in (1, 0, 2, 3, 4):
                    wload(f"wmi{b}", wmi[:, :, b * 512:(b + 1) * 512], wmi_d[b].rearrange("p (k n) -> p k n", k=KC), ("wmi", b))
                for b in range(2):
                    wload(f"wmo{b}", wmo[:, :, b * 512:(b + 1) * 512], wmo_d[b].rearrange("p (k n) -> p k n", k=KC), ("wmo", b))
                wload("poolw", poolw, poolw_d, "poolw")
                P.op("dve", lambda e: e.memset(Sf, 0.0), writes=["Sf"])
                P.op("dve", lambda e: e.memset(Sb, 0.0), writes=["Sb"])
                P.op("dve", lambda e: e.memset(halo, 0.0), writes=[("halo", h) for h in range(4)])
                P.op("dve", lambda e: e.memset(VP[1], 0.0), writes=[("VP", 1)])
                P.op("dve", lambda e: e.memset(VP[2], 0.0), writes=[("VP", 2)])
                V(0)
                for h in range(4):
                    A1(0, h, sgt=VP[1][:, :U], sgk=("VP", 1))

            def V(u):
                vs = vtok[u % 2]
                for ti in range(NTU):
                    b = nbA()
                    ts0 = u * U + ti * 128
                    tsl = slice(ts0, ts0 + 128)
                    P.op("pe", mmgroup(psb[b], [(XT[:, k, tsl], wmi[:, k, 1024:1536]) for k in ALLC]),
                         reads=RK("XT", ALLC, ts0, 128) + [("wmi", 2)], writes=[("ps", b)])
                    P.op("act", lambda e, ti=ti, b=b, vs=vs: e.activation(vs[:, ti, :], psb[b], AF.Identity),
                         reads=[("ps", b)], writes=[("vtok", u % 2, ti)])

            def A1(u, h, sgt=None, sgk="sgt"):
                sgt = sgt_default if sgt is None else sgt
                s_ = u % 2
                t0 = u * U
                tok = slice(t0, t0 + U)
                xk = RK("XT", ALLC, t0, U)
                b = nbA()
                P.op("pe", mmgroup(psb[b][:, :U], [(wmi[:, k, 512 + h * 128:512 + (h + 1) * 128], XT[:, k, tok]) for k in ALLC]),
                     reads=xk + [("wmi", 1)], writes=[("ps", b)])
                P.op("act", lambda e, b=b: e.activation(F_, psb[b][:, :U], AF.Exp, scale=-1.0), reads=[("ps", b)], writes=["F_"])
                P.op("act", lambda e: e.activation(F_, F_, AF.Ln, bias=1.0, scale=1.0), reads=["F_"], writes=["F_"])
                P.op("act", lambda e: e.activation(F_, F_, AF.Exp, scale=-1.0), reads=["F_"], writes=["F_"])
                P.op("dve", lambda e, h=h: e.tensor_scalar(F_, F_, omlc[:, h:h + 1], lbc[:, h:h + 1], ALU.mult, ALU.add),
                     reads=["F_", "lbc", "omlc"], writes=["F_"])
                P.op("act", lambda e: e.activation(L_, F_, AF.Ln), reads=["F_"], writes=["L_"])
                P.op("dve", lambda e: e.tensor_scalar(F_, F_, -1.0, 1.0, ALU.mult, ALU.add), reads=["F_"], writes=["F_"])
                P.op("dve", lambda e: e.tensor_tensor_scan(C_, scanmask, L_, 0.0, ALU.mult, ALU.add),
                     reads=["L_", "scanmask"], writes=["C_"])
                P.op("act", lambda e: e.activation(E_, C_, AF.Exp), reads=["C_"], writes=["E_"])
                P.op("act", lambda e: e.activation(L_, C_, AF.Exp, scale=-1.0), reads=["C_"], writes=["L_"])
                P.op("dve", lambda e: e.tensor_tensor(L_, F_, L_, ALU.mult), reads=["F_", "L_"], writes=["L_"])
                P.op("pool", lambda e, s_=s_, h=h: e.tensor_copy(kd4[s_][:, h, :], L_),
                     reads=["L_"], writes=[("kd", s_, h)])
                P.op("pool", lambda e, s_=s_, h=h: e.tensor_copy(EL[s_][:, h, :], E_.rearrange("p (a b) -> p a b", b=64)[:, :, 63]),
                     reads=["E_"], writes=[("EL", s_)])
                P.op("dve", lambda e: e.tensor_tensor(
                    C_.rearrange("p (a b) -> p a b", b=64), L_.rearrange("p (a b) -> p a b", b=64),
                    E_.rearrange("p (a b) -> p a b", b=64)[:, :, 63:64].to_broadcast([128, NCH, 64]), ALU.mult),
                     reads=["L_", "E_"], writes=["C_"])
                b = nbA()
                P.op("pe", mmgroup(psb[b][:, :U], [(wmi[:, k, h * 128:(h + 1) * 128], XT[:, k, tok]) for k in ALLC]),
                     reads=xk + [("wmi", 0)], writes=[("ps", b)])
                P.op("act", lambda e, b=b: e.activation(sgt, psb[b][:, :U], AF.Exp, scale=-1.0), reads=[("ps", b)], writes=[sgk])
                P.op("act", lambda e: e.activation(sgt, sgt, AF.Ln, bias=1.0, scale=1.0), reads=[sgk], writes=[sgk])
                P.op("act", lambda e: e.activation(sgt, sgt, AF.Exp, scale=-1.0), reads=[sgk], writes=[sgk])
                P.op("dve", lambda e: e.tensor_tensor(sgt, sgt, E_, ALU.mult), reads=[sgk, "E_"], writes=[sgk])
                P.op("dve", lambda e, s_=s_, h=h, b=b: e.tensor_tensor(qd4[s_][:, h, :], psb[b][:, :U], sgt, ALU.mult),
                     reads=[sgk, ("ps", b)], writes=[("qd", s_, h)])
                b = nbA()

                def trk(e, b=b):
                    ins = None
                    for ti in range(NTU):
                        ins = e.transpose(psb[b][:, ti * 128:(ti + 1) * 128], C_[:, ti * 128:(ti + 1) * 128], identF)
                    return ins
                P.op("pe", trk, reads=["C_", "identF"], writes=[("ps", b)])
                P.op("act", lambda e, b=b, s_=s_, h=h: e.activation(kltok4[s_][:, h, :], psb[b][:, :U], AF.Identity),
                     reads=[("ps", b)], writes=[("kltok", s_, h)])

            def A1p(u, hp):
                s_ = u % 2
                t0 = u * U
                tok = slice(t0, t0 + U)
                xk = RK("XT", ALLC, t0, U)
                hh = (2 * hp, 2 * hp + 1)
                b = nbA()
                for j, h in enumerate(hh):
                    P.op("pe", mmgroup(psb[b][:, j * U:(j + 1) * U], [(wmi[:, k, 512 + h * 128:512 + (h + 1) * 128], XT[:, k, tok]) for k in ALLC]),
                         reads=xk + [("wmi", 1)], writes=[("ps", b)])
                pf = psb[b][:, :2 * U].rearrange("p (a b) -> p a b", a=2)
                P.op("act", lambda e, pf=pf: e.activation(F2, pf, AF.Exp, scale=-1.0), reads=[("ps", b)], writes=["F2"])
                P.op("act", lambda e: e.activation(F2, F2, AF.Ln, bias=1.0, scale=1.0), reads=["F2"], writes=["F2"])
                P.op("act", lambda e: e.activation(F2, F2, AF.Exp, scale=-1.0), reads=["F2"], writes=["F2"])
                for j, h in enumerate(hh):
                    P.op("dve", lambda e, j=j, h=h: e.tensor_scalar(F2[:, j, :], F2[:, j, :], omlc[:, h:h + 1], lbc[:, h:h + 1], ALU.mult, ALU.add),
                         reads=["F2", "lbc", "omlc"], writes=["F2"])
                P.op("act", lambda e: e.activation(L2, F2, AF.Ln), reads=["F2"], writes=["L2"])
                P.op("dve", lambda e: e.tensor_scalar(F2, F2, -1.0, 1.0, ALU.mult, ALU.add), reads=["F2"], writes=["F2"])
                for j in range(2):
                    P.op("dve", lambda e, j=j: e.tensor_tensor_scan(C2[:, j, :], scanmask, L2[:, j, :], 0.0, ALU.mult, ALU.add),
                         reads=["L2", "scanmask"], writes=["C2"])
                P.op("act", lambda e: e.activation(E2, C2, AF.Exp), reads=["C2"], writes=["E2"])
                P.op("act", lambda e: e.activation(L2, C2, AF.Exp, scale=-1.0), reads=["C2"], writes=["L2"])
                P.op("dve", lambda e: e.tensor_tensor(L2, F2, L2, ALU.mult), reads=["F2", "L2"], writes=["L2"])
                P.op("pool", lambda e, s_=s_, hp=hp: e.tensor_copy(kd4[s_][:, 2 * hp:2 * hp + 2, :], L2),
                     reads=["L2"], writes=[("kd", s_, 2 * hp), ("kd", s_, 2 * hp + 1)])
                E4 = E2.rearrange("p a (c b) -> p a c b", b=64)
                P.op("pool", lambda e, s_=s_, hp=hp: e.tensor_copy(EL[s_][:, 2 * hp:2 * hp + 2, :], E4[:, :, :, 63]),
                     reads=["E2"], writes=[("EL", s_)])
                P.op("dve", lambda e: e.tensor_tensor(
                    C2.rearrange("p a (c b) -> p (a c) b", b=64), L2.rearrange("p a (c b) -> p (a c) b", b=64),
                    E2.rearrange("p a (c b) -> p (a c) b", b=64)[:, :, 63:64].to_broadcast([128, 2 * NCH, 64]), ALU.mult),
                     reads=["L2", "E2"], writes=["C2"])
                b = nbA()
                for j, h in enumerate(hh):
                    P.op("pe", mmgroup(psb[b][:, j * U:(j + 1) * U], [(wmi[:, k, h * 128:(h + 1) * 128], XT[:, k, tok]) for k in ALLC]),
                         reads=xk + [("wmi", 0)], writes=[("ps", b)])
                pq = psb[b][:, :2 * U].rearrange("p (a b) -> p a b", a=2)
                P.op("act", lambda e, pq=pq: e.activation(sgt2, pq, AF.Exp, scale=-1.0), reads=[("ps", b)], writes=["sgt2"])
                P.op("act", lambda e: e.activation(sgt2, sgt2, AF.Ln, bias=1.0, scale=1.0), reads=["sgt2"], writes=["sgt2"])
                P.op("act", lambda e: e.activation(sgt2, sgt2, AF.Exp, scale=-1.0), reads=["sgt2"], writes=["sgt2"])
                P.op("dve", lambda e: e.tensor_tensor(sgt2, sgt2, E2, ALU.mult), reads=["sgt2", "E2"], writes=["sgt2"])
                P.op("dve", lambda e, s_=s_, hp=hp, pq=pq: e.tensor_tensor(qd4[s_][:, 2 * hp:2 * hp + 2, :], pq, sgt2, ALU.mult),
                     reads=["sgt2", ("ps", b)], writes=[("qd", s_, 2 * hp), ("qd", s_, 2 * hp + 1)])
                b = nbA()

                def trk(e, b=b):
                    ins = None
                    for j in range(2):
                        for ti in range(NTU):
                            ins = e.transpose(psb[b][:, j * U + ti * 128:j * U + (ti + 1) * 128], C2[:, j, ti * 128:(ti + 1) * 128], identF)
                    return ins
                P.op("pe", trk, reads=["C2", "identF"], writes=[("ps", b)])
                P.op("act", lambda e, b=b, s_=s_, hp=hp: e.activation(
                    kltok4[s_][:, 2 * hp:2 * hp + 2, :], psb[b][:, :2 * U].rearrange("p (a b) -> p a b", a=2), AF.Identity),
                     reads=[("ps", b)], writes=[("kltok", s_, 2 * hp), ("kltok", s_, 2 * hp + 1)])

            def A1any(u, hlist):
                hlist = list(hlist)
                if PAIR and len(hlist) % 2 == 0:
                    for hp in sorted(set(h // 2 for h in hlist)):
                        A1p(u, hp)
                else:
                    for h in hlist:
                        A1(u, h)

            H4 = list(range(4))

            def Bchunk(u, c):
                s_ = u % 2
                ti, cc = c // 2, c % 2
                c0 = ti * 128
                q0 = c * 64
                r0 = cc * 64
                if cc == 0:
                    def sc(e):
                        ins = None
                        for h in H4:
                            ins = e.matmul(psb[4][:, h * 128:(h + 1) * 128], kd4[s_][:, h, c0:c0 + 128], qd4[s_][:, h, c0:c0 + 128],
                                           start=True, stop=True)
                        return ins
                    P.op("pe", sc, reads=[("kd", s_, h) for h in H4] + [("qd", s_, h) for h in H4], writes=[("ps", 4)])
                    P.op("dve", lambda e: e.tensor_tensor(sT4, psb[4].rearrange("p (a b) -> p a b", a=4),
                                                          mask2.unsqueeze(1).to_broadcast([128, 4, 128]), ALU.mult),
                         reads=[("ps", 4), "mask2"], writes=["sT4"])

                def pm(e):
                    ins = None
                    for h in H4:
                        ins = e.matmul(psb[4][:, h * 128:(h + 1) * 128], kltok4[s_][r0:r0 + 64, h, c0:c0 + 128],
                                       vtok[s_][r0:r0 + 64, ti, h * 128:(h + 1) * 128], start=True, stop=True)
                    return ins
                P.op("pe", pm, reads=[("kltok", s_, h) for h in H4] + [("vtok", s_, ti)], writes=[("ps", 4)])

                def io(e):
                    ins = None
                    for h in H4:
                        ob = OB[h // 2]
                        col = (h % 2) * U + q0
                        e.matmul(psb[ob][:, col:col + 64], vtok[s_][:, ti, h * 128:(h + 1) * 128], sT4[:, h, r0:r0 + 64],
                                 start=True, stop=False)
                        ins = e.matmul(psb[ob][:, col:col + 64], Sb[:, h, :], qd4[s_][:, h, q0:q0 + 64], start=False, stop=True)
                    return ins
                P.op("pe", io, reads=["sT4", ("vtok", s_, ti), "Sb"] + [("qd", s_, h) for h in H4],
                     writes=[("ps", OB[0]), ("ps", OB[1])])
                P.op("dve", lambda e: e.tensor_tensor(Sf, Sf, EL[s_][:, :, c:c + 1].to_broadcast([128, 4, 128]), ALU.mult),
                     reads=["Sf", ("EL", s_)], writes=["Sf"])
                P.op("dve", lambda e: e.tensor_tensor(Sf, Sf, psb[4].rearrange("p (a b) -> p a b", a=4), ALU.add),
                     reads=["Sf", ("ps", 4)], writes=["Sf"])
                P.op("act", lambda e: e.activation(Sb, Sf, AF.Identity), reads=["Sf"], writes=["Sb"])

            def Cnorm(u, hp):
                t0 = u * U
                tok = slice(t0, t0 + U)
                xk = RK("XT", ALLC, t0, U)
                ob = OB[hp]
                pv = psb[ob][:, :2 * U].rearrange("p (a b) -> p a b", a=2)
                P.op("act", lambda e: e.activation(osq2, pv, AF.Square), reads=[("ps", ob)], writes=["osq2"])
                P.op("pe", mmgroup(psb[4][:, :2 * U], [(onesV, osq2.rearrange("p a b -> p (a b)"))]),
                     reads=["osq2", "onesV"], writes=[("ps", 4)])
                P.op("act", lambda e: e.activation(rs2, psb[4][:, :2 * U].rearrange("p (a b) -> p a b", a=2), AF.Ln,
                                                   bias=RMS_EPS, scale=1.0),
                     reads=[("ps", 4)], writes=["rs2"])
                P.op("act", lambda e: e.activation(rs2, rs2, AF.Exp, scale=-0.5), reads=["rs2"], writes=["rs2"])
                P.op("dve", lambda e: e.tensor_tensor(rs2, pv, rs2, ALU.mult), reads=[("ps", ob), "rs2"], writes=["rs2"])
                b = 1
                for j in range(2):
                    h = 2 * hp + j
                    P.op("pe", mmgroup(psb[b][:, j * U:(j + 1) * U],
                                       [(wmi[:, k, 1536 + h * 128:1536 + (h + 1) * 128], XT[:, k, tok]) for k in ALLC]),
                         reads=xk + [("wmi", 3)], writes=[("ps", b)])
                gv = psb[b][:, :2 * U].rearrange("p (a b) -> p a b", a=2)
                P.op("act", lambda e, gv=gv: e.activation(sgg2, gv, AF.Exp, scale=-1.0), reads=[("ps", b)], writes=["sgg2"])
                P.op("act", lambda e: e.activation(sgg2, sgg2, AF.Ln, bias=1.0, scale=1.0), reads=["sgg2"], writes=["sgg2"])
                P.op("act", lambda e: e.activation(sgg2, sgg2, AF.Exp, scale=-1.0), reads=["sgg2"], writes=["sgg2"])
                P.op("dve", lambda e: e.scalar_tensor_tensor(rs2, rs2, cols[:, C_GN:C_GN + 1], sgg2, ALU.mult, ALU.mult),
                     reads=["rs2", "sgg2", "cols"], writes=["rs2"])
                P.op("dve", lambda e, hp=hp, gv=gv: e.tensor_tensor(MT[:, 2 * hp:2 * hp + 2, :], gv, rs2, ALU.mult),
                     reads=["rs2", ("ps", b)], writes=[("MT", 2 * hp), ("MT", 2 * hp + 1)])

            def Cpool(u, h):
                t0 = u * U
                tok = slice(t0, t0 + U)
                xk = RK("XT", ALLC, t0, U)
                w = POOL_W[h]
                b = nbC()
                P.op("pe", mmgroup(psb[b][:, :U], [(wmi[:, k, 2048 + h * 128:2048 + (h + 1) * 128], XT[:, k, tok]) for k in ALLC]),
                     reads=xk + [("wmi", 4)], writes=[("ps", b)])
                P.op("act", lambda e, b=b: e.activation(VP[0][:, 16:16 + U], psb[b][:, :U], AF.Identity),
                     reads=[("ps", b)], writes=[("VP", 0)])
                P.op("dve", lambda e, h=h: e.tensor_copy(VP[0][:, 0:16], halo[:, h, :]), reads=[("halo", h)], writes=[("VP", 0)])
                P.op("dve", lambda e, h=h: e.tensor_copy(halo[:, h, :], VP[0][:, U:U + 16]), reads=[("VP", 0)], writes=[("halo", h)])
                src = 0
                step = 1
                while step < w:
                    dst = 1 if src != 1 else 2
                    P.op("dve", lambda e, src=src, dst=dst, step=step: e.tensor_tensor(
                        VP[dst][:, step:16 + U], VP[src][:, step:16 + U], VP[src][:, 0:16 + U - step], ALU.add),
                         reads=[("VP", src)], writes=[("VP", dst)])
                    src = dst
                    step *= 2
                P.op("dve", lambda e, src=src, w=w: e.scalar_tensor_tensor(
                    pooled, VP[src][:, 16:16 + U], 1.0 / w, VP[0][:, 16:16 + U], ALU.mult, ALU.subtract),
                     reads=[("VP", src), ("VP", 0)], writes=["pooled"])
                if u == 0:
                    P.op("dve", lambda e, src=src, h=h: e.tensor_tensor(pfix, VP[src][:, 16:32], rcfix[:, h, :], ALU.mult),
                         reads=[("VP", src), ("rcfix", h)], writes=["pfix"])
                    P.op("dve", lambda e: e.tensor_tensor(pooled[:, 0:16], pfix, VP[0][:, 16:32], ALU.subtract),
                         reads=["pfix", ("VP", 0), "pooled"], writes=["pooled"])
                b = nbC()
                P.op("pe", mmgroup(psb[b][:, :U], [(poolw[:, h * 128:(h + 1) * 128], pooled)]),
                     reads=["poolw", "pooled"], writes=[("ps", b)])
                P.op("act", lambda e, b=b, h=h: e.activation(MT[:, 4 + h, :], psb[b][:, :U], AF.Identity,
                                                             scale=cols[:, C_PSC + h:C_PSC + h + 1]),
                     reads=[("ps", b), "cols"], writes=[("MT", 4 + h)])

            def Dout(u):
                t0 = u * U
                tok = slice(t0, t0 + U)
                for m in range(KC):
                    b = nbC()
                    P.op("pe", mmgroup(psb[b][:, :U], [(wmo[:, k, m * 128:(m + 1) * 128], MT[:, k, :]) for k in ALLC]),
                         reads=[("MT", k) for k in ALLC] + [("wmo", m // 4)], writes=[("ps", b)])
                    rk = RK("RT", [m], t0, U)
                    P.op("dve", lambda e, m=m, tok=tok, b=b: e.scalar_tensor_tensor(
                        RT[:, m, tok], RT[:, m, tok], ALPHA, psb[b][:, :U], ALU.mult, ALU.add),
                         reads=[("ps", b)] + rk, writes=rk)

            P.interleave([pre, prologue])
            P.op("dve", lambda e: e.memset(pfix, 0.0),
                 writes=["mean", "m2", "rstd", "zsq", ("zsq", 0), ("zsq", 1), "sgt", "sgt2", "C2", "E2", "F2", "L2", "F_", "L_", "C_", "E_", "pfix", ("VP", 1)])
            for i in range(NUU + 1):
                streams = []

                def s1(i=i):
                    if 1 <= i:
                        Cnorm(i - 1, 0)
                        Cnorm(i - 1, 1)
                    if i < NUU:
                        for c in range(NCH):
                            Bchunk(i, c)
                    if i >= 2:
                        layernorm((i - 2) * U, U, gcol, bcol)
                streams.append(s1)

                def s2(i=i):
                    if i + 1 < NUU:
                        V(i + 1)
                        A1any(i + 1, H4)
                if i + 1 < NUU:
                    streams.append(s2)

                def s3(i=i):
                    for h in H4:
                        Cpool(i - 1, h)
                    Dout(i - 1)
                if i >= 1:
                    streams.append(s3)
                P.interleave(streams)
            return lambda: layernorm((NUU - 1) * U, U, gcol, bcol, cast_eng="dve")

        if stop_after >= 2:
            tail = mixer(C_LN + 16, C_LN + 24, pre=tail)

        def attention(gcol, bcol, pre=None):
            P.new_phase()
            A.off = PH
            wq = A.bf16(KC, D)
            wo_ = A.bf16(KC, D)
            kT = A.bf16(KC, NMEM)
            vm = A.bf16(2, D)
            qT0 = A.bf16(KC, TG)
            R0 = A.off
            memT = A.bf16(KC, NMEM)
            mstage = A.f32(2, D)
            wk = A.bf16(KC, D)
            wv = A.bf16(KC, D)
            def prep():
                for b in range(2):
                    wload(f"wq{b}", wq[:, :, b * 512:(b + 1) * 512], wq_d[b].rearrange("p (k n) -> p k n", k=KC), ("wq", b))
                for b in range(2):
                    wload(f"wk{b}", wk[:, :, b * 512:(b + 1) * 512], wk_d[b].rearrange("p (k n) -> p k n", k=KC), ("wk", b))
                for b in range(2):
                    wload(f"wv{b}", wv[:, :, b * 512:(b + 1) * 512], wv_d[b].rearrange("p (k n) -> p k n", k=KC), ("wv", b))
                for b in range(2):
                    wload(f"wox{b}", wo_[:, :, b * 512:(b + 1) * 512], wo_d[b].rearrange("p (k n) -> p k n", k=KC), ("wo_", b))
                for m in range(KC):
                    b = nextbank()
                    P.op("pe", mmgroup(psb[b][:, :TG], [(wq[:, k, m * 128:(m + 1) * 128], XT[:, k, 0:TG]) for k in ALLC]),
                         reads=RK("XT", ALLC, 0, TG) + [("wq", m // 4)], writes=[("ps", b)])
                    P.op("act", lambda e, m=m, b=b: e.activation(qT0[:, m, :], psb[b][:, :TG], AF.Identity, scale=1.0 / 16.0),
                         reads=[("ps", b)], writes=[("qT", 0, m)])
                for ti in range(2):
                    P.dma("sp", f"mst{ti}", mstage[:, ti, :], mem_d[ti * 128:(ti + 1) * 128, :], writes=[("mst", ti)])
                    for hb in range(2):
                        b = nextbank()

                        def trm(e, ti=ti, hb=hb, b=b):
                            ins = None
                            for q in range(4):
                                c = hb * 4 + q
                                ins = e.transpose(psb[b][:, q * 128:(q + 1) * 128], mstage[:, ti, c * 128:(c + 1) * 128], identF)
                            return ins
                        P.op("pe", trm, reads=[("mst", ti), "identF"], writes=[("ps", b)])
                        P.op("act", lambda e, ti=ti, hb=hb, b=b: e.activation(
                            memT[:, hb * 4:hb * 4 + 4, ti * 128:(ti + 1) * 128], psb[b].rearrange("p (a b) -> p a b", a=4), AF.Identity),
                             reads=[("ps", b)], writes=[("memT", ti, hb)])
                memk = [("memT", ti, hb) for ti in range(2) for hb in range(2)]
                for dc in range(KC):
                    b = nextbank()
                    P.op("pe", mmgroup(psb[b][:, :NMEM], [(wk[:, k, dc * 128:(dc + 1) * 128], memT[:, k, :]) for k in ALLC]),
                         reads=memk + [("wk", dc // 4)], writes=[("ps", b)])
                    P.op("act", lambda e, dc=dc, b=b: e.activation(kT[:, dc, :], psb[b][:, :NMEM], AF.Identity),
                         reads=[("ps", b)], writes=[("kT", dc)])
                for mi in range(2):
                    for nh in range(2):
                        b = nextbank()
                        P.op("pe", mmgroup(psb[b], [(memT[:, k, mi * 128:(mi + 1) * 128], wv[:, k, nh * 512:(nh + 1) * 512]) for k in ALLC]),
                             reads=memk + [("wv", nh)], writes=[("ps", b)])
                        P.op("dve", lambda e, mi=mi, nh=nh, b=b: e.tensor_copy(vm[:, mi, nh * 512:(nh + 1) * 512], psb[b]),
                             reads=[("ps", b)], writes=[("vm", mi, nh)])

            P.interleave([pre, prep])
            P.new_phase()
            A.off = R0
            NTG = S // TG
            qT = [qT0, A.bf16(KC, TG)]
            oT = [A.bf16(KC, TG), A.bf16(KC, TG)]
            ET = [[A.bf16(TG), A.bf16(TG)] for _ in range(4)]
            rec = [A.f32(TG), A.f32(TG)]
            rq = [0]
            rp = [0]
            rs_ = [0]

            def nbQ():
                rq[0] ^= 1
                return (2, 7)[rq[0]]

            HN = TG // 2 if TG >= 512 else TG

            def lnq(tg):
                for t0 in range(tg * TG, (tg + 1) * TG, HN):
                    layernorm(t0, HN, gcol, bcol, cast_eng="dve")

            def nbP():
                rp[0] ^= 1
                return (0, 1)[rp[0]]

            def nbS():
                rs_[0] ^= 1
                return (4, 5)[rs_[0]]

            def Qp(tg):
                t0 = tg * TG
                tok = slice(t0, t0 + TG)
                xk = RK("XT", ALLC, t0, TG)
                q = qT[tg % 2]
                for m in range(KC):
                    b = nbQ()
                    P.op("pe", mmgroup(psb[b][:, :TG], [(wq[:, k, m * 128:(m + 1) * 128], XT[:, k, tok]) for k in ALLC]),
                         reads=xk + [("wq", m // 4)], writes=[("ps", b)])
                    P.op("act", lambda e, m=m, b=b, q=q: e.activation(q[:, m, :], psb[b][:, :TG], AF.Identity, scale=1.0 / 16.0),
                         reads=[("ps", b)], writes=[("qT", tg % 2, m)])

            def Hd(tg):
                q = qT[tg % 2]
                o = oT[tg % 2]
                for h in range(4):
                    for mi in range(2):
                        sb_ = nbS()
                        P.op("pe", mmgroup(psb[sb_][:, :TG], [(kT[:, 2 * h + dd, mi * 128:(mi + 1) * 128], q[:, 2 * h + dd, :]) for dd in range(2)]),
                             reads=[("kT", 2 * h), ("kT", 2 * h + 1), ("qT", tg % 2, 2 * h), ("qT", tg % 2, 2 * h + 1)], writes=[("ps", sb_)])
                        P.op("act", lambda e, h=h, mi=mi, sb_=sb_: e.activation(ET[h][mi], psb[sb_][:, :TG], AF.Exp),
                             reads=[("ps", sb_)], writes=[("ET", h, mi)])
                for h in range(4):
                    i = h % 2
                    P.op("pe", mmgroup(psb[3][:, :TG], [(ones1, ET[h][mi]) for mi in range(2)]),
                         reads=[("ET", h, 0), ("ET", h, 1), "ones1"], writes=[("ps", 3)])
                    P.op("act", lambda e, i=i: e.activation(rec[i], psb[3][:, :TG], AF.Ln), reads=[("ps", 3)], writes=[("rec", i)])
                    P.op("act", lambda e, i=i: e.activation(rec[i], rec[i], AF.Exp, scale=-1.0), reads=[("rec", i)], writes=[("rec", i)])
                    for dd in range(2):
                        dc = 2 * h + dd
                        b = nbP()
                        P.op("pe", mmgroup(psb[b][:, :TG], [(vm[:, mi, dc * 128:(dc + 1) * 128], ET[h][mi]) for mi in range(2)]),
                             reads=[("ET", h, 0), ("ET", h, 1), ("vm", 0, dc // 4), ("vm", 1, dc // 4)], writes=[("ps", b)])
                        P.op("dve", lambda e, dc=dc, b=b, i=i, o=o: e.tensor_tensor(o[:, dc, :], psb[b][:, :TG], rec[i], ALU.mult),
                             reads=[("ps", b), ("rec", i)], writes=[("oT", tg % 2, dc)])

            def Op(tg):
                t0 = tg * TG
                tok = slice(t0, t0 + TG)
                o = oT[tg % 2]
                for m in range(KC):
                    b = nbQ()
                    P.op("pe", mmgroup(psb[b][:, :TG], [(wo_[:, k, m * 128:(m + 1) * 128], o[:, k, :]) for k in ALLC]),
                         reads=[("oT", tg % 2, k) for k in ALLC] + [("wo_", m // 4)], writes=[("ps", b)])
                    rk = RK("RT", [m], t0, TG)
                    P.op("dve", lambda e, m=m, tok=tok, b=b: e.scalar_tensor_tensor(
                        RT[:, m, tok], RT[:, m, tok], ALPHA, psb[b][:, :TG], ALU.mult, ALU.add),
                         reads=[("ps", b)] + rk, writes=rk)

            for tg in range(NTG + 1):
                streams = []
                if tg < NTG:
                    streams.append(lambda tg=tg: Hd(tg))

                def qo(tg=tg):
                    if tg >= 1:
                        Op(tg - 1)
                    if tg + 1 < NTG:
                        Qp(tg + 1)
                if tg >= 1 or tg + 1 < NTG:
                    streams.append(qo)
                if tg >= 2:
                    streams.append(lambda tg=tg: lnq(tg - 2))
                P.interleave(streams)
            return lambda: lnq(NTG - 1)

        if stop_after >= 3:
            tail = attention(C_LN + 32, C_LN + 40, pre=tail)
        if stop_after >= 4:
            tail = ffn(1, C_LN + 48, C_LN + 56, pre=tail, newphase=True)

        P.new_phase()
        A.off = PH
        ostage = [A.f32(D), A.f32(D)]
        otoks = []
        if "O" in DBG:
            for tt in range(NT):
                otoks.append(P.dma("sp", f"out{tt % 2}", out_d[tt * 128:(tt + 1) * 128, :].rearrange("p (c t) -> p c t", c=8),
                                   RT[:, :, tt * 128:(tt + 1) * 128], reads=RK("RT", ALLC, tt * 128, 128)))
        def store_tiles(tiles):
          for tt in tiles:
              sl = tt % 2
              for hb in range(2):
                  b = nextbank()

                  def tro(e, tt=tt, hb=hb, b=b):
                      ins = None
                      for q in range(4):
                          c = hb * 4 + q
                          ins = e.transpose(psb[b][:, q * 128:(q + 1) * 128], RT[:, c, tt * 128:(tt + 1) * 128], identF)
                      return ins
                  P.op("pe", tro, reads=RK("RT", list(range(hb * 4, hb * 4 + 4)), tt * 128, 128) + ["identF"], writes=[("ps", b)])
                  if hb == 0:
                      P.op("dve", lambda e, sl=sl, b=b: e.tensor_copy(ostage[sl][:, 0:512], psb[b]),
                           reads=[("ps", b)], writes=[("ost", sl, 0)])
                  else:
                      P.op("act", lambda e, sl=sl, b=b: e.activation(ostage[sl][:, 512:1024], psb[b], AF.Identity),
                           reads=[("ps", b)], writes=[("ost", sl, 1)])
              otoks.append(P.dma("sp", f"out{sl}", out_d[tt * 128:(tt + 1) * 128, :], ostage[sl],
                                 reads=[("ost", sl, 0), ("ost", sl, 1)]))

        if stop_after >= 4 and HALF % 256 == 0 and HALF >= 512:
            g4, b4 = C_LN + 48, C_LN + 56
            pieces = list(range(HALF, S, 256))
            lnp = lambda t0: (lambda: layernorm(t0, 256, g4, b4))
            tl = lambda t0, n: range(t0 // 128, (t0 + n) // 128)
            P.interleave([lambda: [lnp(pieces[0])(), lnp(pieces[1])()], lambda: store_tiles(range(NT // 2))])
            stored, lndone = HALF, HALF + 512
            for k in range(2, len(pieces)):
                P.interleave([lnp(pieces[k]), lambda a=stored, b=lndone: store_tiles(range(a // 128, b // 128))])
                stored, lndone = lndone, lndone + 256
            store_tiles(range(stored // 128, NT))
        elif tail is not None:
            P.interleave([tail, lambda: store_tiles(range(NT // 2))], weights=[1, 1])
            store_tiles(range(NT // 2, NT))
        else:
            store_tiles(range(NT // 2))
            store_tiles(range(NT // 2, NT))
        P.final_wait("sp", otoks[-2:])
        P.run(st)
    nc._in_names = _names
    return nc


def _blk_cols(w, nblk, width):
    K, N = w.shape
    kc = K // 128
    a = w.reshape(kc, 128, nblk, width).transpose(2, 1, 0, 3)
    return np.ascontiguousarray(a.reshape(nblk, 128, kc * width))


def prep_weights(inp):
    f = lambda a: np.asarray(a, dtype=np.float32)
    out = {}
    for i, nm in ((1, "w_ffn1"), (2, "w_ffn2")):
        w_in = f(inp[nm + "_in"])[0]
        g = w_in[:, :DFF].reshape(KC, 128, JC // 2, 2, 128)
        u = w_in[:, DFF:].reshape(KC, 128, JC // 2, 2, 128)
        blk = np.concatenate([g, u], axis=3)
        blk = blk.transpose(2, 1, 0, 3, 4).reshape(JC // 2, 128, KC * 512)
        out[f"wf{i}_in"] = np.ascontiguousarray(blk)
        w_out = f(inp[nm + "_out"])[0]
        out[f"wf{i}_out"] = _blk_cols(w_out, 8, 128)
    out["wmi"] = _blk_cols(f(inp["w_mix_in"])[0], 5, 512)
    out["wmo"] = _blk_cols(f(inp["w_mix_out"])[0], 2, 512)
    for nm, k in (("wq", "xa_wq"), ("wk", "xa_wk"), ("wv", "xa_wv"), ("wo", "xa_wo")):
        out[nm] = _blk_cols(f(inp[k])[0], 2, 512)
    pw = f(inp["pool_w"])[0]
    out["poolw"] = np.ascontiguousarray(pw.transpose(1, 0, 2).reshape(128, 512))
    cols = np.zeros((128, NCOLS), np.float32)
    col8 = lambda v: f(v).reshape(-1, 128).T
    for i, nm in enumerate(["ln1_g", "ln1_b", "ln2_g", "ln2_b", "ln3_g", "ln3_b", "ln4_g", "ln4_b"]):
        cols[:, C_LN + 8 * i:C_LN + 8 * i + 8] = col8(inp[nm][0])
    cols[:, C_PSC:C_PSC + 4] = col8(inp["pool_scale"][0])
    cols[:, C_GN] = f(inp["hgrn_gnorm"])[0]
    lb = f(inp["hgrn_lb"])
    cols[:, C_LBA:C_LBA + 4] = col8(lb[0])
    cols[:, C_LBB:C_LBB + 4] = col8(lb[1])
    out["cols"] = cols
    return out


_NC_CACHE = {}


def kernel(**inputs):
    x = np.asarray(inputs["x"], dtype=np.float32)
    mem = np.asarray(inputs["mem"], dtype=np.float32)
    B, S, _ = x.shape
    w = prep_weights(inputs)
    key = (S,)
    if key not in _NC_CACHE:
        _NC_CACHE[key] = build(S=S)
    nc = _NC_CACHE[key]
    in_maps = []
    for b in range(B):
        m = dict(w)
        m["x"] = np.ascontiguousarray(x[b])
        m["mem"] = np.ascontiguousarray(mem[b])
        in_maps.append(m)
    res = run_bass_kernel_spmd(nc, in_maps, core_ids=list(range(B)))
    return np.stack([np.asarray(r["out"], dtype=np.float32) for r in res.results], axis=0)
```

```python
from contextlib import ExitStack
import numpy as np
import concourse.bass as bass
import concourse.mybir as mybir
from concourse.bass_utils import run_bass_kernel_spmd

F32 = mybir.dt.float32
BF16 = mybir.dt.bfloat16
I32 = mybir.dt.int32
AF = mybir.ActivationFunctionType
ALU = mybir.AluOpType

D = 1024
KC = 8
DFF = 2816
JC = 22
NMEM = 256
ALPHA = 2.0 ** 0.25
LN_EPS = 1e-5
RMS_EPS = 1e-6
POOL_W = (2, 4, 8, 16)
NCOLS = 80
C_LN = 0
C_PSC = 64
C_GN = 68
C_LBA = 69
C_LBB = 73


class Prog:
    ENGS = ("pe", "act", "dve", "pool", "sp")

    def __init__(self, nc):
        self.nc = nc
        self.ops = {e: [] for e in self.ENGS}
        self.count = {e: 0 for e in self.ENGS}
        self.sem = {}
        self.dsem = {}
        self.last_w = {}
        self.readers = {}
        self.waited = {e: {} for e in self.ENGS}
        self.guard = []
        self.seen = set()
        self.n_waits = 0
        self.fuse_waits = True

    def _yield(self):
        il = getattr(self, "_il", None)
        if il is None:
            return
        import threading
        i = il["tl"].__dict__.get("idx")
        if i is None:
            return
        il["left"][i] -= 1
        if il["left"][i] > 0:
            return
        il["main"].release()
        il["sems"][i].acquire()

    def interleave(self, fns, weights=None):
        import threading
        fns = [f for f in fns if f is not None]
        if not fns:
            return
        n = len(fns)
        weights = list(weights or [1] * n)
        il = dict(tl=threading.local(), sems=[threading.Semaphore(0) for _ in range(n)],
                  main=threading.Semaphore(0), left=[0] * n, alive=[True] * n, err=[])
        self._il = il

        def runner(i):
            il["tl"].idx = i
            il["sems"][i].acquire()
            try:
                fns[i]()
            except BaseException as ex:
                il["err"].append(ex)
            il["alive"][i] = False
            il["tl"].idx = None
            il["main"].release()

        ths = [threading.Thread(target=runner, args=(i,), daemon=True) for i in range(n)]
        for t in ths:
            t.start()
        while any(il["alive"]):
            for i in range(n):
                if il["alive"][i]:
                    il["left"][i] = weights[i]
                    il["sems"][i].release()
                    il["main"].acquire()
                    if il["err"]:
                        self._il = None
                        raise il["err"][0]
        for t in ths:
            t.join()
        self._il = None

    def _sem_for(self, tok):
        if tok[0] == "eng":
            return self.sem[tok[1]], tok[2]
        return self.dsem[tok[1]][0], tok[2]

    def new_phase(self):
        g = {}
        for d in (self.last_w,):
            for t in d.values():
                n = t[0] + ":" + t[1]
                if n not in g or g[n][2] < t[2]:
                    g[n] = t
        for lst in self.readers.values():
            for t in lst:
                n = t[0] + ":" + t[1]
                if n not in g or g[n][2] < t[2]:
                    g[n] = t
        self.guard = list(g.values())
        self.seen = set()

    def _collect(self, eng, reads, writes):
        deps = []
        for r in reads:
            t = self.last_w.get(r)
            if t is not None:
                deps.append(t)
        for w in writes:
            t = self.last_w.get(w)
            if t is not None:
                deps.append(t)
            deps.extend(self.readers.get(w, ()))
            if w not in self.seen:
                self.seen.add(w)
                deps.extend(self.guard)
        need = {}
        for t in deps:
            if t[0] == "eng" and t[1] == eng and eng == "pe":
                continue
            name = t[0] + ":" + t[1]
            v = t[2]
            if self.waited[eng].get(name, 0) >= v:
                continue
            if name not in need or need[name][2] < v:
                need[name] = t
        for name, t in need.items():
            self.waited[eng][name] = t[2]
        return list(need.values())

    def _commit(self, tok, reads, writes):
        for r in reads:
            self.readers.setdefault(r, []).append(tok)
        for w in writes:
            self.last_w[w] = tok
            self.readers[w] = []

    def op(self, eng, fn, reads=(), writes=()):
        waits = self._collect(eng, reads, writes)
        self.count[eng] += 1
        seq = self.count[eng]
        self.n_waits += len(waits)

        def emit(e, fn=fn, waits=waits, eng=eng):
            ws = [self._sem_for(t) for t in waits]
            fuse = bool(ws) and eng != "pe" and self.fuse_waits
            for sv in (ws[:-1] if fuse else ws):
                e.wait_ge(*sv)
            ins = fn(e)
            if fuse:
                ins._wait_ge(*ws[-1])
            ins.then_inc(self.sem[eng], 1)

        self.ops[eng].append(emit)
        tok = ("eng", eng, seq)
        self._commit(tok, reads, writes)
        self._yield()
        return tok

    def dma(self, eng, key, out, in_, reads=(), writes=(), **kw):
        waits = self._collect(eng, reads, writes)
        if key not in self.dsem:
            self.dsem[key] = [None, 0]
        self.dsem[key][1] += 16
        cnt = self.dsem[key][1]
        self.n_waits += len(waits)

        def emit(e, waits=waits, key=key, out=out, in_=in_, kw=kw):
            for t in waits:
                e.wait_ge(*self._sem_for(t))
            e.dma_start(out=out, in_=in_, **kw).then_inc(self.dsem[key][0], 16)

        self.ops[eng].append(emit)
        tok = ("dma", key, cnt)
        self._commit(tok, reads, writes)
        self._yield()
        return tok

    def final_wait(self, eng, toks):
        def emit(e, toks=list(toks)):
            for t in toks:
                e.wait_ge(*self._sem_for(t))
        self.ops[eng].append(emit)

    def run(self, stack):
        nc = self.nc
        for e in self.ENGS:
            self.sem[e] = stack.enter_context(nc.semaphore("prog_" + e))
        for k in self.dsem:
            self.dsem[k][0] = stack.enter_context(nc.semaphore("dma_" + k))
        block = stack.enter_context(nc.Block())

        @block.tensor
        def _(e):
            for f in self.ops["pe"]:
                f(e)

        @block.scalar
        def _(e):
            for f in self.ops["act"]:
                f(e)

        @block.vector
        def _(e):
            for f in self.ops["dve"]:
                f(e)

        @block.gpsimd
        def _(e):
            for f in self.ops["pool"]:
                f(e)

        @block.sync
        def _(e):
            for f in self.ops["sp"]:
                f(e)


def build(S=2048, TG=512, U=256, stop_after=4):
    assert S % (2 * TG) == 0 and TG % U == 0 and U % 128 == 0
    nc = bass.Bass("TRN2", target_bir_lowering=False)
    NT = S // 128
    HALF = S // 2
    NTGH = HALF // TG

    import os
    _u = "u" in os.environ.get("KDBG", "")
    _names = []

    def din(name, shape):
        if _u and stop_after == 0 and name not in ("x", "cols"):
            return None
        _names.append(name)
        return nc.dram_tensor(name, list(shape), F32, kind="ExternalInput").ap()

    x_d = din("x", [S, D])
    mem_d = din("mem", [NMEM, D])
    wf_in_d = [din("wf1_in", [11, 128, KC * 512]), din("wf2_in", [11, 128, KC * 512])]
    wf_out_d = [din("wf1_out", [8, 128, JC * 128]), din("wf2_out", [8, 128, JC * 128])]
    wmi_d = din("wmi", [5, 128, KC * 512])
    wmo_d = din("wmo", [2, 128, KC * 512])
    wq_d = din("wq", [2, 128, KC * 512])
    wk_d = din("wk", [2, 128, KC * 512])
    wv_d = din("wv", [2, 128, KC * 512])
    wo_d = din("wo", [2, 128, KC * 512])
    poolw_d = din("poolw", [128, 512])
    cols_d = din("cols", [128, NCOLS])
    out_d = nc.dram_tensor("out", [S, D], F32, kind="ExternalOutput").ap()

    st = ExitStack()
    with st:
        AW = 53200
        arena = st.enter_context(nc.sbuf_tensor("arena", [128, AW], F32))
        psb = [st.enter_context(nc.psum_tensor(f"ps{i}", [128, 512], F32))[:] for i in range(8)]
        P = Prog(nc)

        class Alloc:
            def __init__(self, base):
                self.off = base

            def f32(self, *shape):
                n = int(np.prod(shape))
                v = arena[:, self.off:self.off + n]
                self.off += n
                assert self.off <= AW, f"arena overflow {self.off}"
                if len(shape) == 2:
                    return v.rearrange("p (a b) -> p a b", a=shape[0])
                if len(shape) == 3:
                    return v.rearrange("p (a b c) -> p a b c", a=shape[0], b=shape[1])
                return v

            def bf16(self, *shape):
                n = int(np.prod(shape))
                assert n % 2 == 0
                v = arena[:, self.off:self.off + n // 2].bitcast(BF16)
                self.off += n // 2
                assert self.off <= AW, f"arena overflow {self.off}"
                if len(shape) == 2:
                    return v.rearrange("p (a b) -> p a b", a=shape[0])
                if len(shape) == 3:
                    return v.rearrange("p (a b c) -> p a b c", a=shape[0], b=shape[1])
                return v

        A = Alloc(0)
        RT = A.f32(KC, S)
        XT = A.bf16(KC, S)
        identF = A.f32(128)
        onesD = A.bf16(128)
        onesV = A.bf16(128)
        ones1 = A.bf16(128)
        mask2 = A.f32(128)
        cols = A.f32(NCOLS)
        lbc = A.f32(4)
        omlc = A.f32(4)
        rcfix = A.f32(4, 16)
        iot = A.f32(16)
        ioti = arena[:, A.off:A.off + 16].bitcast(I32)
        A.off += 16
        scanmask = A.f32(U)
        mean_off = A.off
        mean = A.f32(TG)
        m2 = A.f32(TG)
        rstd = A.f32(TG)
        zsq_off = A.off
        zsq = A.bf16(KC, TG)
        PH = A.off

        def RK(name, cs, t0, n):
            return [(name, c, tt) for c in cs for tt in range(t0 // 128, (t0 + n + 127) // 128)]

        ALLC = list(range(KC))
        bank_rr = [0]

        def nextbank():
            b = bank_rr[0]
            bank_rr[0] = (b + 1) % 4
            return b

        def mmgroup(out, pairs):
            def fn(e, out=out, pairs=pairs):
                n = len(pairs)
                ins = None
                for i, (l, r) in enumerate(pairs):
                    ins = e.matmul(out, l, r, start=(i == 0), stop=(i == n - 1))
                return ins
            return fn

        def wload(key, dst, src, reskey, rows=None):
            P.dma("pool", key, dst, src, writes=[reskey])

        P.dma("sp", "cols", cols, cols_d, writes=["cols"])
        P.op("pool", lambda e: e.memset(identF, 0.0), writes=["identF"])
        P.op("pool", lambda e: e.affine_select(out=identF, in_=identF, pattern=[[-1, 128]], compare_op=ALU.not_equal,
                                               fill=1.0, base=0, channel_multiplier=1),
             reads=["identF"], writes=["identF"])
        import os
        DBG = os.environ.get("KDBG", "")
        P.op("pool", lambda e: e.memset(onesD, 1.0 / 1024.0), writes=["onesD"])
        P.op("pool", lambda e: e.memset(onesV, 1.0 / 128.0), writes=["onesV"])
        P.op("pool", lambda e: e.memset(ones1, 1.0), writes=["ones1"])
        if "m" not in DBG:
            P.op("pool", lambda e: e.memset(mask2, 1.0), writes=["mask2"])
            P.op("pool", lambda e: e.affine_select(out=mask2, in_=mask2, pattern=[[1, 128]], compare_op=ALU.is_ge,
                                                   fill=0.0, base=0, channel_multiplier=-1),
                 reads=["mask2"], writes=["mask2"])
            P.op("pool", lambda e: e.memset(mask2[0:64, 64:128], 0.0), reads=["mask2"], writes=["mask2"])
        if "s" not in DBG:
            P.op("pool", lambda e: e.memset(scanmask, 1.0), writes=["scanmask"])
            P.op("pool", lambda e: e.memset(scanmask.rearrange("p (a b) -> p a b", b=64)[:, :, 0:1], 0.0),
                 reads=["scanmask"], writes=["scanmask"])
        if "i" not in DBG:
            P.op("pool", lambda e: e.iota(ioti, pattern=[[1, 16]], base=1, channel_multiplier=0), writes=["ioti"])
            P.op("pool", lambda e: e.tensor_copy(iot, ioti), reads=["ioti"], writes=["iot"])
            for g, w in enumerate(POOL_W):
                P.op("dve", lambda e, g=g, w=w: e.tensor_scalar_min(rcfix[:, g, :], iot, float(w)),
                     reads=["iot"], writes=[("rcfix", g)])
                P.op("dve", lambda e, g=g: e.reciprocal(rcfix[:, g, :], rcfix[:, g, :]),
                     reads=[("rcfix", g)], writes=[("rcfix", g)])
        if "l" not in DBG:
            P.op("dve", lambda e: e.tensor_tensor(lbc, cols[:, C_LBA:C_LBA + 4], cols[:, C_LBB:C_LBB + 4], ALU.subtract),
                 reads=["cols"], writes=["lbc"])
            P.op("act", lambda e: e.activation(lbc, lbc, AF.Sigmoid), reads=["lbc"], writes=["lbc"])
            P.op("dve", lambda e: e.tensor_scalar(omlc, lbc, -1.0, 1.0, ALU.mult, ALU.add), reads=["lbc"], writes=["omlc"])

        zsq_flat = zsq.rearrange("p c t -> p (c t)")

        def layernorm(t0, n, gcol, bcol, cast_eng="dve"):
            zsq = zsq_flat[:, :KC * n].rearrange("p (c t) -> p c t", c=KC)
            tok = slice(t0, t0 + n)
            rk = RK("RT", ALLC, t0, n)
            xk = RK("XT", ALLC, t0, n)
            P.op(cast_eng, lambda e: e.tensor_copy(XT[:, :, tok], RT[:, :, tok]), reads=rk, writes=xk)
            P.op("act", lambda e: e.activation(zsq[:, :, :n], RT[:, :, tok], AF.Square), reads=rk, writes=["zsq"])
            P.op("pe", mmgroup(psb[6][:, :n], [(onesD, XT[:, c, tok]) for c in ALLC]),
                 reads=xk + ["onesD"], writes=[("ps", 6)])
            if n <= 256:
                msq_ps, msq_key = psb[6][:, 256:256 + n], ("ps", 6)
            else:
                msq_ps, msq_key = psb[7][:, :n], ("ps", 7)
            P.op("pe", mmgroup(msq_ps, [(onesD, zsq[:, c, :n]) for c in ALLC]),
                 reads=["zsq", "onesD"], writes=[msq_key])
            P.op("act", lambda e: e.activation(mean[:, :n], psb[6][:, :n], AF.Identity),
                 reads=[("ps", 6)], writes=["mean"])
            P.op("dve", lambda e: e.tensor_tensor(m2[:, :n], mean[:, :n], mean[:, :n], ALU.mult),
                 reads=["mean"], writes=["m2"])
            P.op("dve", lambda e: e.tensor_tensor(m2[:, :n], msq_ps, m2[:, :n], ALU.subtract),
                 reads=[msq_key, "m2"], writes=["m2"])
            P.op("act", lambda e: e.activation(m2[:, :n], m2[:, :n], AF.Ln, bias=LN_EPS, scale=1.0),
                 reads=["m2"], writes=["m2"])
            P.op("act", lambda e: e.activation(rstd[:, :n], m2[:, :n], AF.Exp, scale=-0.5),
                 reads=["m2"], writes=["rstd"])
            P.op("dve", lambda e: e.tensor_tensor(RT[:, :, tok], RT[:, :, tok],
                                                  mean[:, :n].unsqueeze(1).to_broadcast([128, KC, n]), ALU.subtract),
                 reads=rk + ["mean"], writes=rk)
            P.op("dve", lambda e: e.tensor_tensor(RT[:, :, tok], RT[:, :, tok],
                                                  rstd[:, :n].unsqueeze(1).to_broadcast([128, KC, n]), ALU.mult),
                 reads=rk + ["rstd"], writes=rk)
            for c in ALLC:
                P.op("act", lambda e, c=c: e.activation(RT[:, c, tok], RT[:, c, tok], AF.Identity,
                                                        scale=cols[:, gcol + c:gcol + c + 1],
                                                        bias=cols[:, bcol + c:bcol + c + 1]),
                     reads=RK("RT", [c], t0, n) + ["cols"], writes=RK("RT", [c], t0, n))
            P.op(cast_eng, lambda e: e.tensor_copy(XT[:, :, tok], RT[:, :, tok]), reads=rk, writes=xk)

        A.off = PH
        xstage = [A.f32(D), A.f32(D)]

        def load_x(tiles):
          for tt in tiles:
              sl = tt % 2
              P.dma("sp", f"xs{sl}", xstage[sl], x_d[tt * 128:(tt + 1) * 128, :], writes=[("xs", sl)])
              for hb in range(2):
                  b = 6 + hb

                  def trf(e, sl=sl, hb=hb, b=b):
                      ins = None
                      for q in range(4):
                          c = hb * 4 + q
                          ins = e.transpose(psb[b][:, q * 128:(q + 1) * 128], xstage[sl][:, c * 128:(c + 1) * 128], identF)
                      return ins
                  P.op("pe", trf, reads=[("xs", sl), "identF"], writes=[("ps", b)])
                  cs = list(range(hb * 4, hb * 4 + 4))
                  pv = psb[b].rearrange("p (a b) -> p a b", a=4)
                  P.op("dve", lambda e, hb=hb, tt=tt, pv=pv: e.tensor_copy(RT[:, hb * 4:hb * 4 + 4, tt * 128:(tt + 1) * 128], pv),
                       reads=[("ps", b)], writes=RK("RT", cs, tt * 128, 128))
                  if "A" not in DBG:
                      P.op("act", lambda e, hb=hb, tt=tt: e.activation(XT[:, hb * 4:hb * 4 + 4, tt * 128:(tt + 1) * 128],
                                                                       RT[:, hb * 4:hb * 4 + 4, tt * 128:(tt + 1) * 128], AF.Identity),
                           reads=RK("RT", cs, tt * 128, 128), writes=RK("XT", cs, tt * 128, 128))


        load_x(range(NT // 2))

        def ffn(idx, gcol, bcol, pre=None, newphase=False):
            if newphase:
                P.new_phase()
            A.off = PH + 2 * D
            H = A.bf16(JC, HALF)
            wi = [A.bf16(KC, 512), A.bf16(KC, 512)]
            wo = [A.bf16(JC, 128), A.bf16(JC, 128)]
            sg = [A.f32(TG), A.f32(TG)]
            w_in_d, w_out_d = wf_in_d[idx], wf_out_d[idx]
            cnt = [0, 0, 0]

            def Aph(half):
                t0h = half * HALF
                for jj in range(JC // 2):
                    sl = cnt[0] % 2
                    cnt[0] += 1
                    wload(f"wi{sl}", wi[sl].rearrange("p k n -> p (k n)").rearrange("p (r e) -> p r e", e=2048),
                          w_in_d[jj].rearrange("p (r e) -> p r e", e=2048), ("wi", sl))
                    for jl in range(2):
                        j = 2 * jj + jl
                        for tg in range(NTGH):
                            t0 = t0h + tg * TG
                            tok = slice(t0, t0 + TG)
                            i = cnt[1] % 2
                            cnt[1] += 1
                            gb, ub = i, 2 + i
                            xk = RK("XT", ALLC, t0, TG)
                            P.op("pe", mmgroup(psb[gb][:, :TG], [(wi[sl][:, k, jl * 128:(jl + 1) * 128], XT[:, k, tok]) for k in ALLC]),
                                 reads=[("wi", sl)] + xk, writes=[("ps", gb)])
                            P.op("pe", mmgroup(psb[ub][:, :TG], [(wi[sl][:, k, 256 + jl * 128:256 + (jl + 1) * 128], XT[:, k, tok]) for k in ALLC]),
                                 reads=[("wi", sl)] + xk, writes=[("ps", ub)])
                            P.op("act", lambda e, i=i, gb=gb: e.activation(sg[i], psb[gb][:, :TG], AF.Silu),
                                 reads=[("ps", gb)], writes=[("sg", i)])
                            P.op("dve", lambda e, i=i, ub=ub, j=j, tg=tg: e.scalar_tensor_tensor(
                                H[:, j, tg * TG:(tg + 1) * TG], sg[i], 0.5, psb[ub][:, :TG], ALU.mult, ALU.mult),
                                 reads=[("sg", i), ("ps", ub)], writes=[("H", j, tg)])

            def Bph(half):
                t0h = half * HALF
                for m in range(KC):
                    sl = cnt[2] % 2
                    cnt[2] += 1
                    wload(f"wo{sl}", wo[sl].rearrange("p j n -> p (j n)").rearrange("p (r e) -> p r e", e=1408),
                          w_out_d[m].rearrange("p (r e) -> p r e", e=1408), ("wo", sl))
                    for tg in range(NTGH):
                        t0 = t0h + tg * TG
                        tok = slice(t0, t0 + TG)
                        ob = 4 + (m * NTGH + tg) % 2
                        P.op("pe", mmgroup(psb[ob][:, :TG], [(wo[sl][:, j, :], H[:, j, tg * TG:(tg + 1) * TG]) for j in range(JC)]),
                             reads=[("wo", sl)] + [("H", j, tg) for j in range(JC)], writes=[("ps", ob)])
                        rk = RK("RT", [m], t0, TG)
                        P.op("dve", lambda e, m=m, tok=tok, ob=ob: e.scalar_tensor_tensor(
                            RT[:, m, tok], RT[:, m, tok], ALPHA, psb[ob][:, :TG], ALU.mult, ALU.add),
                             reads=[("ps", ob)] + rk, writes=rk)

            def LNs(half):
                for tg in range(NTGH):
                    layernorm(half * HALF + tg * TG, TG, gcol, bcol)

            if pre is not None:
                P.interleave([pre, lambda: Aph(0)], weights=[1, 6])
            else:
                Aph(0)
            Bph(0)
            P.interleave([lambda: LNs(0), lambda: Aph(1)], weights=[1, 6])
            Bph(1)
            return lambda: LNs(1)

        tail = None
        if stop_after >= 1:
            tail = ffn(0, C_LN + 0, C_LN + 8, pre=lambda: load_x(range(NT // 2, NT)))
        else:
            load_x(range(NT // 2, NT))

        def mixer(gcol, bcol, pre=None):
            P.new_phase()
            A.off = PH
            NTU = U // 128
            NCH = U // 64
            NUU = S // U
            wmi = A.bf16(KC, 2560)
            wmo = A.bf16(KC, D)
            poolw = A.bf16(512)
            vtok = [A.bf16(NTU, 512), A.bf16(NTU, 512)]
            MT = A.bf16(KC, U)
            fl_off = A.off
            F_, L_, C_, E_ = A.f32(U), A.f32(U), A.f32(U), A.f32(U)
            PAIR = (TG >= 2 * U)
            if PAIR:
                F2 = arena[:, fl_off:fl_off + 2 * U].rearrange("p (a b) -> p a b", a=2)
                L2 = arena[:, fl_off + 2 * U:fl_off + 4 * U].rearrange("p (a b) -> p a b", a=2)
                zu = zsq_off + KC * U // 2
                C2 = arena[:, zu:zu + 2 * U].rearrange("p (a b) -> p a b", a=2)
                E2 = arena[:, zu + 2 * U:zu + 4 * U].rearrange("p (a b) -> p a b", a=2)
                sgt2 = arena[:, mean_off:mean_off + 2 * TG].rearrange("p (a b) -> p a b", a=2)[:, :, U:2 * U]
            qd4 = [A.bf16(4, U), A.bf16(4, U)]
            kd4 = [A.bf16(4, U), A.bf16(4, U)]
            kltok4 = [A.bf16(4, U), A.bf16(4, U)]
            EL = [A.f32(4, NCH), A.f32(4, NCH)]
            sgt_default = mean[:, U:2 * U] if TG >= 2 * U else A.f32(U)
            Sf = A.f32(4, 128)
            Sb = A.bf16(4, 128)
            sT4 = A.bf16(4, 128)
            osq2 = A.bf16(2, U)
            rs2 = A.f32(2, U)
            sgg2 = A.f32(2, U)
            VP = [A.f32(16 + U), A.f32(16 + U), A.f32(16 + U)]
            halo = A.f32(4, 16)
            pooled = A.bf16(U)
            pfix = A.f32(16)
            OB = (3, 5)
            rrA = [0]
            rrC = [0]

            def nbA():
                return 0

            def nbC():
                rrC[0] ^= 1
                return (2, 7)[rrC[0]]

            def prologue():
                for b in (1, 0, 2, 3, 4):
                    wload(f"wmi{b}", wmi[:, :, b * 512:(b + 1) * 512], wmi_d[b].rearrange("p (k n) -> p k n", k=KC), ("wmi", b))
                for b in range(2):
                    wload(f"wmo{b}", wmo[:, :, b * 512:(b + 1) * 512], wmo_d[b].rearrange("p (k n) -> p k n", k=KC), ("wmo", b))
                wload("poolw", poolw, poolw_d, "poolw")
                P.op("dve", lambda e: e.memset(Sf, 0.0), writes=["Sf"])
                P.op("dve", lambda e: e.memset(Sb, 0.0), writes=["Sb"])
                P.op("dve", lambda e: e.memset(halo, 0.0), writes=[("halo", h) for h in range(4)])
                P.op("dve", lambda e: e.memset(VP[1], 0.0), writes=[("VP", 1)])
                P.op("dve", lambda e: e.memset(VP[2], 0.0), writes=[("VP", 2)])
                V(0)
                for h in range(4):
                    A1(0, h, sgt=VP[1][:, :U], sgk=("VP", 1))

            def V(u):
                vs = vtok[u % 2]
                for ti in range(NTU):
                    b = nbA()
                    ts0 = u * U + ti * 128
                    tsl = slice(ts0, ts0 + 128)
                    P.op("pe", mmgroup(psb[b], [(XT[:, k, tsl], wmi[:, k, 1024:1536]) for k in ALLC]),
                         reads=RK("XT", ALLC, ts0, 128) + [("wmi", 2)], writes=[("ps", b)])
                    P.op("act", lambda e, ti=ti, b=b, vs=vs: e.activation(vs[:, ti, :], psb[b], AF.Identity),
                         reads=[("ps", b)], writes=[("vtok", u % 2, ti)])

            def A1(u, h, sgt=None, sgk="sgt"):
                sgt = sgt_default if sgt is None else sgt
                s_ = u % 2
                t0 = u * U
                tok = slice(t0, t0 + U)
                xk = RK("XT", ALLC, t0, U)
                b = nbA()
                P.op("pe", mmgroup(psb[b][:, :U], [(wmi[:, k, 512 + h * 128:512 + (h + 1) * 128], XT[:, k, tok]) for k in ALLC]),
                     reads=xk + [("wmi", 1)], writes=[("ps", b)])
                P.op("act", lambda e, b=b: e.activation(F_, psb[b][:, :U], AF.Exp, scale=-1.0), reads=[("ps", b)], writes=["F_"])
                P.op("act", lambda e: e.activation(F_, F_, AF.Ln, bias=1.0, scale=1.0), reads=["F_"], writes=["F_"])
                P.op("act", lambda e: e.activation(F_, F_, AF.Exp, scale=-1.0), reads=["F_"], writes=["F_"])
                P.op("dve", lambda e, h=h: e.tensor_scalar(F_, F_, omlc[:, h:h + 1], lbc[:, h:h + 1], ALU.mult, ALU.add),
                     reads=["F_", "lbc", "omlc"], writes=["F_"])
                P.op("act", lambda e: e.activation(L_, F_, AF.Ln), reads=["F_"], writes=["L_"])
                P.op("dve", lambda e: e.tensor_scalar(F_, F_, -1.0, 1.0, ALU.mult, ALU.add), reads=["F_"], writes=["F_"])
                P.op("dve", lambda e: e.tensor_tensor_scan(C_, scanmask, L_, 0.0, ALU.mult, ALU.add),
                     reads=["L_", "scanmask"], writes=["C_"])
                P.op("act", lambda e: e.activation(E_, C_, AF.Exp), reads=["C_"], writes=["E_"])
                P.op("act", lambda e: e.activation(L_, C_, AF.Exp, scale=-1.0), reads=["C_"], writes=["L_"])
                P.op("dve", lambda e: e.tensor_tensor(L_, F_, L_, ALU.mult), reads=["F_", "L_"], writes=["L_"])
                P.op("pool", lambda e, s_=s_, h=h: e.tensor_copy(kd4[s_][:, h, :], L_),
                     reads=["L_"], writes=[("kd", s_, h)])
                P.op("pool", lambda e, s_=s_, h=h: e.tensor_copy(EL[s_][:, h, :], E_.rearrange("p (a b) -> p a b", b=64)[:, :, 63]),
                     reads=["E_"], writes=[("EL", s_)])
                P.op("dve", lambda e: e.tensor_tensor(
                    C_.rearrange("p (a b) -> p a b", b=64), L_.rearrange("p (a b) -> p a b", b=64),
                    E_.rearrange("p (a b) -> p a b", b=64)[:, :, 63:64].to_broadcast([128, NCH, 64]), ALU.mult),
                     reads=["L_", "E_"], writes=["C_"])
                b = nbA()
                P.op("pe", mmgroup(psb[b][:, :U], [(wmi[:, k, h * 128:(h + 1) * 128], XT[:, k, tok]) for k in ALLC]),
                     reads=xk + [("wmi", 0)], writes=[("ps", b)])
                P.op("act", lambda e, b=b: e.activation(sgt, psb[b][:, :U], AF.Exp, scale=-1.0), reads=[("ps", b)], writes=[sgk])
                P.op("act", lambda e: e.activation(sgt, sgt, AF.Ln, bias=1.0, scale=1.0), reads=[sgk], writes=[sgk])
                P.op("act", lambda e: e.activation(sgt, sgt, AF.Exp, scale=-1.0), reads=[sgk], writes=[sgk])
                P.op("dve", lambda e: e.tensor_tensor(sgt, sgt, E_, ALU.mult), reads=[sgk, "E_"], writes=[sgk])
                P.op("dve", lambda e, s_=s_, h=h, b=b: e.tensor_tensor(qd4[s_][:, h, :], psb[b][:, :U], sgt, ALU.mult),
                     reads=[sgk, ("ps", b)], writes=[("qd", s_, h)])
                b = nbA()

                def trk(e, b=b):
                    ins = None
                    for ti in range(NTU):
                        ins = e.transpose(psb[b][:, ti * 128:(ti + 1) * 128], C_[:, ti * 128:(ti + 1) * 128], identF)
                    return ins
                P.op("pe", trk, reads=["C_", "identF"], writes=[("ps", b)])
                P.op("act", lambda e, b=b, s_=s_, h=h: e.activation(kltok4[s_][:, h, :], psb[b][:, :U], AF.Identity),
                     reads=[("ps", b)], writes=[("kltok", s_, h)])

            def A1p(u, hp):
                s_ = u % 2
                t0 = u * U
                tok = slice(t0, t0 + U)
                xk = RK("XT", ALLC, t0, U)
                hh = (2 * hp, 2 * hp + 1)
                b = nbA()
                for j, h in enumerate(hh):
                    P.op("pe", mmgroup(psb[b][:, j * U:(j + 1) * U], [(wmi[:, k, 512 + h * 128:512 + (h + 1) * 128], XT[:, k, tok]) for k in ALLC]),
                         reads=xk + [("wmi", 1)], writes=[("ps", b)])
                pf = psb[b][:, :2 * U].rearrange("p (a b) -> p a b", a=2)
                P.op("act", lambda e, pf=pf: e.activation(F2, pf, AF.Exp, scale=-1.0), reads=[("ps", b)], writes=["F2"])
                P.op("act", lambda e: e.activation(F2, F2, AF.Ln, bias=1.0, scale=1.0), reads=["F2"], writes=["F2"])
                P.op("act", lambda e: e.activation(F2, F2, AF.Exp, scale=-1.0), reads=["F2"], writes=["F2"])
                for j, h in enumerate(hh):
                    P.op("act", lambda e, j=j, h=h: e.activation(F2[:, j, :], F2[:, j, :], AF.Identity,
                                                                 scale=omlc[:, h:h + 1], bias=lbc[:, h:h + 1]),
                         reads=["F2", "lbc", "omlc"], writes=["F2"])
                P.op("act", lambda e: e.activation(L2, F2, AF.Ln), reads=["F2"], writes=["L2"])
                P.op("act", lambda e: e.activation(F2, F2, AF.Identity, scale=-1.0, bias=1.0), reads=["F2"], writes=["F2"])
                for j in range(2):
                    P.op("dve", lambda e, j=j: e.tensor_tensor_scan(C2[:, j, :], scanmask, L2[:, j, :], 0.0, ALU.mult, ALU.add),
                         reads=["L2", "scanmask"], writes=["C2"])
                P.op("act", lambda e: e.activation(E2, C2, AF.Exp), reads=["C2"], writes=["E2"])
                P.op("act", lambda e: e.activation(L2, C2, AF.Exp, scale=-1.0), reads=["C2"], writes=["L2"])
                P.op("dve", lambda e: e.tensor_tensor(L2, F2, L2, ALU.mult), reads=["F2", "L2"], writes=["L2"])
                P.op("pool", lambda e, s_=s_, hp=hp: e.tensor_copy(kd4[s_][:, 2 * hp:2 * hp + 2, :], L2),
                     reads=["L2"], writes=[("kd", s_, 2 * hp), ("kd", s_, 2 * hp + 1)])
                E4 = E2.rearrange("p a (c b) -> p a c b", b=64)
                P.op("pool", lambda e, s_=s_, hp=hp: e.tensor_copy(EL[s_][:, 2 * hp:2 * hp + 2, :], E4[:, :, :, 63]),
                     reads=["E2"], writes=[("EL", s_)])
                P.op("dve", lambda e: e.tensor_tensor(
                    C2.rearrange("p a (c b) -> p (a c) b", b=64), L2.rearrange("p a (c b) -> p (a c) b", b=64),
                    E2.rearrange("p a (c b) -> p (a c) b", b=64)[:, :, 63:64].to_broadcast([128, 2 * NCH, 64]), ALU.mult),
                     reads=["L2", "E2"], writes=["C2"])
                b = nbA()
                for j, h in enumerate(hh):
                    P.op("pe", mmgroup(psb[b][:, j * U:(j + 1) * U], [(wmi[:, k, h * 128:(h + 1) * 128], XT[:, k, tok]) for k in ALLC]),
                         reads=xk + [("wmi", 0)], writes=[("ps", b)])
                pq = psb[b][:, :2 * U].rearrange("p (a b) -> p a b", a=2)
                P.op("act", lambda e, pq=pq: e.activation(sgt2, pq, AF.Exp, scale=-1.0), reads=[("ps", b)], writes=["sgt2"])
                P.op("act", lambda e: e.activation(sgt2, sgt2, AF.Ln, bias=1.0, scale=1.0), reads=["sgt2"], writes=["sgt2"])
                P.op("act", lambda e: e.activation(sgt2, sgt2, AF.Exp, scale=-1.0), reads=["sgt2"], writes=["sgt2"])
                P.op("dve", lambda e: e.tensor_tensor(sgt2, sgt2, E2, ALU.mult), reads=["sgt2", "E2"], writes=["sgt2"])
                P.op("dve", lambda e, s_=s_, hp=hp, pq=pq: e.tensor_tensor(qd4[s_][:, 2 * hp:2 * hp + 2, :], pq, sgt2, ALU.mult),
                     reads=["sgt2", ("ps", b)], writes=[("qd", s_, 2 * hp), ("qd", s_, 2 * hp + 1)])
                b = nbA()

                def trk(e, b=b):
                    ins = None
                    for j in range(2):
                        for ti in range(NTU):
                            ins = e.transpose(psb[b][:, j * U + ti * 128:j * U + (ti + 1) * 128], C2[:, j, ti * 128:(ti + 1) * 128], identF)
                    return ins
                P.op("pe", trk, reads=["C2", "identF"], writes=[("ps", b)])
                P.op("act", lambda e, b=b, s_=s_, hp=hp: e.activation(
                    kltok4[s_][:, 2 * hp:2 * hp + 2, :], psb[b][:, :2 * U].rearrange("p (a b) -> p a b", a=2), AF.Identity),
                     reads=[("ps", b)], writes=[("kltok", s_, 2 * hp), ("kltok", s_, 2 * hp + 1)])

            def A1any(u, hlist):
                hlist = list(hlist)
                if PAIR and len(hlist) % 2 == 0:
                    for hp in sorted(set(h // 2 for h in hlist)):
                        A1p(u, hp)
                else:
                    for h in hlist:
                        A1(u, h)

            H4 = list(range(4))

            def Bchunk(u, c):
                s_ = u % 2
                ti, cc = c // 2, c % 2
                c0 = ti * 128
                q0 = c * 64
                r0 = cc * 64
                if cc == 0:
                    def sc(e):
                        ins = None
                        for h in H4:
                            ins = e.matmul(psb[4][:, h * 128:(h + 1) * 128], kd4[s_][:, h, c0:c0 + 128], qd4[s_][:, h, c0:c0 + 128],
                                           start=True, stop=True)
                        return ins
                    P.op("pe", sc, reads=[("kd", s_, h) for h in H4] + [("qd", s_, h) for h in H4], writes=[("ps", 4)])
                    P.op("dve", lambda e: e.tensor_tensor(sT4, psb[4].rearrange("p (a b) -> p a b", a=4),
                                                          mask2.unsqueeze(1).to_broadcast([128, 4, 128]), ALU.mult),
                         reads=[("ps", 4), "mask2"], writes=["sT4"])

                def pm(e):
                    ins = None
                    for h in H4:
                        ins = e.matmul(psb[4][:, h * 128:(h + 1) * 128], kltok4[s_][r0:r0 + 64, h, c0:c0 + 128],
                                       vtok[s_][r0:r0 + 64, ti, h * 128:(h + 1) * 128], start=True, stop=True)
                    return ins
                P.op("pe", pm, reads=[("kltok", s_, h) for h in H4] + [("vtok", s_, ti)], writes=[("ps", 4)])

                def io(e):
                    ins = None
                    for h in H4:
                        ob = OB[h // 2]
                        col = (h % 2) * U + q0
                        e.matmul(psb[ob][:, col:col + 64], vtok[s_][:, ti, h * 128:(h + 1) * 128], sT4[:, h, r0:r0 + 64],
                                 start=True, stop=False)
                        ins = e.matmul(psb[ob][:, col:col + 64], Sb[:, h, :], qd4[s_][:, h, q0:q0 + 64], start=False, stop=True)
                    return ins
                P.op("pe", io, reads=["sT4", ("vtok", s_, ti), "Sb"] + [("qd", s_, h) for h in H4],
                     writes=[("ps", OB[0]), ("ps", OB[1])])
                P.op("dve", lambda e: e.tensor_tensor(Sf, Sf, EL[s_][:, :, c:c + 1].to_broadcast([128, 4, 128]), ALU.mult),
                     reads=["Sf", ("EL", s_)], writes=["Sf"])
                P.op("dve", lambda e: e.tensor_tensor(Sf, Sf, psb[4].rearrange("p (a b) -> p a b", a=4), ALU.add),
                     reads=["Sf", ("ps", 4)], writes=["Sf"])
                P.op("act", lambda e: e.activation(Sb, Sf, AF.Identity), reads=["Sf"], writes=["Sb"])

            def Cnorm(u, hp):
                t0 = u * U
                tok = slice(t0, t0 + U)
                xk = RK("XT", ALLC, t0, U)
                ob = OB[hp]
                pv = psb[ob][:, :2 * U].rearrange("p (a b) -> p a b", a=2)
                P.op("act", lambda e: e.activation(osq2, pv, AF.Square), reads=[("ps", ob)], writes=["osq2"])
                P.op("pe", mmgroup(psb[4][:, :2 * U], [(onesV, osq2.rearrange("p a b -> p (a b)"))]),
                     reads=["osq2", "onesV"], writes=[("ps", 4)])
                P.op("act", lambda e: e.activation(rs2, psb[4][:, :2 * U].rearrange("p (a b) -> p a b", a=2), AF.Ln,
                                                   bias=RMS_EPS, scale=1.0),
                     reads=[("ps", 4)], writes=["rs2"])
                P.op("act", lambda e: e.activation(rs2, rs2, AF.Exp, scale=-0.5), reads=["rs2"], writes=["rs2"])
                P.op("dve", lambda e: e.tensor_tensor(rs2, pv, rs2, ALU.mult), reads=[("ps", ob), "rs2"], writes=["rs2"])
                b = 1
                for j in range(2):
                    h = 2 * hp + j
                    P.op("pe", mmgroup(psb[b][:, j * U:(j + 1) * U],
                                       [(wmi[:, k, 1536 + h * 128:1536 + (h + 1) * 128], XT[:, k, tok]) for k in ALLC]),
                         reads=xk + [("wmi", 3)], writes=[("ps", b)])
                gv = psb[b][:, :2 * U].rearrange("p (a b) -> p a b", a=2)
                P.op("act", lambda e, gv=gv: e.activation(sgg2, gv, AF.Exp, scale=-1.0), reads=[("ps", b)], writes=["sgg2"])
                P.op("act", lambda e: e.activation(sgg2, sgg2, AF.Ln, bias=1.0, scale=1.0), reads=["sgg2"], writes=["sgg2"])
                P.op("act", lambda e: e.activation(sgg2, sgg2, AF.Exp, scale=-1.0), reads=["sgg2"], writes=["sgg2"])
                P.op("dve", lambda e: e.scalar_tensor_tensor(rs2, rs2, cols[:, C_GN:C_GN + 1], sgg2, ALU.mult, ALU.mult),
                     reads=["rs2", "sgg2", "cols"], writes=["rs2"])
                P.op("dve", lambda e, hp=hp, gv=gv: e.tensor_tensor(MT[:, 2 * hp:2 * hp + 2, :], gv, rs2, ALU.mult),
                     reads=["rs2", ("ps", b)], writes=[("MT", 2 * hp), ("MT", 2 * hp + 1)])

            def Cpool(u, h):
                t0 = u * U
                tok = slice(t0, t0 + U)
                xk = RK("XT", ALLC, t0, U)
                w = POOL_W[h]
                b = nbC()
                P.op("pe", mmgroup(psb[b][:, :U], [(wmi[:, k, 2048 + h * 128:2048 + (h + 1) * 128], XT[:, k, tok]) for k in ALLC]),
                     reads=xk + [("wmi", 4)], writes=[("ps", b)])
                P.op("act", lambda e, b=b: e.activation(VP[0][:, 16:16 + U], psb[b][:, :U], AF.Identity),
                     reads=[("ps", b)], writes=[("VP", 0)])
                P.op("act", lambda e, h=h: e.activation(VP[0][:, 0:16], halo[:, h, :], AF.Identity), reads=[("halo", h)], writes=[("VP", 0)])
                P.op("act", lambda e, h=h: e.activation(halo[:, h, :], VP[0][:, U:U + 16], AF.Identity), reads=[("VP", 0)], writes=[("halo", h)])
                src = 0
                step = 1
                while step < w:
                    dst = 1 if src != 1 else 2
                    P.op("dve", lambda e, src=src, dst=dst, step=step: e.tensor_tensor(
                        VP[dst][:, step:16 + U], VP[src][:, step:16 + U], VP[src][:, 0:16 + U - step], ALU.add),
                         reads=[("VP", src)], writes=[("VP", dst)])
                    src = dst
                    step *= 2
                P.op("dve", lambda e, src=src, w=w: e.scalar_tensor_tensor(
                    pooled, VP[src][:, 16:16 + U], 1.0 / w, VP[0][:, 16:16 + U], ALU.mult, ALU.subtract),
                     reads=[("VP", src), ("VP", 0)], writes=["pooled"])
                if u == 0:
                    P.op("dve", lambda e, src=src, h=h: e.tensor_tensor(pfix, VP[src][:, 16:32], rcfix[:, h, :], ALU.mult),
                         reads=[("VP", src), ("rcfix", h)], writes=["pfix"])
                    P.op("dve", lambda e: e.tensor_tensor(pooled[:, 0:16], pfix, VP[0][:, 16:32], ALU.subtract),
                         reads=["pfix", ("VP", 0), "pooled"], writes=["pooled"])
                b = nbC()
                P.op("pe", mmgroup(psb[b][:, :U], [(poolw[:, h * 128:(h + 1) * 128], pooled)]),
                     reads=["poolw", "pooled"], writes=[("ps", b)])
                P.op("act", lambda e, b=b, h=h: e.activation(MT[:, 4 + h, :], psb[b][:, :U], AF.Identity,
                                                             scale=cols[:, C_PSC + h:C_PSC + h + 1]),
                     reads=[("ps", b), "cols"], writes=[("MT", 4 + h)])

            def Dout(u):
                t0 = u * U
                tok = slice(t0, t0 + U)
                for m in range(KC):
                    b = nbC()
                    P.op("pe", mmgroup(psb[b][:, :U], [(wmo[:, k, m * 128:(m + 1) * 128], MT[:, k, :]) for k in ALLC]),
                         reads=[("MT", k) for k in ALLC] + [("wmo", m // 4)], writes=[("ps", b)])
                    rk = RK("RT", [m], t0, U)
                    P.op("dve", lambda e, m=m, tok=tok, b=b: e.scalar_tensor_tensor(
                        RT[:, m, tok], RT[:, m, tok], ALPHA, psb[b][:, :U], ALU.mult, ALU.add),
                         reads=[("ps", b)] + rk, writes=rk)

            P.interleave([pre, prologue])
            P.op("dve", lambda e: e.memset(pfix, 0.0),
                 writes=["mean", "m2", "rstd", "zsq", "sgt", "sgt2", "C2", "E2", "F2", "L2", "F_", "L_", "C_", "E_", "pfix", ("VP", 1)])
            for i in range(NUU + 1):
                streams = []

                def s1(i=i):
                    if 1 <= i:
                        Cnorm(i - 1, 0)
                        Cnorm(i - 1, 1)
                    if i < NUU:
                        for c in range(NCH):
                            Bchunk(i, c)
                    if i >= 2:
                        layernorm((i - 2) * U, U, gcol, bcol)
                streams.append(s1)

                def s2(i=i):
                    if i + 1 < NUU:
                        V(i + 1)
                        A1any(i + 1, H4)
                if i + 1 < NUU:
                    streams.append(s2)

                def s3(i=i):
                    for h in H4:
                        Cpool(i - 1, h)
                    Dout(i - 1)
                if i >= 1:
                    streams.append(s3)
                P.interleave(streams)
            return lambda: layernorm((NUU - 1) * U, U, gcol, bcol, cast_eng="dve")

        if stop_after >= 2:
            tail = mixer(C_LN + 16, C_LN + 24, pre=tail)

        def attention(gcol, bcol, pre=None):
            P.new_phase()
            A.off = PH
            wq = A.bf16(KC, D)
            wo_ = A.bf16(KC, D)
            kT = A.bf16(KC, NMEM)
            vm = A.bf16(2, D)
            qT0 = A.bf16(KC, TG)
            R0 = A.off
            memT = A.bf16(KC, NMEM)
            mstage = A.f32(2, D)
            wk = A.bf16(KC, D)
            wv = A.bf16(KC, D)
            def prep():
                for b in range(2):
                    wload(f"wq{b}", wq[:, :, b * 512:(b + 1) * 512], wq_d[b].rearrange("p (k n) -> p k n", k=KC), ("wq", b))
                for b in range(2):
                    wload(f"wk{b}", wk[:, :, b * 512:(b + 1) * 512], wk_d[b].rearrange("p (k n) -> p k n", k=KC), ("wk", b))
                for b in range(2):
                    wload(f"wv{b}", wv[:, :, b * 512:(b + 1) * 512], wv_d[b].rearrange("p (k n) -> p k n", k=KC), ("wv", b))
                for b in range(2):
                    wload(f"wox{b}", wo_[:, :, b * 512:(b + 1) * 512], wo_d[b].rearrange("p (k n) -> p k n", k=KC), ("wo_", b))
                for m in range(KC):
                    b = nextbank()
                    P.op("pe", mmgroup(psb[b][:, :TG], [(wq[:, k, m * 128:(m + 1) * 128], XT[:, k, 0:TG]) for k in ALLC]),
                         reads=RK("XT", ALLC, 0, TG) + [("wq", m // 4)], writes=[("ps", b)])
                    P.op("act", lambda e, m=m, b=b: e.activation(qT0[:, m, :], psb[b][:, :TG], AF.Identity, scale=1.0 / 16.0),
                         reads=[("ps", b)], writes=[("qT", 0, m)])
                for ti in range(2):
                    P.dma("sp", f"mst{ti}", mstage[:, ti, :], mem_d[ti * 128:(ti + 1) * 128, :], writes=[("mst", ti)])
                    for hb in range(2):
                        b = nextbank()

                        def trm(e, ti=ti, hb=hb, b=b):
                            ins = None
                            for q in range(4):
                                c = hb * 4 + q
                                ins = e.transpose(psb[b][:, q * 128:(q + 1) * 128], mstage[:, ti, c * 128:(c + 1) * 128], identF)
                            return ins
                        P.op("pe", trm, reads=[("mst", ti), "identF"], writes=[("ps", b)])
                        P.op("act", lambda e, ti=ti, hb=hb, b=b: e.activation(
                            memT[:, hb * 4:hb * 4 + 4, ti * 128:(ti + 1) * 128], psb[b].rearrange("p (a b) -> p a b", a=4), AF.Identity),
                             reads=[("ps", b)], writes=[("memT", ti, hb)])
                memk = [("memT", ti, hb) for ti in range(2) for hb in range(2)]
                for dc in range(KC):
                    b = nextbank()
                    P.op("pe", mmgroup(psb[b][:, :NMEM], [(wk[:, k, dc * 128:(dc + 1) * 128], memT[:, k, :]) for k in ALLC]),
                         reads=memk + [("wk", dc // 4)], writes=[("ps", b)])
                    P.op("act", lambda e, dc=dc, b=b: e.activation(kT[:, dc, :], psb[b][:, :NMEM], AF.Identity),
                         reads=[("ps", b)], writes=[("kT", dc)])
                for mi in range(2):
                    for nh in range(2):
                        b = nextbank()
                        P.op("pe", mmgroup(psb[b], [(memT[:, k, mi * 128:(mi + 1) * 128], wv[:, k, nh * 512:(nh + 1) * 512]) for k in ALLC]),
                             reads=memk + [("wv", nh)], writes=[("ps", b)])
                        P.op("dve", lambda e, mi=mi, nh=nh, b=b: e.tensor_copy(vm[:, mi, nh * 512:(nh + 1) * 512], psb[b]),
                             reads=[("ps", b)], writes=[("vm", mi, nh)])

            P.interleave([pre, prep])
            P.new_phase()
            A.off = R0
            NTG = S // TG
            qT = [qT0, A.bf16(KC, TG)]
            oT = [A.bf16(KC, TG), A.bf16(KC, TG)]
            ET = [[A.bf16(TG), A.bf16(TG)] for _ in range(4)]
            rec = [A.f32(TG), A.f32(TG)]
            rq = [0]
            rp = [0]
            rs_ = [0]

            def nbQ():
                rq[0] ^= 1
                return (2, 7)[rq[0]]

            HN = TG // 2 if TG >= 512 else TG

            def lnq(tg):
                for t0 in range(tg * TG, (tg + 1) * TG, HN):
                    layernorm(t0, HN, gcol, bcol, cast_eng="dve")

            def nbP():
                rp[0] ^= 1
                return (0, 1)[rp[0]]

            def nbS():
                rs_[0] ^= 1
                return (4, 5)[rs_[0]]

            def Qp(tg):
                t0 = tg * TG
                tok = slice(t0, t0 + TG)
                xk = RK("XT", ALLC, t0, TG)
                q = qT[tg % 2]
                for m in range(KC):
                    b = nbQ()
                    P.op("pe", mmgroup(psb[b][:, :TG], [(wq[:, k, m * 128:(m + 1) * 128], XT[:, k, tok]) for k in ALLC]),
                         reads=xk + [("wq", m // 4)], writes=[("ps", b)])
                    P.op("act", lambda e, m=m, b=b, q=q: e.activation(q[:, m, :], psb[b][:, :TG], AF.Identity, scale=1.0 / 16.0),
                         reads=[("ps", b)], writes=[("qT", tg % 2, m)])

            def Hd(tg):
                q = qT[tg % 2]
                o = oT[tg % 2]
                for h in range(4):
                    for mi in range(2):
                        sb_ = nbS()
                        P.op("pe", mmgroup(psb[sb_][:, :TG], [(kT[:, 2 * h + dd, mi * 128:(mi + 1) * 128], q[:, 2 * h + dd, :]) for dd in range(2)]),
                             reads=[("kT", 2 * h), ("kT", 2 * h + 1), ("qT", tg % 2, 2 * h), ("qT", tg % 2, 2 * h + 1)], writes=[("ps", sb_)])
                        P.op("act", lambda e, h=h, mi=mi, sb_=sb_: e.activation(ET[h][mi], psb[sb_][:, :TG], AF.Exp),
                             reads=[("ps", sb_)], writes=[("ET", h, mi)])
                for h in range(4):
                    i = h % 2
                    P.op("pe", mmgroup(psb[3][:, :TG], [(ones1, ET[h][mi]) for mi in range(2)]),
                         reads=[("ET", h, 0), ("ET", h, 1), "ones1"], writes=[("ps", 3)])
                    P.op("act", lambda e, i=i: e.activation(rec[i], psb[3][:, :TG], AF.Ln), reads=[("ps", 3)], writes=[("rec", i)])
                    P.op("act", lambda e, i=i: e.activation(rec[i], rec[i], AF.Exp, scale=-1.0), reads=[("rec", i)], writes=[("rec", i)])
                    for dd in range(2):
                        dc = 2 * h + dd
                        b = nbP()
                        P.op("pe", mmgroup(psb[b][:, :TG], [(vm[:, mi, dc * 128:(dc + 1) * 128], ET[h][mi]) for mi in range(2)]),
                             reads=[("ET", h, 0), ("ET", h, 1), ("vm", 0, dc // 4), ("vm", 1, dc // 4)], writes=[("ps", b)])
                        P.op("dve", lambda e, dc=dc, b=b, i=i, o=o: e.tensor_tensor(o[:, dc, :], psb[b][:, :TG], rec[i], ALU.mult),
                             reads=[("ps", b), ("rec", i)], writes=[("oT", tg % 2, dc)])

            def Op(tg):
                t0 = tg * TG
                tok = slice(t0, t0 + TG)
                o = oT[tg % 2]
                for m in range(KC):
                    b = nbQ()
                    P.op("pe", mmgroup(psb[b][:, :TG], [(wo_[:, k, m * 128:(m + 1) * 128], o[:, k, :]) for k in ALLC]),
                         reads=[("oT", tg % 2, k) for k in ALLC] + [("wo_", m // 4)], writes=[("ps", b)])
                    rk = RK("RT", [m], t0, TG)
                    P.op("dve", lambda e, m=m, tok=tok, b=b: e.scalar_tensor_tensor(
                        RT[:, m, tok], RT[:, m, tok], ALPHA, psb[b][:, :TG], ALU.mult, ALU.add),
                         reads=[("ps", b)] + rk, writes=rk)

            for tg in range(NTG + 1):
                streams = []
                if tg < NTG:
                    streams.append(lambda tg=tg: Hd(tg))

                def qo(tg=tg):
                    if tg >= 1:
                        Op(tg - 1)
                    if tg + 1 < NTG:
                        Qp(tg + 1)
                if tg >= 1 or tg + 1 < NTG:
                    streams.append(qo)
                if tg >= 2:
                    streams.append(lambda tg=tg: lnq(tg - 2))
                P.interleave(streams)
            return lambda: lnq(NTG - 1)

        if stop_after >= 3:
            tail = attention(C_LN + 32, C_LN + 40, pre=tail)
        if stop_after >= 4:
            tail = ffn(1, C_LN + 48, C_LN + 56, pre=tail, newphase=True)

        P.new_phase()
        A.off = PH
        ostage = [A.f32(D), A.f32(D)]
        otoks = []
        if "O" in DBG:
            for tt in range(NT):
                otoks.append(P.dma("sp", f"out{tt % 2}", out_d[tt * 128:(tt + 1) * 128, :].rearrange("p (c t) -> p c t", c=8),
                                   RT[:, :, tt * 128:(tt + 1) * 128], reads=RK("RT", ALLC, tt * 128, 128)))
        def store_tiles(tiles):
          for tt in tiles:
              sl = tt % 2
              for hb in range(2):
                  b = nextbank()

                  def tro(e, tt=tt, hb=hb, b=b):
                      ins = None
                      for q in range(4):
                          c = hb * 4 + q
                          ins = e.transpose(psb[b][:, q * 128:(q + 1) * 128], RT[:, c, tt * 128:(tt + 1) * 128], identF)
                      return ins
                  P.op("pe", tro, reads=RK("RT", list(range(hb * 4, hb * 4 + 4)), tt * 128, 128) + ["identF"], writes=[("ps", b)])
                  if hb == 0:
                      P.op("dve", lambda e, sl=sl, b=b: e.tensor_copy(ostage[sl][:, 0:512], psb[b]),
                           reads=[("ps", b)], writes=[("ost", sl, 0)])
                  else:
                      P.op("act", lambda e, sl=sl, b=b: e.activation(ostage[sl][:, 512:1024], psb[b], AF.Identity),
                           reads=[("ps", b)], writes=[("ost", sl, 1)])
              otoks.append(P.dma("sp", f"out{sl}", out_d[tt * 128:(tt + 1) * 128, :], ostage[sl],
                                 reads=[("ost", sl, 0), ("ost", sl, 1)]))

        if stop_after >= 4 and HALF % 256 == 0 and HALF >= 512:
            g4, b4 = C_LN + 48, C_LN + 56
            pieces = list(range(HALF, S, 256))
            lnp = lambda t0: (lambda: layernorm(t0, 256, g4, b4))
            tl = lambda t0, n: range(t0 // 128, (t0 + n) // 128)
            P.interleave([lambda: [lnp(pieces[0])(), lnp(pieces[1])()], lambda: store_tiles(range(NT // 2))])
            stored, lndone = HALF, HALF + 512
            for k in range(2, len(pieces)):
                P.interleave([lnp(pieces[k]), lambda a=stored, b=lndone: store_tiles(range(a // 128, b // 128))])
                stored, lndone = lndone, lndone + 256
            store_tiles(range(stored // 128, NT))
        elif tail is not None:
            P.interleave([tail, lambda: store_tiles(range(NT // 2))], weights=[1, 1])
            store_tiles(range(NT // 2, NT))
        else:
            store_tiles(range(NT // 2))
            store_tiles(range(NT // 2, NT))
        P.final_wait("sp", otoks[-2:])
        P.run(st)
    nc._in_names = _names
    return nc


def _blk_cols(w, nblk, width):
    K, N = w.shape
    kc = K // 128
    a = w.reshape(kc, 128, nblk, width).transpose(2, 1, 0, 3)
    return np.ascontiguousarray(a.reshape(nblk, 128, kc * width))


def prep_weights(inp):
    f = lambda a: np.asarray(a, dtype=np.float32)
    out = {}
    for i, nm in ((1, "w_ffn1"), (2, "w_ffn2")):
        w_in = f(inp[nm + "_in"])[0]
        g = w_in[:, :DFF].reshape(KC, 128, JC // 2, 2, 128)
        u = w_in[:, DFF:].reshape(KC, 128, JC // 2, 2, 128)
        blk = np.concatenate([g, u], axis=3)
        blk = blk.transpose(2, 1, 0, 3, 4).reshape(JC // 2, 128, KC * 512)
        out[f"wf{i}_in"] = np.ascontiguousarray(blk)
        w_out = f(inp[nm + "_out"])[0]
        out[f"wf{i}_out"] = _blk_cols(w_out, 8, 128)
    out["wmi"] = _blk_cols(f(inp["w_mix_in"])[0], 5, 512)
    out["wmo"] = _blk_cols(f(inp["w_mix_out"])[0], 2, 512)
    for nm, k in (("wq", "xa_wq"), ("wk", "xa_wk"), ("wv", "xa_wv"), ("wo", "xa_wo")):
        out[nm] = _blk_cols(f(inp[k])[0], 2, 512)
    pw = f(inp["pool_w"])[0]
    out["poolw"] = np.ascontiguousarray(pw.transpose(1, 0, 2).reshape(128, 512))
    cols = np.zeros((128, NCOLS), np.float32)
    col8 = lambda v: f(v).reshape(-1, 128).T
    for i, nm in enumerate(["ln1_g", "ln1_b", "ln2_g", "ln2_b", "ln3_g", "ln3_b", "ln4_g", "ln4_b"]):
        cols[:, C_LN + 8 * i:C_LN + 8 * i + 8] = col8(inp[nm][0])
    cols[:, C_PSC:C_PSC + 4] = col8(inp["pool_scale"][0])
    cols[:, C_GN] = f(inp["hgrn_gnorm"])[0]
    lb = f(inp["hgrn_lb"])
    cols[:, C_LBA:C_LBA + 4] = col8(lb[0])
    cols[:, C_LBB:C_LBB + 4] = col8(lb[1])
    out["cols"] = cols
    return out


_NC_CACHE = {}


def kernel(**inputs):
    x = np.asarray(inputs["x"], dtype=np.float32)
    mem = np.asarray(inputs["mem"], dtype=np.float32)
    B, S, _ = x.shape
    w = prep_weights(inputs)
    key = (S,)
    if key not in _NC_CACHE:
        _NC_CACHE[key] = build(S=S)
    nc = _NC_CACHE[key]
    in_maps = []
    for b in range(B):
        m = dict(w)
        m["x"] = np.ascontiguousarray(x[b])
        m["mem"] = np.ascontiguousarray(mem[b])
        in_maps.append(m)
    res = run_bass_kernel_spmd(nc, in_maps, core_ids=list(range(B)))
    return np.stack([np.asarray(r["out"], dtype=np.float32) for r in res.results], axis=0)
```

```python
from contextlib import ExitStack
import numpy as np
import concourse.bass as bass
import concourse.mybir as mybir
from concourse.bass_utils import run_bass_kernel_spmd

F32 = mybir.dt.float32
BF16 = mybir.dt.bfloat16
I32 = mybir.dt.int32
AF = mybir.ActivationFunctionType
ALU = mybir.AluOpType

D = 1024
KC = 8
DFF = 2816
JC = 22
NMEM = 256
ALPHA = 2.0 ** 0.25
LN_EPS = 1e-5
RMS_EPS = 1e-6
POOL_W = (2, 4, 8, 16)
NCOLS = 80
C_LN = 0
C_PSC = 64
C_GN = 68
C_LBA = 69
C_LBB = 73


class Prog:
    ENGS = ("pe", "act", "dve", "pool", "sp")

    def __init__(self, nc):
        self.nc = nc
        self.ops = {e: [] for e in self.ENGS}
        self.count = {e: 0 for e in self.ENGS}
        self.sem = {}
        self.dsem = {}
        self.last_w = {}
        self.readers = {}
        self.waited = {e: {} for e in self.ENGS}
        self.guard = []
        self.seen = set()
        self.n_waits = 0
        self.fuse_waits = True

    def _yield(self):
        il = getattr(self, "_il", None)
        if il is None:
            return
        import threading
        i = il["tl"].__dict__.get("idx")
        if i is None:
            return
        il["left"][i] -= 1
        if il["left"][i] > 0:
            return
        il["main"].release()
        il["sems"][i].acquire()

    def interleave(self, fns, weights=None):
        import threading
        fns = [f for f in fns if f is not None]
        if not fns:
            return
        n = len(fns)
        weights = list(weights or [1] * n)
        il = dict(tl=threading.local(), sems=[threading.Semaphore(0) for _ in range(n)],
                  main=threading.Semaphore(0), left=[0] * n, alive=[True] * n, err=[])
        self._il = il

        def runner(i):
            il["tl"].idx = i
            il["sems"][i].acquire()
            try:
                fns[i]()
            except BaseException as ex:
                il["err"].append(ex)
            il["alive"][i] = False
            il["tl"].idx = None
            il["main"].release()

        ths = [threading.Thread(target=runner, args=(i,), daemon=True) for i in range(n)]
        for t in ths:
            t.start()
        while any(il["alive"]):
            for i in range(n):
                if il["alive"][i]:
                    il["left"][i] = weights[i]
                    il["sems"][i].release()
                    il["main"].acquire()
                    if il["err"]:
                        self._il = None
                        raise il["err"][0]
        for t in ths:
            t.join()
        self._il = None

    def _sem_for(self, tok):
        if tok[0] == "eng":
            return self.sem[tok[1]], tok[2]
        return self.dsem[tok[1]][0], tok[2]

    def new_phase(self):
        g = {}
        for d in (self.last_w,):
            for t in d.values():
                n = t[0] + ":" + t[1]
                if n not in g or g[n][2] < t[2]:
                    g[n] = t
        for lst in self.readers.values():
            for t in lst:
                n = t[0] + ":" + t[1]
                if n not in g or g[n][2] < t[2]:
                    g[n] = t
        self.guard = list(g.values())
        self.seen = set()

    def _collect(self, eng, reads, writes):
        deps = []
        for r in reads:
            t = self.last_w.get(r)
            if t is not None:
                deps.append(t)
        for w in writes:
            t = self.last_w.get(w)
            if t is not None:
                deps.append(t)
            deps.extend(self.readers.get(w, ()))
            if w not in self.seen:
                self.seen.add(w)
                deps.extend(self.guard)
        need = {}
        for t in deps:
            if t[0] == "eng" and t[1] == eng and eng == "pe":
                continue
            name = t[0] + ":" + t[1]
            v = t[2]
            if self.waited[eng].get(name, 0) >= v:
                continue
            if name not in need or need[name][2] < v:
                need[name] = t
        for name, t in need.items():
            self.waited[eng][name] = t[2]
        return list(need.values())

    def _commit(self, tok, reads, writes):
        for r in reads:
            self.readers.setdefault(r, []).append(tok)
        for w in writes:
            self.last_w[w] = tok
            self.readers[w] = []

    def op(self, eng, fn, reads=(), writes=()):
        waits = self._collect(eng, reads, writes)
        self.count[eng] += 1
        seq = self.count[eng]
        self.n_waits += len(waits)

        def emit(e, fn=fn, waits=waits, eng=eng):
            ws = [self._sem_for(t) for t in waits]
            fuse = bool(ws) and eng != "pe" and self.fuse_waits
            for sv in (ws[:-1] if fuse else ws):
                e.wait_ge(*sv)
            ins = fn(e)
            if fuse:
                ins._wait_ge(*ws[-1])
            ins.then_inc(self.sem[eng], 1)

        self.ops[eng].append(emit)
        tok = ("eng", eng, seq)
        self._commit(tok, reads, writes)
        self._yield()
        return tok

    def dma(self, eng, key, out, in_, reads=(), writes=(), **kw):
        waits = self._collect(eng, reads, writes)
        if key not in self.dsem:
            self.dsem[key] = [None, 0]
        self.dsem[key][1] += 16
        cnt = self.dsem[key][1]
        self.n_waits += len(waits)

        def emit(e, waits=waits, key=key, out=out, in_=in_, kw=kw):
            for t in waits:
                e.wait_ge(*self._sem_for(t))
            e.dma_start(out=out, in_=in_, **kw).then_inc(self.dsem[key][0], 16)

        self.ops[eng].append(emit)
        tok = ("dma", key, cnt)
        self._commit(tok, reads, writes)
        self._yield()
        return tok

    def final_wait(self, eng, toks):
        def emit(e, toks=list(toks)):
            for t in toks:
                e.wait_ge(*self._sem_for(t))
        self.ops[eng].append(emit)

    def run(self, stack):
        nc = self.nc
        for e in self.ENGS:
            self.sem[e] = stack.enter_context(nc.semaphore("prog_" + e))
        for k in self.dsem:
            self.dsem[k][0] = stack.enter_context(nc.semaphore("dma_" + k))
        block = stack.enter_context(nc.Block())

        @block.tensor
        def _(e):
            for f in self.ops["pe"]:
                f(e)

        @block.scalar
        def _(e):
            for f in self.ops["act"]:
                f(e)

        @block.vector
        def _(e):
            for f in self.ops["dve"]:
                f(e)

        @block.gpsimd
        def _(e):
            for f in self.ops["pool"]:
                f(e)

        @block.sync
        def _(e):
            for f in self.ops["sp"]:
                f(e)


def build(S=2048, TG=512, U=256, stop_after=4):
    assert S % (2 * TG) == 0 and TG % U == 0 and U % 128 == 0
    nc = bass.Bass("TRN2", target_bir_lowering=False)
    NT = S // 128
    HALF = S // 2
    NTGH = HALF // TG

    import os
    _u = "u" in os.environ.get("KDBG", "")
    _names = []

    def din(name, shape):
        if _u and stop_after == 0 and name not in ("x", "cols"):
            return None
        _names.append(name)
        return nc.dram_tensor(name, list(shape), F32, kind="ExternalInput").ap()

    x_d = din("x", [S, D])
    mem_d = din("mem", [NMEM, D])
    wf_in_d = [din("wf1_in", [11, 128, KC * 512]), din("wf2_in", [11, 128, KC * 512])]
    wf_out_d = [din("wf1_out", [8, 128, JC * 128]), din("wf2_out", [8, 128, JC * 128])]
    wmi_d = din("wmi", [5, 128, KC * 512])
    wmo_d = din("wmo", [2, 128, KC * 512])
    wq_d = din("wq", [2, 128, KC * 512])
    wk_d = din("wk", [2, 128, KC * 512])
    wv_d = din("wv", [2, 128, KC * 512])
    wo_d = din("wo", [2, 128, KC * 512])
    poolw_d = din("poolw", [128, 512])
    cols_d = din("cols", [128, NCOLS])
    out_d = nc.dram_tensor("out", [S, D], F32, kind="ExternalOutput").ap()

    st = ExitStack()
    with st:
        AW = 53200
        arena = st.enter_context(nc.sbuf_tensor("arena", [128, AW], F32))
        psb = [st.enter_context(nc.psum_tensor(f"ps{i}", [128, 512], F32))[:] for i in range(8)]
        P = Prog(nc)

        class Alloc:
            def __init__(self, base):
                self.off = base

            def f32(self, *shape):
                n = int(np.prod(shape))
                v = arena[:, self.off:self.off + n]
                self.off += n
                assert self.off <= AW, f"arena overflow {self.off}"
                if len(shape) == 2:
                    return v.rearrange("p (a b) -> p a b", a=shape[0])
                if len(shape) == 3:
                    return v.rearrange("p (a b c) -> p a b c", a=shape[0], b=shape[1])
                return v

            def bf16(self, *shape):
                n = int(np.prod(shape))
                assert n % 2 == 0
                v = arena[:, self.off:self.off + n // 2].bitcast(BF16)
                self.off += n // 2
                assert self.off <= AW, f"arena overflow {self.off}"
                if len(shape) == 2:
                    return v.rearrange("p (a b) -> p a b", a=shape[0])
                if len(shape) == 3:
                    return v.rearrange("p (a b c) -> p a b c", a=shape[0], b=shape[1])
                return v

        A = Alloc(0)
        RT = A.f32(KC, S)
        XT = A.bf16(KC, S)
        identF = A.f32(128)
        onesD = A.bf16(128)
        onesV = A.bf16(128)
        ones1 = A.bf16(128)
        mask2 = A.f32(128)
        cols = A.f32(NCOLS)
        lbc = A.f32(4)
        omlc = A.f32(4)
        rcfix = A.f32(4, 16)
        iot = A.f32(16)
        ioti = arena[:, A.off:A.off + 16].bitcast(I32)
        A.off += 16
        scanmask = A.f32(U)
        mean_off = A.off
        mean = A.f32(TG)
        m2 = A.f32(TG)
        rstd = A.f32(TG)
        zsq_off = A.off
        zsq = A.bf16(KC, TG)
        PH = A.off

        def RK(name, cs, t0, n):
            return [(name, c, tt) for c in cs for tt in range(t0 // 128, (t0 + n + 127) // 128)]

        ALLC = list(range(KC))
        bank_rr = [0]

        def nextbank():
            b = bank_rr[0]
            bank_rr[0] = (b + 1) % 4
            return b

        def mmgroup(out, pairs):
            def fn(e, out=out, pairs=pairs):
                n = len(pairs)
                ins = None
                for i, (l, r) in enumerate(pairs):
                    ins = e.matmul(out, l, r, start=(i == 0), stop=(i == n - 1))
                return ins
            return fn

        def wload(key, dst, src, reskey, rows=None):
            P.dma("pool", key, dst, src, writes=[reskey])

        P.dma("sp", "cols", cols, cols_d, writes=["cols"])
        P.op("pool", lambda e: e.memset(identF, 0.0), writes=["identF"])
        P.op("pool", lambda e: e.affine_select(out=identF, in_=identF, pattern=[[-1, 128]], compare_op=ALU.not_equal,
                                               fill=1.0, base=0, channel_multiplier=1),
             reads=["identF"], writes=["identF"])
        import os
        DBG = os.environ.get("KDBG", "")
        P.op("pool", lambda e: e.memset(onesD, 1.0 / 1024.0), writes=["onesD"])
        P.op("pool", lambda e: e.memset(onesV, 1.0 / 128.0), writes=["onesV"])
        P.op("pool", lambda e: e.memset(ones1, 1.0), writes=["ones1"])
        if "m" not in DBG:
            P.op("pool", lambda e: e.memset(mask2, 1.0), writes=["mask2"])
            P.op("pool", lambda e: e.affine_select(out=mask2, in_=mask2, pattern=[[1, 128]], compare_op=ALU.is_ge,
                                                   fill=0.0, base=0, channel_multiplier=-1),
                 reads=["mask2"], writes=["mask2"])
            P.op("pool", lambda e: e.memset(mask2[0:64, 64:128], 0.0), reads=["mask2"], writes=["mask2"])
        if "s" not in DBG:
            P.op("pool", lambda e: e.memset(scanmask, 1.0), writes=["scanmask"])
            P.op("pool", lambda e: e.memset(scanmask.rearrange("p (a b) -> p a b", b=64)[:, :, 0:1], 0.0),
                 reads=["scanmask"], writes=["scanmask"])
        if "i" not in DBG:
            P.op("pool", lambda e: e.iota(ioti, pattern=[[1, 16]], base=1, channel_multiplier=0), writes=["ioti"])
            P.op("pool", lambda e: e.tensor_copy(iot, ioti), reads=["ioti"], writes=["iot"])
            for g, w in enumerate(POOL_W):
                P.op("dve", lambda e, g=g, w=w: e.tensor_scalar_min(rcfix[:, g, :], iot, float(w)),
                     reads=["iot"], writes=[("rcfix", g)])
                P.op("dve", lambda e, g=g: e.reciprocal(rcfix[:, g, :], rcfix[:, g, :]),
                     reads=[("rcfix", g)], writes=[("rcfix", g)])
        if "l" not in DBG:
            P.op("dve", lambda e: e.tensor_tensor(lbc, cols[:, C_LBA:C_LBA + 4], cols[:, C_LBB:C_LBB + 4], ALU.subtract),
                 reads=["cols"], writes=["lbc"])
            P.op("act", lambda e: e.activation(lbc, lbc, AF.Sigmoid), reads=["lbc"], writes=["lbc"])
            P.op("dve", lambda e: e.tensor_scalar(omlc, lbc, -1.0, 1.0, ALU.mult, ALU.add), reads=["lbc"], writes=["omlc"])

        zsq_flat = zsq.rearrange("p c t -> p (c t)")

        def layernorm(t0, n, gcol, bcol, cast_eng="dve"):
            zsq = zsq_flat[:, :KC * n].rearrange("p (c t) -> p c t", c=KC)
            tok = slice(t0, t0 + n)
            rk = RK("RT", ALLC, t0, n)
            xk = RK("XT", ALLC, t0, n)
            P.op(cast_eng, lambda e: e.tensor_copy(XT[:, :, tok], RT[:, :, tok]), reads=rk, writes=xk)
            P.op("act", lambda e: e.activation(zsq[:, :, :n], RT[:, :, tok], AF.Square), reads=rk, writes=["zsq"])
            P.op("pe", mmgroup(psb[6][:, :n], [(onesD, XT[:, c, tok]) for c in ALLC]),
                 reads=xk + ["onesD"], writes=[("ps", 6)])
            if n <= 256:
                msq_ps, msq_key = psb[6][:, 256:256 + n], ("ps", 6)
            else:
                msq_ps, msq_key = psb[7][:, :n], ("ps", 7)
            P.op("pe", mmgroup(msq_ps, [(onesD, zsq[:, c, :n]) for c in ALLC]),
                 reads=["zsq", "onesD"], writes=[msq_key])
            P.op("act", lambda e: e.activation(mean[:, :n], psb[6][:, :n], AF.Identity),
                 reads=[("ps", 6)], writes=["mean"])
            P.op("dve", lambda e: e.tensor_tensor(m2[:, :n], mean[:, :n], mean[:, :n], ALU.mult),
                 reads=["mean"], writes=["m2"])
            P.op("dve", lambda e: e.tensor_tensor(m2[:, :n], msq_ps, m2[:, :n], ALU.subtract),
                 reads=[msq_key, "m2"], writes=["m2"])
            P.op("act", lambda e: e.activation(m2[:, :n], m2[:, :n], AF.Ln, bias=LN_EPS, scale=1.0),
                 reads=["m2"], writes=["m2"])
            P.op("act", lambda e: e.activation(rstd[:, :n], m2[:, :n], AF.Exp, scale=-0.5),
                 reads=["m2"], writes=["rstd"])
            P.op("dve", lambda e: e.tensor_tensor(RT[:, :, tok], RT[:, :, tok],
                                                  mean[:, :n].unsqueeze(1).to_broadcast([128, KC, n]), ALU.subtract),
                 reads=rk + ["mean"], writes=rk)
            P.op("dve", lambda e: e.tensor_tensor(RT[:, :, tok], RT[:, :, tok],
                                                  rstd[:, :n].unsqueeze(1).to_broadcast([128, KC, n]), ALU.mult),
                 reads=rk + ["rstd"], writes=rk)
            for c in ALLC:
                P.op("act", lambda e, c=c: e.activation(RT[:, c, tok], RT[:, c, tok], AF.Identity,
                                                        scale=cols[:, gcol + c:gcol + c + 1],
                                                        bias=cols[:, bcol + c:bcol + c + 1]),
                     reads=RK("RT", [c], t0, n) + ["cols"], writes=RK("RT", [c], t0, n))
            P.op(cast_eng, lambda e: e.tensor_copy(XT[:, :, tok], RT[:, :, tok]), reads=rk, writes=xk)

        A.off = PH
        xstage = [A.f32(D), A.f32(D)]

        def load_x(tiles):
          for tt in tiles:
              sl = tt % 2
              P.dma("sp", f"xs{sl}", xstage[sl], x_d[tt * 128:(tt + 1) * 128, :], writes=[("xs", sl)])
              for hb in range(2):
                  b = 6 + hb

                  def trf(e, sl=sl, hb=hb, b=b):
                      ins = None
                      for q in range(4):
                          c = hb * 4 + q
                          ins = e.transpose(psb[b][:, q * 128:(q + 1) * 128], xstage[sl][:, c * 128:(c + 1) * 128], identF)
                      return ins
                  P.op("pe", trf, reads=[("xs", sl), "identF"], writes=[("ps", b)])
                  cs = list(range(hb * 4, hb * 4 + 4))
                  pv = psb[b].rearrange("p (a b) -> p a b", a=4)
                  P.op("dve", lambda e, hb=hb, tt=tt, pv=pv: e.tensor_copy(RT[:, hb * 4:hb * 4 + 4, tt * 128:(tt + 1) * 128], pv),
                       reads=[("ps", b)], writes=RK("RT", cs, tt * 128, 128))
                  if "A" not in DBG:
                      P.op("act", lambda e, hb=hb, tt=tt: e.activation(XT[:, hb * 4:hb * 4 + 4, tt * 128:(tt + 1) * 128],
                                                                       RT[:, hb * 4:hb * 4 + 4, tt * 128:(tt + 1) * 128], AF.Identity),
                           reads=RK("RT", cs, tt * 128, 128), writes=RK("XT", cs, tt * 128, 128))


        load_x(range(NT // 2))

        def ffn(idx, gcol, bcol, pre=None, newphase=False):
            if newphase:
                P.new_phase()
            A.off = PH + 2 * D
            H = A.bf16(JC, HALF)
            wi = [A.bf16(KC, 512), A.bf16(KC, 512)]
            wo = [A.bf16(JC, 128), A.bf16(JC, 128)]
            sg = [A.f32(TG), A.f32(TG)]
            w_in_d, w_out_d = wf_in_d[idx], wf_out_d[idx]
            cnt = [0, 0, 0]

            def Aph(half):
                t0h = half * HALF
                for jj in range(JC // 2):
                    sl = cnt[0] % 2
                    cnt[0] += 1
                    wload(f"wi{sl}", wi[sl].rearrange("p k n -> p (k n)").rearrange("p (r e) -> p r e", e=2048),
                          w_in_d[jj].rearrange("p (r e) -> p r e", e=2048), ("wi", sl))
                    for jl in range(2):
                        j = 2 * jj + jl
                        for tg in range(NTGH):
                            t0 = t0h + tg * TG
                            tok = slice(t0, t0 + TG)
                            i = cnt[1] % 2
                            cnt[1] += 1
                            gb, ub = i, 2 + i
                            xk = RK("XT", ALLC, t0, TG)
                            P.op("pe", mmgroup(psb[gb][:, :TG], [(wi[sl][:, k, jl * 128:(jl + 1) * 128], XT[:, k, tok]) for k in ALLC]),
                                 reads=[("wi", sl)] + xk, writes=[("ps", gb)])
                            P.op("pe", mmgroup(psb[ub][:, :TG], [(wi[sl][:, k, 256 + jl * 128:256 + (jl + 1) * 128], XT[:, k, tok]) for k in ALLC]),
                                 reads=[("wi", sl)] + xk, writes=[("ps", ub)])
                            P.op("act", lambda e, i=i, gb=gb: e.activation(sg[i], psb[gb][:, :TG], AF.Silu),
                                 reads=[("ps", gb)], writes=[("sg", i)])
                            P.op("dve", lambda e, i=i, ub=ub, j=j, tg=tg: e.scalar_tensor_tensor(
                                H[:, j, tg * TG:(tg + 1) * TG], sg[i], 0.5, psb[ub][:, :TG], ALU.mult, ALU.mult),
                                 reads=[("sg", i), ("ps", ub)], writes=[("H", j, tg)])

            def Bph(half):
                t0h = half * HALF
                for m in range(KC):
                    sl = cnt[2] % 2
                    cnt[2] += 1
                    wload(f"wo{sl}", wo[sl].rearrange("p j n -> p (j n)").rearrange("p (r e) -> p r e", e=1408),
                          w_out_d[m].rearrange("p (r e) -> p r e", e=1408), ("wo", sl))
                    for tg in range(NTGH):
                        t0 = t0h + tg * TG
                        tok = slice(t0, t0 + TG)
                        ob = 4 + (m * NTGH + tg) % 2
                        P.op("pe", mmgroup(psb[ob][:, :TG], [(wo[sl][:, j, :], H[:, j, tg * TG:(tg + 1) * TG]) for j in range(JC)]),
                             reads=[("wo", sl)] + [("H", j, tg) for j in range(JC)], writes=[("ps", ob)])
                        rk = RK("RT", [m], t0, TG)
                        P.op("dve", lambda e, m=m, tok=tok, ob=ob: e.scalar_tensor_tensor(
                            RT[:, m, tok], RT[:, m, tok], ALPHA, psb[ob][:, :TG], ALU.mult, ALU.add),
                             reads=[("ps", ob)] + rk, writes=rk)

            def LNs(half):
                for tg in range(NTGH):
                    layernorm(half * HALF + tg * TG, TG, gcol, bcol)

            if pre is not None:
                P.interleave([pre, lambda: Aph(0)], weights=[1, 6])
            else:
                Aph(0)
            Bph(0)
            P.interleave([lambda: LNs(0), lambda: Aph(1)], weights=[1, 6])
            Bph(1)
            return lambda: LNs(1)

        tail = None
        if stop_after >= 1:
            tail = ffn(0, C_LN + 0, C_LN + 8, pre=lambda: load_x(range(NT // 2, NT)))
        else:
            load_x(range(NT // 2, NT))

        def mixer(gcol, bcol, pre=None):
            P.new_phase()
            A.off = PH
            NTU = U // 128
            NCH = U // 64
            NUU = S // U
            wmi = A.bf16(KC, 2560)
            wmo = A.bf16(KC, D)
            poolw = A.bf16(512)
            vtok = [A.bf16(NTU, 512), A.bf16(NTU, 512)]
            MT = A.bf16(KC, U)
            fl_off = A.off
            F_, L_, C_, E_ = A.f32(U), A.f32(U), A.f32(U), A.f32(U)
            PAIR = (TG >= 2 * U)
            if PAIR:
                F2 = arena[:, fl_off:fl_off + 2 * U].rearrange("p (a b) -> p a b", a=2)
                L2 = arena[:, fl_off + 2 * U:fl_off + 4 * U].rearrange("p (a b) -> p a b", a=2)
                zu = zsq_off + KC * U // 2
                C2_d = arena[:, zu:zu + 2 * U].rearrange("p (a b) -> p a b", a=2)
                E2_d = arena[:, zu + 2 * U:zu + 4 * U].rearrange("p (a b) -> p a b", a=2)
                sgt2_d = arena[:, mean_off:mean_off + 2 * TG].rearrange("p (a b) -> p a b", a=2)[:, :, U:2 * U]
            qd4 = [A.bf16(4, U)]
            qd1_off = A.off
            qd4.append(A.bf16(4, U))
            kd4 = [A.bf16(4, U)]
            kd1_off = A.off
            kd4.append(A.bf16(4, U))
            kltok4 = [A.bf16(4, U)]
            kl1_off = A.off
            kltok4.append(A.bf16(4, U))
            EL = [A.f32(4, NCH), A.f32(4, NCH)]
            sgt_default = mean[:, U:2 * U] if TG >= 2 * U else A.f32(U)
            Sf = A.f32(4, 128)
            Sb = A.bf16(4, 128)
            sT4 = A.bf16(4, 128)
            osq2 = A.bf16(2, U)
            rs2 = A.f32(2, U)
            sgg2 = A.f32(2, U)
            VP = [A.f32(16 + U), A.f32(16 + U), A.f32(16 + U)]
            halo = A.f32(4, 16)
            pooled = A.bf16(U)
            pfix = A.f32(16)
            OB = (3, 5)
            rrA = [0]
            rrC = [0]

            def nbA():
                return 0

            def nbC():
                rrC[0] ^= 1
                return (2, 7)[rrC[0]]

            def prologue():
                for b in (1, 0, 2, 3, 4):
                    wload(f"wmi{b}", wmi[:, :, b * 512:(b + 1) * 512], wmi_d[b].rearrange("p (k n) -> p k n", k=KC), ("wmi", b))
                for b in range(2):
                    wload(f"wmo{b}", wmo[:, :, b * 512:(b + 1) * 512], wmo_d[b].rearrange("p (k n) -> p k n", k=KC), ("wmo", b))
                wload("poolw", poolw, poolw_d, "poolw")
                P.op("dve", lambda e: e.memset(Sf, 0.0), writes=["Sf"])
                P.op("dve", lambda e: e.memset(Sb, 0.0), writes=["Sb"])
                P.op("dve", lambda e: e.memset(halo, 0.0), writes=[("halo", h) for h in range(4)])
                P.op("dve", lambda e: e.memset(VP[1], 0.0), writes=[("VP", 1)])
                P.op("dve", lambda e: e.memset(VP[2], 0.0), writes=[("VP", 2)])
                V(0)
                if PAIR:
                    f3 = lambda off: arena[:, off:off + 2 * U].rearrange("p (a b) -> p a b", a=2)
                    alt0 = (f3(kd1_off), f3(qd1_off), f3(kl1_off))
                    A1p(0, 0, alt=alt0)
                    A1p(0, 1, alt=alt0)
                else:
                    for h in range(4):
                        A1(0, h, sgt=VP[1][:, :U], sgk=("VP", 1))

            def V(u):
                vs = vtok[u % 2]
                for ti in range(NTU):
                    b = nbA()
                    ts0 = u * U + ti * 128
                    tsl = slice(ts0, ts0 + 128)
                    P.op("pe", mmgroup(psb[b], [(XT[:, k, tsl], wmi[:, k, 1024:1536]) for k in ALLC]),
                         reads=RK("XT", ALLC, ts0, 128) + [("wmi", 2)], writes=[("ps", b)])
                    P.op("act", lambda e, ti=ti, b=b, vs=vs: e.activation(vs[:, ti, :], psb[b], AF.Identity),
                         reads=[("ps", b)], writes=[("vtok", u % 2, ti)])

            def A1(u, h, sgt=None, sgk="sgt"):
                sgt = sgt_default if sgt is None else sgt
                s_ = u % 2
                t0 = u * U
                tok = slice(t0, t0 + U)
                xk = RK("XT", ALLC, t0, U)
                b = nbA()
                P.op("pe", mmgroup(psb[b][:, :U], [(wmi[:, k, 512 + h * 128:512 + (h + 1) * 128], XT[:, k, tok]) for k in ALLC]),
                     reads=xk + [("wmi", 1)], writes=[("ps", b)])
                P.op("act", lambda e, b=b: e.activation(F_, psb[b][:, :U], AF.Exp, scale=-1.0), reads=[("ps", b)], writes=["F_"])
                P.op("act", lambda e: e.activation(F_, F_, AF.Ln, bias=1.0, scale=1.0), reads=["F_"], writes=["F_"])
                P.op("act", lambda e: e.activation(F_, F_, AF.Exp, scale=-1.0), reads=["F_"], writes=["F_"])
                P.op("dve", lambda e, h=h: e.tensor_scalar(F_, F_, omlc[:, h:h + 1], lbc[:, h:h + 1], ALU.mult, ALU.add),
                     reads=["F_", "lbc", "omlc"], writes=["F_"])
                P.op("act", lambda e: e.activation(L_, F_, AF.Ln), reads=["F_"], writes=["L_"])
                P.op("dve", lambda e: e.tensor_scalar(F_, F_, -1.0, 1.0, ALU.mult, ALU.add), reads=["F_"], writes=["F_"])
                P.op("dve", lambda e: e.tensor_tensor_scan(C_, scanmask, L_, 0.0, ALU.mult, ALU.add),
                     reads=["L_", "scanmask"], writes=["C_"])
                P.op("act", lambda e: e.activation(E_, C_, AF.Exp), reads=["C_"], writes=["E_"])
                P.op("act", lambda e: e.activation(L_, C_, AF.Exp, scale=-1.0), reads=["C_"], writes=["L_"])
                P.op("dve", lambda e: e.tensor_tensor(L_, F_, L_, ALU.mult), reads=["F_", "L_"], writes=["L_"])
                P.op("pool", lambda e, s_=s_, h=h: e.tensor_copy(kd4[s_][:, h, :], L_),
                     reads=["L_"], writes=[("kd", s_, h)])
                P.op("pool", lambda e, s_=s_, h=h: e.tensor_copy(EL[s_][:, h, :], E_.rearrange("p (a b) -> p a b", b=64)[:, :, 63]),
                     reads=["E_"], writes=[("EL", s_)])
                P.op("dve", lambda e: e.tensor_tensor(
                    C_.rearrange("p (a b) -> p a b", b=64), L_.rearrange("p (a b) -> p a b", b=64),
                    E_.rearrange("p (a b) -> p a b", b=64)[:, :, 63:64].to_broadcast([128, NCH, 64]), ALU.mult),
                     reads=["L_", "E_"], writes=["C_"])
                b = nbA()
                P.op("pe", mmgroup(psb[b][:, :U], [(wmi[:, k, h * 128:(h + 1) * 128], XT[:, k, tok]) for k in ALLC]),
                     reads=xk + [("wmi", 0)], writes=[("ps", b)])
                P.op("act", lambda e, b=b: e.activation(sgt, psb[b][:, :U], AF.Exp, scale=-1.0), reads=[("ps", b)], writes=[sgk])
                P.op("act", lambda e: e.activation(sgt, sgt, AF.Ln, bias=1.0, scale=1.0), reads=[sgk], writes=[sgk])
                P.op("act", lambda e: e.activation(sgt, sgt, AF.Exp, scale=-1.0), reads=[sgk], writes=[sgk])
                P.op("dve", lambda e: e.tensor_tensor(sgt, sgt, E_, ALU.mult), reads=[sgk, "E_"], writes=[sgk])
                P.op("dve", lambda e, s_=s_, h=h, b=b: e.tensor_tensor(qd4[s_][:, h, :], psb[b][:, :U], sgt, ALU.mult),
                     reads=[sgk, ("ps", b)], writes=[("qd", s_, h)])
                b = nbA()

                def trk(e, b=b):
                    ins = None
                    for ti in range(NTU):
                        ins = e.transpose(psb[b][:, ti * 128:(ti + 1) * 128], C_[:, ti * 128:(ti + 1) * 128], identF)
                    return ins
                P.op("pe", trk, reads=["C_", "identF"], writes=[("ps", b)])
                P.op("act", lambda e, b=b, s_=s_, h=h: e.activation(kltok4[s_][:, h, :], psb[b][:, :U], AF.Identity),
                     reads=[("ps", b)], writes=[("kltok", s_, h)])

            def A1p(u, hp, alt=None):
                s_ = u % 2
                t0 = u * U
                tok = slice(t0, t0 + U)
                xk = RK("XT", ALLC, t0, U)
                hh = (2 * hp, 2 * hp + 1)
                if alt is None:
                    C2, E2, sgt2, kC, kE, kS = C2_d, E2_d, sgt2_d, "C2", "E2", "sgt2"
                else:
                    C2, E2, sgt2 = alt
                    kC, kE, kS = "C2alt", "E2alt", "sgt2alt"
                b = nbA()
                for j, h in enumerate(hh):
                    P.op("pe", mmgroup(psb[b][:, j * U:(j + 1) * U], [(wmi[:, k, 512 + h * 128:512 + (h + 1) * 128], XT[:, k, tok]) for k in ALLC]),
                         reads=xk + [("wmi", 1)], writes=[("ps", b)])
                pf = psb[b][:, :2 * U].rearrange("p (a b) -> p a b", a=2)
                P.op("act", lambda e, pf=pf: e.activation(F2, pf, AF.Exp, scale=-1.0), reads=[("ps", b)], writes=["F2"])
                P.op("act", lambda e: e.activation(F2, F2, AF.Ln, bias=1.0, scale=1.0), reads=["F2"], writes=["F2"])
                P.op("act", lambda e: e.activation(F2, F2, AF.Exp, scale=-1.0), reads=["F2"], writes=["F2"])
                for j, h in enumerate(hh):
                    P.op("act", lambda e, j=j, h=h: e.activation(F2[:, j, :], F2[:, j, :], AF.Identity,
                                                                 scale=omlc[:, h:h + 1], bias=lbc[:, h:h + 1]),
                         reads=["F2", "lbc", "omlc"], writes=["F2"])
                P.op("act", lambda e: e.activation(L2, F2, AF.Ln), reads=["F2"], writes=["L2"])
                P.op("act", lambda e: e.activation(F2, F2, AF.Identity, scale=-1.0, bias=1.0), reads=["F2"], writes=["F2"])
                for j in range(2):
                    P.op("dve", lambda e, j=j: e.tensor_tensor_scan(C2[:, j, :], scanmask, L2[:, j, :], 0.0, ALU.mult, ALU.add),
                         reads=["L2", "scanmask"], writes=[kC])
                P.op("act", lambda e: e.activation(E2, C2, AF.Exp), reads=[kC], writes=[kE])
                P.op("act", lambda e: e.activation(L2, C2, AF.Exp, scale=-1.0), reads=[kC], writes=["L2"])
                P.op("dve", lambda e: e.tensor_tensor(L2, F2, L2, ALU.mult), reads=["F2", "L2"], writes=["L2"])
                P.op("pool", lambda e, s_=s_, hp=hp: e.tensor_copy(kd4[s_][:, 2 * hp:2 * hp + 2, :], L2),
                     reads=["L2"], writes=[("kd", s_, 2 * hp), ("kd", s_, 2 * hp + 1)])
                E4 = E2.rearrange("p a (c b) -> p a c b", b=64)
                P.op("pool", lambda e, s_=s_, hp=hp: e.tensor_copy(EL[s_][:, 2 * hp:2 * hp + 2, :], E4[:, :, :, 63]),
                     reads=[kE], writes=[("EL", s_)])
                P.op("dve", lambda e: e.tensor_tensor(
                    C2.rearrange("p a (c b) -> p (a c) b", b=64), L2.rearrange("p a (c b) -> p (a c) b", b=64),
                    E2.rearrange("p a (c b) -> p (a c) b", b=64)[:, :, 63:64].to_broadcast([128, 2 * NCH, 64]), ALU.mult),
                     reads=["L2", kE], writes=[kC])
                b = nbA()
                for j, h in enumerate(hh):
                    P.op("pe", mmgroup(psb[b][:, j * U:(j + 1) * U], [(wmi[:, k, h * 128:(h + 1) * 128], XT[:, k, tok]) for k in ALLC]),
                         reads=xk + [("wmi", 0)], writes=[("ps", b)])
                pq = psb[b][:, :2 * U].rearrange("p (a b) -> p a b", a=2)
                P.op("act", lambda e, pq=pq: e.activation(sgt2, pq, AF.Exp, scale=-1.0), reads=[("ps", b)], writes=[kS])
                P.op("act", lambda e: e.activation(sgt2, sgt2, AF.Ln, bias=1.0, scale=1.0), reads=[kS], writes=[kS])
                P.op("act", lambda e: e.activation(sgt2, sgt2, AF.Exp, scale=-1.0), reads=[kS], writes=[kS])
                P.op("dve", lambda e: e.tensor_tensor(sgt2, sgt2, E2, ALU.mult), reads=[kS, kE], writes=[kS])
                P.op("dve", lambda e, s_=s_, hp=hp, pq=pq: e.tensor_tensor(qd4[s_][:, 2 * hp:2 * hp + 2, :], pq, sgt2, ALU.mult),
                     reads=[kS, ("ps", b)], writes=[("qd", s_, 2 * hp), ("qd", s_, 2 * hp + 1)])
                b = nbA()

                def trk(e, b=b):
                    ins = None
                    for j in range(2):
                        for ti in range(NTU):
                            ins = e.transpose(psb[b][:, j * U + ti * 128:j * U + (ti + 1) * 128], C2[:, j, ti * 128:(ti + 1) * 128], identF)
                    return ins
                P.op("pe", trk, reads=[kC, "identF"], writes=[("ps", b)])
                P.op("act", lambda e, b=b, s_=s_, hp=hp: e.activation(
                    kltok4[s_][:, 2 * hp:2 * hp + 2, :], psb[b][:, :2 * U].rearrange("p (a b) -> p a b", a=2), AF.Identity),
                     reads=[("ps", b)], writes=[("kltok", s_, 2 * hp), ("kltok", s_, 2 * hp + 1)])

            def A1any(u, hlist):
                hlist = list(hlist)
                if PAIR and len(hlist) % 2 == 0:
                    for hp in sorted(set(h // 2 for h in hlist)):
                        A1p(u, hp)
                else:
                    for h in hlist:
                        A1(u, h)

            H4 = list(range(4))

            def Bchunk(u, c):
                s_ = u % 2
                ti, cc = c // 2, c % 2
                c0 = ti * 128
                q0 = c * 64
                r0 = cc * 64
                if cc == 0:
                    def sc(e):
                        ins = None
                        for h in H4:
                            ins = e.matmul(psb[4][:, h * 128:(h + 1) * 128], kd4[s_][:, h, c0:c0 + 128], qd4[s_][:, h, c0:c0 + 128],
                                           start=True, stop=True)
                        return ins
                    P.op("pe", sc, reads=[("kd", s_, h) for h in H4] + [("qd", s_, h) for h in H4], writes=[("ps", 4)])
                    P.op("dve", lambda e: e.tensor_tensor(sT4, psb[4].rearrange("p (a b) -> p a b", a=4),
                                                          mask2.unsqueeze(1).to_broadcast([128, 4, 128]), ALU.mult),
                         reads=[("ps", 4), "mask2"], writes=["sT4"])

                def pm(e):
                    ins = None
                    for h in H4:
                        ins = e.matmul(psb[4][:, h * 128:(h + 1) * 128], kltok4[s_][r0:r0 + 64, h, c0:c0 + 128],
                                       vtok[s_][r0:r0 + 64, ti, h * 128:(h + 1) * 128], start=True, stop=True)
                    return ins
                P.op("pe", pm, reads=[("kltok", s_, h) for h in H4] + [("vtok", s_, ti)], writes=[("ps", 4)])

                def io(e):
                    ins = None
                    for h in H4:
                        ob = OB[h // 2]
                        col = (h % 2) * U + q0
                        e.matmul(psb[ob][:, col:col + 64], vtok[s_][:, ti, h * 128:(h + 1) * 128], sT4[:, h, r0:r0 + 64],
                                 start=True, stop=False)
                        ins = e.matmul(psb[ob][:, col:col + 64], Sb[:, h, :], qd4[s_][:, h, q0:q0 + 64], start=False, stop=True)
                    return ins
                P.op("pe", io, reads=["sT4", ("vtok", s_, ti), "Sb"] + [("qd", s_, h) for h in H4],
                     writes=[("ps", OB[0]), ("ps", OB[1])])
                P.op("dve", lambda e: e.tensor_tensor(Sf, Sf, EL[s_][:, :, c:c + 1].to_broadcast([128, 4, 128]), ALU.mult),
                     reads=["Sf", ("EL", s_)], writes=["Sf"])
                P.op("dve", lambda e: e.tensor_tensor(Sf, Sf, psb[4].rearrange("p (a b) -> p a b", a=4), ALU.add),
                     reads=["Sf", ("ps", 4)], writes=["Sf"])
                P.op("act", lambda e: e.activation(Sb, Sf, AF.Identity), reads=["Sf"], writes=["Sb"])

            def Cnorm(u, hp):
                t0 = u * U
                tok = slice(t0, t0 + U)
                xk = RK("XT", ALLC, t0, U)
                ob = OB[hp]
                pv = psb[ob][:, :2 * U].rearrange("p (a b) -> p a b", a=2)
                P.op("act", lambda e: e.activation(osq2, pv, AF.Square), reads=[("ps", ob)], writes=["osq2"])
                P.op("pe", mmgroup(psb[4][:, :2 * U], [(onesV, osq2.rearrange("p a b -> p (a b)"))]),
                     reads=["osq2", "onesV"], writes=[("ps", 4)])
                P.op("act", lambda e: e.activation(rs2, psb[4][:, :2 * U].rearrange("p (a b) -> p a b", a=2), AF.Ln,
                                                   bias=RMS_EPS, scale=1.0),
                     reads=[("ps", 4)], writes=["rs2"])
                P.op("act", lambda e: e.activation(rs2, rs2, AF.Exp, scale=-0.5), reads=["rs2"], writes=["rs2"])
                P.op("dve", lambda e: e.tensor_tensor(rs2, pv, rs2, ALU.mult), reads=[("ps", ob), "rs2"], writes=["rs2"])
                b = 1
                for j in range(2):
                    h = 2 * hp + j
                    P.op("pe", mmgroup(psb[b][:, j * U:(j + 1) * U],
                                       [(wmi[:, k, 1536 + h * 128:1536 + (h + 1) * 128], XT[:, k, tok]) for k in ALLC]),
                         reads=xk + [("wmi", 3)], writes=[("ps", b)])
                gv = psb[b][:, :2 * U].rearrange("p (a b) -> p a b", a=2)
                P.op("act", lambda e, gv=gv: e.activation(sgg2, gv, AF.Exp, scale=-1.0), reads=[("ps", b)], writes=["sgg2"])
                P.op("act", lambda e: e.activation(sgg2, sgg2, AF.Ln, bias=1.0, scale=1.0), reads=["sgg2"], writes=["sgg2"])
                P.op("act", lambda e: e.activation(sgg2, sgg2, AF.Exp, scale=-1.0), reads=["sgg2"], writes=["sgg2"])
                P.op("dve", lambda e: e.scalar_tensor_tensor(rs2, rs2, cols[:, C_GN:C_GN + 1], sgg2, ALU.mult, ALU.mult),
                     reads=["rs2", "sgg2", "cols"], writes=["rs2"])
                P.op("dve", lambda e, hp=hp, gv=gv: e.tensor_tensor(MT[:, 2 * hp:2 * hp + 2, :], gv, rs2, ALU.mult),
                     reads=["rs2", ("ps", b)], writes=[("MT", 2 * hp), ("MT", 2 * hp + 1)])

            def Cpool(u, h):
                t0 = u * U
                tok = slice(t0, t0 + U)
                xk = RK("XT", ALLC, t0, U)
                w = POOL_W[h]
                b = nbC()
                P.op("pe", mmgroup(psb[b][:, :U], [(wmi[:, k, 2048 + h * 128:2048 + (h + 1) * 128], XT[:, k, tok]) for k in ALLC]),
                     reads=xk + [("wmi", 4)], writes=[("ps", b)])
                P.op("act", lambda e, b=b: e.activation(VP[0][:, 16:16 + U], psb[b][:, :U], AF.Identity),
                     reads=[("ps", b)], writes=[("VP", 0)])
                P.op("act", lambda e, h=h: e.activation(VP[0][:, 0:16], halo[:, h, :], AF.Identity), reads=[("halo", h)], writes=[("VP", 0)])
                P.op("act", lambda e, h=h: e.activation(halo[:, h, :], VP[0][:, U:U + 16], AF.Identity), reads=[("VP", 0)], writes=[("halo", h)])
                src = 0
                step = 1
                while step < w:
                    dst = 1 if src != 1 else 2
                    P.op("dve", lambda e, src=src, dst=dst, step=step: e.tensor_tensor(
                        VP[dst][:, step:16 + U], VP[src][:, step:16 + U], VP[src][:, 0:16 + U - step], ALU.add),
                         reads=[("VP", src)], writes=[("VP", dst)])
                    src = dst
                    step *= 2
                P.op("dve", lambda e, src=src, w=w: e.scalar_tensor_tensor(
                    pooled, VP[src][:, 16:16 + U], 1.0 / w, VP[0][:, 16:16 + U], ALU.mult, ALU.subtract),
                     reads=[("VP", src), ("VP", 0)], writes=["pooled"])
                if u == 0:
                    P.op("dve", lambda e, src=src, h=h: e.tensor_tensor(pfix, VP[src][:, 16:32], rcfix[:, h, :], ALU.mult),
                         reads=[("VP", src), ("rcfix", h)], writes=["pfix"])
                    P.op("dve", lambda e: e.tensor_tensor(pooled[:, 0:16], pfix, VP[0][:, 16:32], ALU.subtract),
                         reads=["pfix", ("VP", 0), "pooled"], writes=["pooled"])
                b = nbC()
                P.op("pe", mmgroup(psb[b][:, :U], [(poolw[:, h * 128:(h + 1) * 128], pooled)]),
                     reads=["poolw", "pooled"], writes=[("ps", b)])
                P.op("act", lambda e, b=b, h=h: e.activation(MT[:, 4 + h, :], psb[b][:, :U], AF.Identity,
                                                             scale=cols[:, C_PSC + h:C_PSC + h + 1]),
                     reads=[("ps", b), "cols"], writes=[("MT", 4 + h)])

            def Dout(u):
                t0 = u * U
                tok = slice(t0, t0 + U)
                for m in range(KC):
                    b = nbC()
                    P.op("pe", mmgroup(psb[b][:, :U], [(wmo[:, k, m * 128:(m + 1) * 128], MT[:, k, :]) for k in ALLC]),
                         reads=[("MT", k) for k in ALLC] + [("wmo", m // 4)], writes=[("ps", b)])
                    rk = RK("RT", [m], t0, U)
                    P.op("dve", lambda e, m=m, tok=tok, b=b: e.scalar_tensor_tensor(
                        RT[:, m, tok], RT[:, m, tok], ALPHA, psb[b][:, :U], ALU.mult, ALU.add),
                         reads=[("ps", b)] + rk, writes=rk)

            P.interleave([pre, prologue])
            P.op("dve", lambda e: e.memset(pfix, 0.0),
                 writes=["mean", "m2", "rstd", "zsq", "sgt", "sgt2", "C2", "E2", "F2", "L2", "F_", "L_", "C_", "E_", "pfix", ("VP", 1),
                         "C2alt", "E2alt", "sgt2alt"] + [(nm, 1, h) for nm in ("qd", "kd", "kltok") for h in range(4)])
            for i in range(NUU + 1):
                streams = []

                def s1(i=i):
                    if 1 <= i:
                        Cnorm(i - 1, 0)
                        Cnorm(i - 1, 1)
                    if i < NUU:
                        for c in range(NCH):
                            Bchunk(i, c)
                    if i >= 2:
                        layernorm((i - 2) * U, U, gcol, bcol)
                streams.append(s1)

                def s2(i=i):
                    if i + 1 < NUU:
                        V(i + 1)
                        A1any(i + 1, H4)
                if i + 1 < NUU:
                    streams.append(s2)

                def s3(i=i):
                    for h in H4:
                        Cpool(i - 1, h)
                    Dout(i - 1)
                if i >= 1:
                    streams.append(s3)
                P.interleave(streams)
            return lambda: layernorm((NUU - 1) * U, U, gcol, bcol, cast_eng="dve")

        if stop_after >= 2:
            tail = mixer(C_LN + 16, C_LN + 24, pre=tail)

        def attention(gcol, bcol, pre=None):
            P.new_phase()
            A.off = PH
            wq = A.bf16(KC, D)
            wo_ = A.bf16(KC, D)
            kT = A.bf16(KC, NMEM)
            vm = A.bf16(2, D)
            qT0 = A.bf16(KC, TG)
            R0 = A.off
            memT = A.bf16(KC, NMEM)
            mstage = A.f32(2, D)
            wk = A.bf16(KC, D)
            wv = A.bf16(KC, D)
            def prep():
                for b in range(2):
                    wload(f"wq{b}", wq[:, :, b * 512:(b + 1) * 512], wq_d[b].rearrange("p (k n) -> p k n", k=KC), ("wq", b))
                for b in range(2):
                    wload(f"wk{b}", wk[:, :, b * 512:(b + 1) * 512], wk_d[b].rearrange("p (k n) -> p k n", k=KC), ("wk", b))
                for b in range(2):
                    wload(f"wv{b}", wv[:, :, b * 512:(b + 1) * 512], wv_d[b].rearrange("p (k n) -> p k n", k=KC), ("wv", b))
                for b in range(2):
                    wload(f"wox{b}", wo_[:, :, b * 512:(b + 1) * 512], wo_d[b].rearrange("p (k n) -> p k n", k=KC), ("wo_", b))
                for m in range(KC):
                    b = nextbank()
                    P.op("pe", mmgroup(psb[b][:, :TG], [(wq[:, k, m * 128:(m + 1) * 128], XT[:, k, 0:TG]) for k in ALLC]),
                         reads=RK("XT", ALLC, 0, TG) + [("wq", m // 4)], writes=[("ps", b)])
                    P.op("act", lambda e, m=m, b=b: e.activation(qT0[:, m, :], psb[b][:, :TG], AF.Identity, scale=1.0 / 16.0),
                         reads=[("ps", b)], writes=[("qT", 0, m)])
                for ti in range(2):
                    P.dma("sp", f"mst{ti}", mstage[:, ti, :], mem_d[ti * 128:(ti + 1) * 128, :], writes=[("mst", ti)])
                    for hb in range(2):
                        b = nextbank()

                        def trm(e, ti=ti, hb=hb, b=b):
                            ins = None
                            for q in range(4):
                                c = hb * 4 + q
                                ins = e.transpose(psb[b][:, q * 128:(q + 1) * 128], mstage[:, ti, c * 128:(c + 1) * 128], identF)
                            return ins
                        P.op("pe", trm, reads=[("mst", ti), "identF"], writes=[("ps", b)])
                        P.op("act", lambda e, ti=ti, hb=hb, b=b: e.activation(
                            memT[:, hb * 4:hb * 4 + 4, ti * 128:(ti + 1) * 128], psb[b].rearrange("p (a b) -> p a b", a=4), AF.Identity),
                             reads=[("ps", b)], writes=[("memT", ti, hb)])
                memk = [("memT", ti, hb) for ti in range(2) for hb in range(2)]
                for dc in range(KC):
                    b = nextbank()
                    P.op("pe", mmgroup(psb[b][:, :NMEM], [(wk[:, k, dc * 128:(dc + 1) * 128], memT[:, k, :]) for k in ALLC]),
                         reads=memk + [("wk", dc // 4)], writes=[("ps", b)])
                    P.op("act", lambda e, dc=dc, b=b: e.activation(kT[:, dc, :], psb[b][:, :NMEM], AF.Identity),
                         reads=[("ps", b)], writes=[("kT", dc)])
                for mi in range(2):
                    for nh in range(2):
                        b = nextbank()
                        P.op("pe", mmgroup(psb[b], [(memT[:, k, mi * 128:(mi + 1) * 128], wv[:, k, nh * 512:(nh + 1) * 512]) for k in ALLC]),
                             reads=memk + [("wv", nh)], writes=[("ps", b)])
                        P.op("dve", lambda e, mi=mi, nh=nh, b=b: e.tensor_copy(vm[:, mi, nh * 512:(nh + 1) * 512], psb[b]),
                             reads=[("ps", b)], writes=[("vm", mi, nh)])

            P.interleave([pre, prep])
            P.new_phase()
            A.off = R0
            NTG = S // TG
            qT = [qT0, A.bf16(KC, TG)]
            oT = [A.bf16(KC, TG), A.bf16(KC, TG)]
            ET = [[A.bf16(TG), A.bf16(TG)] for _ in range(4)]
            rec = [A.f32(TG), A.f32(TG)]
            rq = [0]
            rp = [0]
            rs_ = [0]

            def nbQ():
                rq[0] ^= 1
                return (2, 7)[rq[0]]

            HN = TG // 2 if TG >= 512 else TG

            def lnq(tg):
                for t0 in range(tg * TG, (tg + 1) * TG, HN):
                    layernorm(t0, HN, gcol, bcol, cast_eng="dve")

            def nbP():
                rp[0] ^= 1
                return (0, 1)[rp[0]]

            def nbS():
                rs_[0] ^= 1
                return (4, 5)[rs_[0]]

            def Qp(tg):
                t0 = tg * TG
                tok = slice(t0, t0 + TG)
                xk = RK("XT", ALLC, t0, TG)
                q = qT[tg % 2]
                for m in range(KC):
                    b = nbQ()
                    P.op("pe", mmgroup(psb[b][:, :TG], [(wq[:, k, m * 128:(m + 1) * 128], XT[:, k, tok]) for k in ALLC]),
                         reads=xk + [("wq", m // 4)], writes=[("ps", b)])
                    P.op("act", lambda e, m=m, b=b, q=q: e.activation(q[:, m, :], psb[b][:, :TG], AF.Identity, scale=1.0 / 16.0),
                         reads=[("ps", b)], writes=[("qT", tg % 2, m)])

            def Hd(tg):
                q = qT[tg % 2]
                o = oT[tg % 2]
                for h in range(4):
                    for mi in range(2):
                        sb_ = nbS()
                        P.op("pe", mmgroup(psb[sb_][:, :TG], [(kT[:, 2 * h + dd, mi * 128:(mi + 1) * 128], q[:, 2 * h + dd, :]) for dd in range(2)]),
                             reads=[("kT", 2 * h), ("kT", 2 * h + 1), ("qT", tg % 2, 2 * h), ("qT", tg % 2, 2 * h + 1)], writes=[("ps", sb_)])
                        P.op("act", lambda e, h=h, mi=mi, sb_=sb_: e.activation(ET[h][mi], psb[sb_][:, :TG], AF.Exp),
                             reads=[("ps", sb_)], writes=[("ET", h, mi)])
                for h in range(4):
                    i = h % 2
                    P.op("pe", mmgroup(psb[3][:, :TG], [(ones1, ET[h][mi]) for mi in range(2)]),
                         reads=[("ET", h, 0), ("ET", h, 1), "ones1"], writes=[("ps", 3)])
                    P.op("act", lambda e, i=i: e.activation(rec[i], psb[3][:, :TG], AF.Ln), reads=[("ps", 3)], writes=[("rec", i)])
                    P.op("act", lambda e, i=i: e.activation(rec[i], rec[i], AF.Exp, scale=-1.0), reads=[("rec", i)], writes=[("rec", i)])
                    for dd in range(2):
                        dc = 2 * h + dd
                        b = nbP()
                        P.op("pe", mmgroup(psb[b][:, :TG], [(vm[:, mi, dc * 128:(dc + 1) * 128], ET[h][mi]) for mi in range(2)]),
                             reads=[("ET", h, 0), ("ET", h, 1), ("vm", 0, dc // 4), ("vm", 1, dc // 4)], writes=[("ps", b)])
                        P.op("dve", lambda e, dc=dc, b=b, i=i, o=o: e.tensor_tensor(o[:, dc, :], psb[b][:, :TG], rec[i], ALU.mult),
                             reads=[("ps", b), ("rec", i)], writes=[("oT", tg % 2, dc)])

            def Op(tg):
                t0 = tg * TG
                tok = slice(t0, t0 + TG)
                o = oT[tg % 2]
                for m in range(KC):
                    b = nbQ()
                    P.op("pe", mmgroup(psb[b][:, :TG], [(wo_[:, k, m * 128:(m + 1) * 128], o[:, k, :]) for k in ALLC]),
                         reads=[("oT", tg % 2, k) for k in ALLC] + [("wo_", m // 4)], writes=[("ps", b)])
                    rk = RK("RT", [m], t0, TG)
                    P.op("dve", lambda e, m=m, tok=tok, b=b: e.scalar_tensor_tensor(
                        RT[:, m, tok], RT[:, m, tok], ALPHA, psb[b][:, :TG], ALU.mult, ALU.add),
                         reads=[("ps", b)] + rk, writes=rk)

            for tg in range(NTG + 1):
                streams = []
                if tg < NTG:
                    streams.append(lambda tg=tg: Hd(tg))

                def qo(tg=tg):
                    if tg >= 1:
                        Op(tg - 1)
                    if tg + 1 < NTG:
                        Qp(tg + 1)
                if tg >= 1 or tg + 1 < NTG:
                    streams.append(qo)
                if tg >= 2:
                    streams.append(lambda tg=tg: lnq(tg - 2))
                P.interleave(streams)
            return lambda: lnq(NTG - 1)

        if stop_after >= 3:
            tail = attention(C_LN + 32, C_LN + 40, pre=tail)
        if stop_after >= 4:
            tail = ffn(1, C_LN + 48, C_LN + 56, pre=tail, newphase=True)

        P.new_phase()
        A.off = PH
        ostage = [A.f32(D), A.f32(D)]
        otoks = []
        if "O" in DBG:
            for tt in range(NT):
                otoks.append(P.dma("sp", f"out{tt % 2}", out_d[tt * 128:(tt + 1) * 128, :].rearrange("p (c t) -> p c t", c=8),
                                   RT[:, :, tt * 128:(tt + 1) * 128], reads=RK("RT", ALLC, tt * 128, 128)))
        def store_tiles(tiles):
          for tt in tiles:
              sl = tt % 2
              for hb in range(2):
                  b = nextbank()

                  def tro(e, tt=tt, hb=hb, b=b):
                      ins = None
                      for q in range(4):
                          c = hb * 4 + q
                          ins = e.transpose(psb[b][:, q * 128:(q + 1) * 128], RT[:, c, tt * 128:(tt + 1) * 128], identF)
                      return ins
                  P.op("pe", tro, reads=RK("RT", list(range(hb * 4, hb * 4 + 4)), tt * 128, 128) + ["identF"], writes=[("ps", b)])
                  if hb == 0:
                      P.op("dve", lambda e, sl=sl, b=b: e.tensor_copy(ostage[sl][:, 0:512], psb[b]),
                           reads=[("ps", b)], writes=[("ost", sl, 0)])
                  else:
                      P.op("act", lambda e, sl=sl, b=b: e.activation(ostage[sl][:, 512:1024], psb[b], AF.Identity),
                           reads=[("ps", b)], writes=[("ost", sl, 1)])
              otoks.append(P.dma("sp", f"out{sl}", out_d[tt * 128:(tt + 1) * 128, :], ostage[sl],
                                 reads=[("ost", sl, 0), ("ost", sl, 1)]))

        if stop_after >= 4 and HALF % 256 == 0 and HALF >= 512:
            g4, b4 = C_LN + 48, C_LN + 56
            pieces = list(range(HALF, S, 256))
            lnp = lambda t0: (lambda: layernorm(t0, 256, g4, b4))
            tl = lambda t0, n: range(t0 // 128, (t0 + n) // 128)
            P.interleave([lambda: [lnp(pieces[0])(), lnp(pieces[1])()], lambda: store_tiles(range(NT // 2))])
            stored, lndone = HALF, HALF + 512
            for k in range(2, len(pieces)):
                P.interleave([lnp(pieces[k]), lambda a=stored, b=lndone: store_tiles(range(a // 128, b // 128))])
                stored, lndone = lndone, lndone + 256
            store_tiles(range(stored // 128, NT))
        elif tail is not None:
            P.interleave([tail, lambda: store_tiles(range(NT // 2))], weights=[1, 1])
            store_tiles(range(NT // 2, NT))
        else:
            store_tiles(range(NT // 2))
            store_tiles(range(NT // 2, NT))
        P.final_wait("sp", otoks[-2:])
        P.run(st)
    nc._in_names = _names
    return nc


def _blk_cols(w, nblk, width):
    K, N = w.shape
    kc = K // 128
    a = w.reshape(kc, 128, nblk, width).transpose(2, 1, 0, 3)
    return np.ascontiguousarray(a.reshape(nblk, 128, kc * width))


def prep_weights(inp):
    f = lambda a: np.asarray(a, dtype=np.float32)
    out = {}
    for i, nm in ((1, "w_ffn1"), (2, "w_ffn2")):
        w_in = f(inp[nm + "_in"])[0]
        g = w_in[:, :DFF].reshape(KC, 128, JC // 2, 2, 128)
        u = w_in[:, DFF:].reshape(KC, 128, JC // 2, 2, 128)
        blk = np.concatenate([g, u], axis=3)
        blk = blk.transpose(2, 1, 0, 3, 4).reshape(JC // 2, 128, KC * 512)
        out[f"wf{i}_in"] = np.ascontiguousarray(blk)
        w_out = f(inp[nm + "_out"])[0]
        out[f"wf{i}_out"] = _blk_cols(w_out, 8, 128)
    out["wmi"] = _blk_cols(f(inp["w_mix_in"])[0], 5, 512)
    out["wmo"] = _blk_cols(f(inp["w_mix_out"])[0], 2, 512)
    for nm, k in (("wq", "xa_wq"), ("wk", "xa_wk"), ("wv", "xa_wv"), ("wo", "xa_wo")):
        out[nm] = _blk_cols(f(inp[k])[0], 2, 512)
    pw = f(inp["pool_w"])[0]
    out["poolw"] = np.ascontiguousarray(pw.transpose(1, 0, 2).reshape(128, 512))
    cols = np.zeros((128, NCOLS), np.float32)
    col8 = lambda v: f(v).reshape(-1, 128).T
    for i, nm in enumerate(["ln1_g", "ln1_b", "ln2_g", "ln2_b", "ln3_g", "ln3_b", "ln4_g", "ln4_b"]):
        cols[:, C_LN + 8 * i:C_LN + 8 * i + 8] = col8(inp[nm][0])
    cols[:, C_PSC:C_PSC + 4] = col8(inp["pool_scale"][0])
    cols[:, C_GN] = f(inp["hgrn_gnorm"])[0]
    lb = f(inp["hgrn_lb"])
    cols[:, C_LBA:C_LBA + 4] = col8(lb[0])
    cols[:, C_LBB:C_LBB + 4] = col8(lb[1])
    out["cols"] = cols
    return out


_NC_CACHE = {}


def kernel(**inputs):
    x = np.asarray(inputs["x"], dtype=np.float32)
    mem = np.asarray(inputs["mem"], dtype=np.float32)
    B, S, _ = x.shape
    w = prep_weights(inputs)
    key = (S,)
    if key not in _NC_CACHE:
        _NC_CACHE[key] = build(S=S)
    nc = _NC_CACHE[key]
    in_maps = []
    for b in range(B):
        m = dict(w)
        m["x"] = np.ascontiguousarray(x[b])
        m["mem"] = np.ascontiguousarray(mem[b])
        in_maps.append(m)
    res = run_bass_kernel_spmd(nc, in_maps, core_ids=list(range(B)))
    return np.stack([np.asarray(r["out"], dtype=np.float32) for r in res.results], axis=0)
```

```python
from contextlib import ExitStack
import numpy as np
import concourse.bass as bass
import concourse.mybir as mybir
from concourse.bass_utils import run_bass_kernel_spmd

F32 = mybir.dt.float32
BF16 = mybir.dt.bfloat16
I32 = mybir.dt.int32
AF = mybir.ActivationFunctionType
ALU = mybir.AluOpType

D = 1024
KC = 8
DFF = 2816
JC = 22
NMEM = 256
ALPHA = 2.0 ** 0.25
LN_EPS = 1e-5
RMS_EPS = 1e-6
POOL_W = (2, 4, 8, 16)
NCOLS = 80
C_LN = 0
C_PSC = 64
C_GN = 68
C_LBA = 69
C_LBB = 73


class Prog:
    ENGS = ("pe", "act", "dve", "pool", "sp")

    def __init__(self, nc):
        self.nc = nc
        self.ops = {e: [] for e in self.ENGS}
        self.count = {e: 0 for e in self.ENGS}
        self.sem = {}
        self.dsem = {}
        self.last_w = {}
        self.readers = {}
        self.waited = {e: {} for e in self.ENGS}
        self.guard = []
        self.seen = set()
        self.n_waits = 0
        self.fuse_waits = True

    def _yield(self):
        il = getattr(self, "_il", None)
        if il is None:
            return
        import threading
        i = il["tl"].__dict__.get("idx")
        if i is None:
            return
        il["left"][i] -= 1
        if il["left"][i] > 0:
            return
        il["main"].release()
        il["sems"][i].acquire()

    def stream_wait(self, cond):
        while not cond():
            il = getattr(self, "_il", None)
            i = il["tl"].__dict__.get("idx") if il is not None else None
            if i is None:
                raise RuntimeError("stream_wait outside an interleave / condition never set")
            il["main"].release()
            il["sems"][i].acquire()

    def interleave(self, fns, weights=None):
        import threading
        fns = [f for f in fns if f is not None]
        if not fns:
            return
        n = len(fns)
        weights = list(weights or [1] * n)
        il = dict(tl=threading.local(), sems=[threading.Semaphore(0) for _ in range(n)],
                  main=threading.Semaphore(0), left=[0] * n, alive=[True] * n, err=[])
        self._il = il

        def runner(i):
            il["tl"].idx = i
            il["sems"][i].acquire()
            try:
                fns[i]()
            except BaseException as ex:
                il["err"].append(ex)
            il["alive"][i] = False
            il["tl"].idx = None
            il["main"].release()

        ths = [threading.Thread(target=runner, args=(i,), daemon=True) for i in range(n)]
        for t in ths:
            t.start()
        while any(il["alive"]):
            for i in range(n):
                if il["alive"][i]:
                    il["left"][i] = weights[i]
                    il["sems"][i].release()
                    il["main"].acquire()
                    if il["err"]:
                        self._il = None
                        raise il["err"][0]
        for t in ths:
            t.join()
        self._il = None

    def _sem_for(self, tok):
        if tok[0] == "eng":
            return self.sem[tok[1]], tok[2]
        return self.dsem[tok[1]][0], tok[2]

    def new_phase(self):
        g = {}
        for d in (self.last_w,):
            for t in d.values():
                n = t[0] + ":" + t[1]
                if n not in g or g[n][2] < t[2]:
                    g[n] = t
        for lst in self.readers.values():
            for t in lst:
                n = t[0] + ":" + t[1]
                if n not in g or g[n][2] < t[2]:
                    g[n] = t
        self.guard = list(g.values())
        self.seen = set()

    def _collect(self, eng, reads, writes):
        deps = []
        for r in reads:
            t = self.last_w.get(r)
            if t is not None:
                deps.append(t)
        for w in writes:
            t = self.last_w.get(w)
            if t is not None:
                deps.append(t)
            deps.extend(self.readers.get(w, ()))
            if w not in self.seen:
                self.seen.add(w)
                deps.extend(self.guard)
        need = {}
        for t in deps:
            if t[0] == "eng" and t[1] == eng and eng == "pe":
                continue
            name = t[0] + ":" + t[1]
            v = t[2]
            if self.waited[eng].get(name, 0) >= v:
                continue
            if name not in need or need[name][2] < v:
                need[name] = t
        for name, t in need.items():
            self.waited[eng][name] = t[2]
        return list(need.values())

    def _commit(self, tok, reads, writes):
        for r in reads:
            self.readers.setdefault(r, []).append(tok)
        for w in writes:
            self.last_w[w] = tok
            self.readers[w] = []

    def op(self, eng, fn, reads=(), writes=()):
        waits = self._collect(eng, reads, writes)
        self.count[eng] += 1
        seq = self.count[eng]
        self.n_waits += len(waits)

        def emit(e, fn=fn, waits=waits, eng=eng):
            ws = [self._sem_for(t) for t in waits]
            fuse = bool(ws) and eng != "pe" and self.fuse_waits
            for sv in (ws[:-1] if fuse else ws):
                e.wait_ge(*sv)
            ins = fn(e)
            if fuse:
                ins._wait_ge(*ws[-1])
            ins.then_inc(self.sem[eng], 1)

        self.ops[eng].append(emit)
        tok = ("eng", eng, seq)
        self._commit(tok, reads, writes)
        self._yield()
        return tok

    def dma(self, eng, key, out, in_, reads=(), writes=(), **kw):
        waits = self._collect(eng, reads, writes)
        if key not in self.dsem:
            self.dsem[key] = [None, 0]
        self.dsem[key][1] += 16
        cnt = self.dsem[key][1]
        self.n_waits += len(waits)

        def emit(e, waits=waits, key=key, out=out, in_=in_, kw=kw):
            for t in waits:
                e.wait_ge(*self._sem_for(t))
            e.dma_start(out=out, in_=in_, **kw).then_inc(self.dsem[key][0], 16)

        self.ops[eng].append(emit)
        tok = ("dma", key, cnt)
        self._commit(tok, reads, writes)
        self._yield()
        return tok

    def final_wait(self, eng, toks):
        def emit(e, toks=list(toks)):
            for t in toks:
                e.wait_ge(*self._sem_for(t))
        self.ops[eng].append(emit)

    def run(self, stack):
        nc = self.nc
        for e in self.ENGS:
            self.sem[e] = stack.enter_context(nc.semaphore("prog_" + e))
        for k in self.dsem:
            self.dsem[k][0] = stack.enter_context(nc.semaphore("dma_" + k))
        block = stack.enter_context(nc.Block())

        @block.tensor
        def _(e):
            for f in self.ops["pe"]:
                f(e)

        @block.scalar
        def _(e):
            for f in self.ops["act"]:
                f(e)

        @block.vector
        def _(e):
            for f in self.ops["dve"]:
                f(e)

        @block.gpsimd
        def _(e):
            for f in self.ops["pool"]:
                f(e)

        @block.sync
        def _(e):
            for f in self.ops["sp"]:
                f(e)


def build(S=2048, TG=512, U=256, stop_after=4):
    assert S % (2 * TG) == 0 and TG % U == 0 and U % 128 == 0
    nc = bass.Bass("TRN2", target_bir_lowering=False)
    NT = S // 128
    HALF = S // 2
    NTGH = HALF // TG

    import os
    _u = "u" in os.environ.get("KDBG", "")
    _names = []

    def din(name, shape):
        if _u and stop_after == 0 and name not in ("x", "cols"):
            return None
        _names.append(name)
        return nc.dram_tensor(name, list(shape), F32, kind="ExternalInput").ap()

    x_d = din("x", [S, D])
    mem_d = din("mem", [NMEM, D])
    wf_in_d = [din("wf1_in", [11, 128, KC * 512]), din("wf2_in", [11, 128, KC * 512])]
    wf_out_d = [din("wf1_out", [8, 128, JC * 128]), din("wf2_out", [8, 128, JC * 128])]
    wmi_d = din("wmi", [5, 128, KC * 512])
    wmo_d = din("wmo", [2, 128, KC * 512])
    wq_d = din("wq", [2, 128, KC * 512])
    wk_d = din("wk", [2, 128, KC * 512])
    wv_d = din("wv", [2, 128, KC * 512])
    wo_d = din("wo", [2, 128, KC * 512])
    poolw_d = din("poolw", [128, 512])
    cols_d = din("cols", [128, NCOLS])
    out_d = nc.dram_tensor("out", [S, D], F32, kind="ExternalOutput").ap()

    st = ExitStack()
    with st:
        AW = 53200
        arena = st.enter_context(nc.sbuf_tensor("arena", [128, AW], F32))
        psb = [st.enter_context(nc.psum_tensor(f"ps{i}", [128, 512], F32))[:] for i in range(8)]
        P = Prog(nc)

        class Alloc:
            def __init__(self, base):
                self.off = base

            def f32(self, *shape):
                n = int(np.prod(shape))
                v = arena[:, self.off:self.off + n]
                self.off += n
                assert self.off <= AW, f"arena overflow {self.off}"
                if len(shape) == 2:
                    return v.rearrange("p (a b) -> p a b", a=shape[0])
                if len(shape) == 3:
                    return v.rearrange("p (a b c) -> p a b c", a=shape[0], b=shape[1])
                return v

            def bf16(self, *shape):
                n = int(np.prod(shape))
                assert n % 2 == 0
                v = arena[:, self.off:self.off + n // 2].bitcast(BF16)
                self.off += n // 2
                assert self.off <= AW, f"arena overflow {self.off}"
                if len(shape) == 2:
                    return v.rearrange("p (a b) -> p a b", a=shape[0])
                if len(shape) == 3:
                    return v.rearrange("p (a b c) -> p a b c", a=shape[0], b=shape[1])
                return v

        A = Alloc(0)
        RT = A.f32(KC, S)
        XT = A.bf16(KC, S)
        identF = A.f32(128)
        onesD = A.bf16(128)
        onesV = A.bf16(128)
        ones1 = A.bf16(128)
        mask2 = A.f32(128)
        cols = A.f32(NCOLS)
        lbc = A.f32(4)
        omlc = A.f32(4)
        rcfix = A.f32(4, 16)
        iot = A.f32(16)
        ioti = arena[:, A.off:A.off + 16].bitcast(I32)
        A.off += 16
        scanmask = A.f32(U)
        mean_off = A.off
        mean = A.f32(TG)
        m2 = A.f32(TG)
        rstd = A.f32(TG)
        zsq_off = A.off
        zsq = A.bf16(KC, TG)
        PH = A.off

        def RK(name, cs, t0, n):
            return [(name, c, tt) for c in cs for tt in range(t0 // 128, (t0 + n + 127) // 128)]

        ALLC = list(range(KC))
        bank_rr = [0]

        def nextbank():
            b = bank_rr[0]
            bank_rr[0] = (b + 1) % 4
            return b

        def mmgroup(out, pairs):
            def fn(e, out=out, pairs=pairs):
                n = len(pairs)
                ins = None
                for i, (l, r) in enumerate(pairs):
                    ins = e.matmul(out, l, r, start=(i == 0), stop=(i == n - 1))
                return ins
            return fn

        def wload(key, dst, src, reskey, rows=None):
            P.dma("pool", key, dst, src, writes=[reskey])

        P.dma("sp", "cols", cols, cols_d, writes=["cols"])
        P.op("pool", lambda e: e.memset(identF, 0.0), writes=["identF"])
        P.op("pool", lambda e: e.affine_select(out=identF, in_=identF, pattern=[[-1, 128]], compare_op=ALU.not_equal,
                                               fill=1.0, base=0, channel_multiplier=1),
             reads=["identF"], writes=["identF"])
        import os
        DBG = os.environ.get("KDBG", "")
        P.op("pool", lambda e: e.memset(onesD, 1.0 / 1024.0), writes=["onesD"])
        P.op("pool", lambda e: e.memset(onesV, 1.0 / 128.0), writes=["onesV"])
        P.op("pool", lambda e: e.memset(ones1, 1.0), writes=["ones1"])
        if "m" not in DBG:
            P.op("pool", lambda e: e.memset(mask2, 1.0), writes=["mask2"])
            P.op("pool", lambda e: e.affine_select(out=mask2, in_=mask2, pattern=[[1, 128]], compare_op=ALU.is_ge,
                                                   fill=0.0, base=0, channel_multiplier=-1),
                 reads=["mask2"], writes=["mask2"])
            P.op("pool", lambda e: e.memset(mask2[0:64, 64:128], 0.0), reads=["mask2"], writes=["mask2"])
        if "s" not in DBG:
            P.op("pool", lambda e: e.memset(scanmask, 1.0), writes=["scanmask"])
            P.op("pool", lambda e: e.memset(scanmask.rearrange("p (a b) -> p a b", b=64)[:, :, 0:1], 0.0),
                 reads=["scanmask"], writes=["scanmask"])
        if "i" not in DBG:
            P.op("pool", lambda e: e.iota(ioti, pattern=[[1, 16]], base=1, channel_multiplier=0), writes=["ioti"])
            P.op("pool", lambda e: e.tensor_copy(iot, ioti), reads=["ioti"], writes=["iot"])
            for g, w in enumerate(POOL_W):
                P.op("dve", lambda e, g=g, w=w: e.tensor_scalar_min(rcfix[:, g, :], iot, float(w)),
                     reads=["iot"], writes=[("rcfix", g)])
                P.op("dve", lambda e, g=g: e.reciprocal(rcfix[:, g, :], rcfix[:, g, :]),
                     reads=[("rcfix", g)], writes=[("rcfix", g)])
        if "l" not in DBG:
            P.op("dve", lambda e: e.tensor_tensor(lbc, cols[:, C_LBA:C_LBA + 4], cols[:, C_LBB:C_LBB + 4], ALU.subtract),
                 reads=["cols"], writes=["lbc"])
            P.op("act", lambda e: e.activation(lbc, lbc, AF.Sigmoid), reads=["lbc"], writes=["lbc"])
            P.op("dve", lambda e: e.tensor_scalar(omlc, lbc, -1.0, 1.0, ALU.mult, ALU.add), reads=["lbc"], writes=["omlc"])

        zsq_flat = zsq.rearrange("p c t -> p (c t)")

        def layernorm(t0, n, gcol, bcol, cast_eng="dve"):
            zsq = zsq_flat[:, :KC * n].rearrange("p (c t) -> p c t", c=KC)
            tok = slice(t0, t0 + n)
            rk = RK("RT", ALLC, t0, n)
            xk = RK("XT", ALLC, t0, n)
            P.op(cast_eng, lambda e: e.tensor_copy(XT[:, :, tok], RT[:, :, tok]), reads=rk, writes=xk)
            P.op("act", lambda e: e.activation(zsq[:, :, :n], RT[:, :, tok], AF.Square), reads=rk, writes=["zsq"])
            P.op("pe", mmgroup(psb[6][:, :n], [(onesD, XT[:, c, tok]) for c in ALLC]),
                 reads=xk + ["onesD"], writes=[("ps", 6)])
            if n <= 256:
                msq_ps, msq_key = psb[6][:, 256:256 + n], ("ps", 6)
            else:
                msq_ps, msq_key = psb[7][:, :n], ("ps", 7)
            P.op("pe", mmgroup(msq_ps, [(onesD, zsq[:, c, :n]) for c in ALLC]),
                 reads=["zsq", "onesD"], writes=[msq_key])
            P.op("act", lambda e: e.activation(mean[:, :n], psb[6][:, :n], AF.Identity),
                 reads=[("ps", 6)], writes=["mean"])
            P.op("dve", lambda e: e.tensor_tensor(m2[:, :n], mean[:, :n], mean[:, :n], ALU.mult),
                 reads=["mean"], writes=["m2"])
            P.op("dve", lambda e: e.tensor_tensor(m2[:, :n], msq_ps, m2[:, :n], ALU.subtract),
                 reads=[msq_key, "m2"], writes=["m2"])
            P.op("act", lambda e: e.activation(m2[:, :n], m2[:, :n], AF.Ln, bias=LN_EPS, scale=1.0),
                 reads=["m2"], writes=["m2"])
            P.op("act", lambda e: e.activation(rstd[:, :n], m2[:, :n], AF.Exp, scale=-0.5),
                 reads=["m2"], writes=["rstd"])
            P.op("dve", lambda e: e.tensor_tensor(RT[:, :, tok], RT[:, :, tok],
                                                  mean[:, :n].unsqueeze(1).to_broadcast([128, KC, n]), ALU.subtract),
                 reads=rk + ["mean"], writes=rk)
            P.op("dve", lambda e: e.tensor_tensor(RT[:, :, tok], RT[:, :, tok],
                                                  rstd[:, :n].unsqueeze(1).to_broadcast([128, KC, n]), ALU.mult),
                 reads=rk + ["rstd"], writes=rk)
            for c in ALLC:
                P.op("act", lambda e, c=c: e.activation(RT[:, c, tok], RT[:, c, tok], AF.Identity,
                                                        scale=cols[:, gcol + c:gcol + c + 1],
                                                        bias=cols[:, bcol + c:bcol + c + 1]),
                     reads=RK("RT", [c], t0, n) + ["cols"], writes=RK("RT", [c], t0, n))
            P.op(cast_eng, lambda e: e.tensor_copy(XT[:, :, tok], RT[:, :, tok]), reads=rk, writes=xk)

        A.off = PH
        xstage = [A.f32(D), A.f32(D)]

        def load_x(tiles):
          for tt in tiles:
              sl = tt % 2
              P.dma("sp", f"xs{sl}", xstage[sl], x_d[tt * 128:(tt + 1) * 128, :], writes=[("xs", sl)])
              for hb in range(2):
                  b = 6 + hb

                  def trf(e, sl=sl, hb=hb, b=b):
                      ins = None
                      for q in range(4):
                          c = hb * 4 + q
                          ins = e.transpose(psb[b][:, q * 128:(q + 1) * 128], xstage[sl][:, c * 128:(c + 1) * 128], identF)
                      return ins
                  P.op("pe", trf, reads=[("xs", sl), "identF"], writes=[("ps", b)])
                  cs = list(range(hb * 4, hb * 4 + 4))
                  pv = psb[b].rearrange("p (a b) -> p a b", a=4)
                  P.op("dve", lambda e, hb=hb, tt=tt, pv=pv: e.tensor_copy(RT[:, hb * 4:hb * 4 + 4, tt * 128:(tt + 1) * 128], pv),
                       reads=[("ps", b)], writes=RK("RT", cs, tt * 128, 128))
                  if "A" not in DBG:
                      P.op("act", lambda e, hb=hb, tt=tt: e.activation(XT[:, hb * 4:hb * 4 + 4, tt * 128:(tt + 1) * 128],
                                                                       RT[:, hb * 4:hb * 4 + 4, tt * 128:(tt + 1) * 128], AF.Identity),
                           reads=RK("RT", cs, tt * 128, 128), writes=RK("XT", cs, tt * 128, 128))


        load_x(range(NT // 2))

        def ffn(idx, gcol, bcol, pre=None, newphase=False):
            if newphase:
                P.new_phase()
            A.off = PH + 2 * D
            H = A.bf16(JC, HALF)
            wi = [A.bf16(KC, 512), A.bf16(KC, 512)]
            wo = [A.bf16(JC, 128), A.bf16(JC, 128)]
            sg = [A.f32(TG), A.f32(TG)]
            w_in_d, w_out_d = wf_in_d[idx], wf_out_d[idx]
            cnt = [0, 0, 0]

            def Aph(half):
                t0h = half * HALF
                for jj in range(JC // 2):
                    sl = cnt[0] % 2
                    cnt[0] += 1
                    wload(f"wi{sl}", wi[sl].rearrange("p k n -> p (k n)").rearrange("p (r e) -> p r e", e=2048),
                          w_in_d[jj].rearrange("p (r e) -> p r e", e=2048), ("wi", sl))
                    for jl in range(2):
                        j = 2 * jj + jl
                        for tg in range(NTGH):
                            t0 = t0h + tg * TG
                            tok = slice(t0, t0 + TG)
                            i = cnt[1] % 2
                            cnt[1] += 1
                            gb, ub = i, 2 + i
                            xk = RK("XT", ALLC, t0, TG)
                            P.op("pe", mmgroup(psb[gb][:, :TG], [(wi[sl][:, k, jl * 128:(jl + 1) * 128], XT[:, k, tok]) for k in ALLC]),
                                 reads=[("wi", sl)] + xk, writes=[("ps", gb)])
                            P.op("pe", mmgroup(psb[ub][:, :TG], [(wi[sl][:, k, 256 + jl * 128:256 + (jl + 1) * 128], XT[:, k, tok]) for k in ALLC]),
                                 reads=[("wi", sl)] + xk, writes=[("ps", ub)])
                            P.op("act", lambda e, i=i, gb=gb: e.activation(sg[i], psb[gb][:, :TG], AF.Silu),
                                 reads=[("ps", gb)], writes=[("sg", i)])
                            P.op("dve", lambda e, i=i, ub=ub, j=j, tg=tg: e.scalar_tensor_tensor(
                                H[:, j, tg * TG:(tg + 1) * TG], sg[i], 0.5, psb[ub][:, :TG], ALU.mult, ALU.mult),
                                 reads=[("sg", i), ("ps", ub)], writes=[("H", j, tg)])

            def Bph(half):
                t0h = half * HALF
                for m in range(KC):
                    sl = cnt[2] % 2
                    cnt[2] += 1
                    wload(f"wo{sl}", wo[sl].rearrange("p j n -> p (j n)").rearrange("p (r e) -> p r e", e=1408),
                          w_out_d[m].rearrange("p (r e) -> p r e", e=1408), ("wo", sl))
                    for tg in range(NTGH):
                        t0 = t0h + tg * TG
                        tok = slice(t0, t0 + TG)
                        ob = 4 + (m * NTGH + tg) % 2
                        P.op("pe", mmgroup(psb[ob][:, :TG], [(wo[sl][:, j, :], H[:, j, tg * TG:(tg + 1) * TG]) for j in range(JC)]),
                             reads=[("wo", sl)] + [("H", j, tg) for j in range(JC)], writes=[("ps", ob)])
                        rk = RK("RT", [m], t0, TG)
                        P.op("dve", lambda e, m=m, tok=tok, ob=ob: e.scalar_tensor_tensor(
                            RT[:, m, tok], RT[:, m, tok], ALPHA, psb[ob][:, :TG], ALU.mult, ALU.add),
                             reads=[("ps", ob)] + rk, writes=rk)

            def LNs(half):
                for tg in range(NTGH):
                    layernorm(half * HALF + tg * TG, TG, gcol, bcol)

            if pre is not None:
                P.interleave([pre, lambda: Aph(0)], weights=[1, 6])
            else:
                Aph(0)
            Bph(0)
            P.interleave([lambda: LNs(0), lambda: Aph(1)], weights=[1, 6])
            Bph(1)
            return lambda: LNs(1)

        tail = None
        if stop_after >= 1:
            tail = ffn(0, C_LN + 0, C_LN + 8, pre=lambda: load_x(range(NT // 2, NT)))
        else:
            load_x(range(NT // 2, NT))

        def mixer(gcol, bcol, pre=None):
            P.new_phase()
            A.off = PH
            NTU = U // 128
            NCH = U // 64
            NUU = S // U
            wmi = A.bf16(KC, 2560)
            wmo = A.bf16(KC, D)
            poolw = A.bf16(512)
            vtok = [A.bf16(NTU, 512), A.bf16(NTU, 512)]
            MT = A.bf16(KC, U)
            fl_off = A.off
            F_, L_, C_, E_ = A.f32(U), A.f32(U), A.f32(U), A.f32(U)
            PAIR = (TG >= 2 * U)
            if PAIR:
                F2 = arena[:, fl_off:fl_off + 2 * U].rearrange("p (a b) -> p a b", a=2)
                L2 = arena[:, fl_off + 2 * U:fl_off + 4 * U].rearrange("p (a b) -> p a b", a=2)
                zu = zsq_off + KC * U // 2
                C2 = arena[:, zu:zu + 2 * U].rearrange("p (a b) -> p a b", a=2)
                E2 = arena[:, zu + 2 * U:zu + 4 * U].rearrange("p (a b) -> p a b", a=2)
                sgt2 = arena[:, mean_off:mean_off + 2 * TG].rearrange("p (a b) -> p a b", a=2)[:, :, U:2 * U]
            qd4 = [A.bf16(4, U), A.bf16(4, U)]
            kd4 = [A.bf16(4, U), A.bf16(4, U)]
            kltok4 = [A.bf16(4, U), A.bf16(4, U)]
            EL = [A.f32(4, NCH), A.f32(4, NCH)]
            sgt_default = mean[:, U:2 * U] if TG >= 2 * U else A.f32(U)
            Sf = A.f32(4, 128)
            Sb = A.bf16(4, 128)
            sT4 = A.bf16(4, 128)
            osq2 = A.bf16(2, U)
            rs2 = A.f32(2, U)
            sgg2 = A.f32(2, U)
            VP = [A.f32(16 + U), A.f32(16 + U), A.f32(16 + U)]
            halo = A.f32(4, 16)
            pooled = A.bf16(U)
            pfix = A.f32(16)
            OB = (3, 5)
            rrA = [0]
            rrC = [0]

            def nbA():
                return 0

            def nbC():
                rrC[0] ^= 1
                return (2, 7)[rrC[0]]

            def prologue():
                for b in (1, 0, 2, 3, 4):
                    wload(f"wmi{b}", wmi[:, :, b * 512:(b + 1) * 512], wmi_d[b].rearrange("p (k n) -> p k n", k=KC), ("wmi", b))
                for b in range(2):
                    wload(f"wmo{b}", wmo[:, :, b * 512:(b + 1) * 512], wmo_d[b].rearrange("p (k n) -> p k n", k=KC), ("wmo", b))
                wload("poolw", poolw, poolw_d, "poolw")
                P.op("dve", lambda e: e.memset(Sf, 0.0), writes=["Sf"])
                P.op("dve", lambda e: e.memset(Sb, 0.0), writes=["Sb"])
                P.op("dve", lambda e: e.memset(halo, 0.0), writes=[("halo", h) for h in range(4)])
                P.op("dve", lambda e: e.memset(VP[1], 0.0), writes=[("VP", 1)])
                P.op("dve", lambda e: e.memset(VP[2], 0.0), writes=[("VP", 2)])
                V(0)
                for h in range(4):
                    A1(0, h, sgt=VP[1][:, :U], sgk=("VP", 1))

            def V(u):
                vs = vtok[u % 2]
                for ti in range(NTU):
                    b = nbA()
                    ts0 = u * U + ti * 128
                    tsl = slice(ts0, ts0 + 128)
                    P.op("pe", mmgroup(psb[b], [(XT[:, k, tsl], wmi[:, k, 1024:1536]) for k in ALLC]),
                         reads=RK("XT", ALLC, ts0, 128) + [("wmi", 2)], writes=[("ps", b)])
                    P.op("act", lambda e, ti=ti, b=b, vs=vs: e.activation(vs[:, ti, :], psb[b], AF.Identity),
                         reads=[("ps", b)], writes=[("vtok", u % 2, ti)])

            def A1(u, h, sgt=None, sgk="sgt"):
                sgt = sgt_default if sgt is None else sgt
                s_ = u % 2
                t0 = u * U
                tok = slice(t0, t0 + U)
                xk = RK("XT", ALLC, t0, U)
                b = nbA()
                P.op("pe", mmgroup(psb[b][:, :U], [(wmi[:, k, 512 + h * 128:512 + (h + 1) * 128], XT[:, k, tok]) for k in ALLC]),
                     reads=xk + [("wmi", 1)], writes=[("ps", b)])
                P.op("act", lambda e, b=b: e.activation(F_, psb[b][:, :U], AF.Exp, scale=-1.0), reads=[("ps", b)], writes=["F_"])
                P.op("act", lambda e: e.activation(F_, F_, AF.Ln, bias=1.0, scale=1.0), reads=["F_"], writes=["F_"])
                P.op("act", lambda e: e.activation(F_, F_, AF.Exp, scale=-1.0), reads=["F_"], writes=["F_"])
                P.op("dve", lambda e, h=h: e.tensor_scalar(F_, F_, omlc[:, h:h + 1], lbc[:, h:h + 1], ALU.mult, ALU.add),
                     reads=["F_", "lbc", "omlc"], writes=["F_"])
                P.op("act", lambda e: e.activation(L_, F_, AF.Ln), reads=["F_"], writes=["L_"])
                P.op("dve", lambda e: e.tensor_scalar(F_, F_, -1.0, 1.0, ALU.mult, ALU.add), reads=["F_"], writes=["F_"])
                P.op("dve", lambda e: e.tensor_tensor_scan(C_, scanmask, L_, 0.0, ALU.mult, ALU.add),
                     reads=["L_", "scanmask"], writes=["C_"])
                P.op("act", lambda e: e.activation(E_, C_, AF.Exp), reads=["C_"], writes=["E_"])
                P.op("act", lambda e: e.activation(L_, C_, AF.Exp, scale=-1.0), reads=["C_"], writes=["L_"])
                P.op("dve", lambda e: e.tensor_tensor(L_, F_, L_, ALU.mult), reads=["F_", "L_"], writes=["L_"])
                P.op("pool", lambda e, s_=s_, h=h: e.tensor_copy(kd4[s_][:, h, :], L_),
                     reads=["L_"], writes=[("kd", s_, h)])
                P.op("pool", lambda e, s_=s_, h=h: e.tensor_copy(EL[s_][:, h, :], E_.rearrange("p (a b) -> p a b", b=64)[:, :, 63]),
                     reads=["E_"], writes=[("EL", s_)])
                P.op("dve", lambda e: e.tensor_tensor(
                    C_.rearrange("p (a b) -> p a b", b=64), L_.rearrange("p (a b) -> p a b", b=64),
                    E_.rearrange("p (a b) -> p a b", b=64)[:, :, 63:64].to_broadcast([128, NCH, 64]), ALU.mult),
                     reads=["L_", "E_"], writes=["C_"])
                b = nbA()
                P.op("pe", mmgroup(psb[b][:, :U], [(wmi[:, k, h * 128:(h + 1) * 128], XT[:, k, tok]) for k in ALLC]),
                     reads=xk + [("wmi", 0)], writes=[("ps", b)])
                P.op("act", lambda e, b=b: e.activation(sgt, psb[b][:, :U], AF.Exp, scale=-1.0), reads=[("ps", b)], writes=[sgk])
                P.op("act", lambda e: e.activation(sgt, sgt, AF.Ln, bias=1.0, scale=1.0), reads=[sgk], writes=[sgk])
                P.op("act", lambda e: e.activation(sgt, sgt, AF.Exp, scale=-1.0), reads=[sgk], writes=[sgk])
                P.op("dve", lambda e: e.tensor_tensor(sgt, sgt, E_, ALU.mult), reads=[sgk, "E_"], writes=[sgk])
                P.op("dve", lambda e, s_=s_, h=h, b=b: e.tensor_tensor(qd4[s_][:, h, :], psb[b][:, :U], sgt, ALU.mult),
                     reads=[sgk, ("ps", b)], writes=[("qd", s_, h)])
                b = nbA()

                def trk(e, b=b):
                    ins = None
                    for ti in range(NTU):
                        ins = e.transpose(psb[b][:, ti * 128:(ti + 1) * 128], C_[:, ti * 128:(ti + 1) * 128], identF)
                    return ins
                P.op("pe", trk, reads=["C_", "identF"], writes=[("ps", b)])
                P.op("act", lambda e, b=b, s_=s_, h=h: e.activation(kltok4[s_][:, h, :], psb[b][:, :U], AF.Identity),
                     reads=[("ps", b)], writes=[("kltok", s_, h)])

            def A1p(u, hp):
                s_ = u % 2
                t0 = u * U
                tok = slice(t0, t0 + U)
                xk = RK("XT", ALLC, t0, U)
                hh = (2 * hp, 2 * hp + 1)
                b = nbA()
                for j, h in enumerate(hh):
                    P.op("pe", mmgroup(psb[b][:, j * U:(j + 1) * U], [(wmi[:, k, 512 + h * 128:512 + (h + 1) * 128], XT[:, k, tok]) for k in ALLC]),
                         reads=xk + [("wmi", 1)], writes=[("ps", b)])
                pf = psb[b][:, :2 * U].rearrange("p (a b) -> p a b", a=2)
                P.op("act", lambda e, pf=pf: e.activation(F2, pf, AF.Exp, scale=-1.0), reads=[("ps", b)], writes=["F2"])
                P.op("act", lambda e: e.activation(F2, F2, AF.Ln, bias=1.0, scale=1.0), reads=["F2"], writes=["F2"])
                P.op("act", lambda e: e.activation(F2, F2, AF.Exp, scale=-1.0), reads=["F2"], writes=["F2"])
                for j, h in enumerate(hh):
                    P.op("act", lambda e, j=j, h=h: e.activation(F2[:, j, :], F2[:, j, :], AF.Identity,
                                                                 scale=omlc[:, h:h + 1], bias=lbc[:, h:h + 1]),
                         reads=["F2", "lbc", "omlc"], writes=["F2"])
                P.op("act", lambda e: e.activation(L2, F2, AF.Ln), reads=["F2"], writes=["L2"])
                P.op("act", lambda e: e.activation(F2, F2, AF.Identity, scale=-1.0, bias=1.0), reads=["F2"], writes=["F2"])
                for j in range(2):
                    P.op("dve", lambda e, j=j: e.tensor_tensor_scan(C2[:, j, :], scanmask, L2[:, j, :], 0.0, ALU.mult, ALU.add),
                         reads=["L2", "scanmask"], writes=["C2"])
                P.op("act", lambda e: e.activation(E2, C2, AF.Exp), reads=["C2"], writes=["E2"])
                P.op("act", lambda e: e.activation(L2, C2, AF.Exp, scale=-1.0), reads=["C2"], writes=["L2"])
                P.op("dve", lambda e: e.tensor_tensor(L2, F2, L2, ALU.mult), reads=["F2", "L2"], writes=["L2"])
                P.op("pool", lambda e, s_=s_, hp=hp: e.tensor_copy(kd4[s_][:, 2 * hp:2 * hp + 2, :], L2),
                     reads=["L2"], writes=[("kd", s_, 2 * hp), ("kd", s_, 2 * hp + 1)])
                E4 = E2.rearrange("p a (c b) -> p a c b", b=64)
                P.op("pool", lambda e, s_=s_, hp=hp: e.tensor_copy(EL[s_][:, 2 * hp:2 * hp + 2, :], E4[:, :, :, 63]),
                     reads=["E2"], writes=[("EL", s_)])
                P.op("dve", lambda e: e.tensor_tensor(
                    C2.rearrange("p a (c b) -> p (a c) b", b=64), L2.rearrange("p a (c b) -> p (a c) b", b=64),
                    E2.rearrange("p a (c b) -> p (a c) b", b=64)[:, :, 63:64].to_broadcast([128, 2 * NCH, 64]), ALU.mult),
                     reads=["L2", "E2"], writes=["C2"])
                b = nbA()
                for j, h in enumerate(hh):
                    P.op("pe", mmgroup(psb[b][:, j * U:(j + 1) * U], [(wmi[:, k, h * 128:(h + 1) * 128], XT[:, k, tok]) for k in ALLC]),
                         reads=xk + [("wmi", 0)], writes=[("ps", b)])
                pq = psb[b][:, :2 * U].rearrange("p (a b) -> p a b", a=2)
                P.op("act", lambda e, pq=pq: e.activation(sgt2, pq, AF.Exp, scale=-1.0), reads=[("ps", b)], writes=["sgt2"])
                P.op("act", lambda e: e.activation(sgt2, sgt2, AF.Ln, bias=1.0, scale=1.0), reads=["sgt2"], writes=["sgt2"])
                P.op("act", lambda e: e.activation(sgt2, sgt2, AF.Exp, scale=-1.0), reads=["sgt2"], writes=["sgt2"])
                P.op("dve", lambda e: e.tensor_tensor(sgt2, sgt2, E2, ALU.mult), reads=["sgt2", "E2"], writes=["sgt2"])
                P.op("dve", lambda e, s_=s_, hp=hp, pq=pq: e.tensor_tensor(qd4[s_][:, 2 * hp:2 * hp + 2, :], pq, sgt2, ALU.mult),
                     reads=["sgt2", ("ps", b)], writes=[("qd", s_, 2 * hp), ("qd", s_, 2 * hp + 1)])
                b = nbA()

                def trk(e, b=b):
                    ins = None
                    for j in range(2):
                        for ti in range(NTU):
                            ins = e.transpose(psb[b][:, j * U + ti * 128:j * U + (ti + 1) * 128], C2[:, j, ti * 128:(ti + 1) * 128], identF)
                    return ins
                P.op("pe", trk, reads=["C2", "identF"], writes=[("ps", b)])
                P.op("act", lambda e, b=b, s_=s_, hp=hp: e.activation(
                    kltok4[s_][:, 2 * hp:2 * hp + 2, :], psb[b][:, :2 * U].rearrange("p (a b) -> p a b", a=2), AF.Identity),
                     reads=[("ps", b)], writes=[("kltok", s_, 2 * hp), ("kltok", s_, 2 * hp + 1)])

            def A1any(u, hlist):
                hlist = list(hlist)
                if PAIR and len(hlist) % 2 == 0:
                    for hp in sorted(set(h // 2 for h in hlist)):
                        A1p(u, hp)
                else:
                    for h in hlist:
                        A1(u, h)

            H4 = list(range(4))

            def Bchunk(u, c):
                s_ = u % 2
                ti, cc = c // 2, c % 2
                c0 = ti * 128
                q0 = c * 64
                r0 = cc * 64
                if cc == 0:
                    def sc(e):
                        ins = None
                        for h in H4:
                            ins = e.matmul(psb[4][:, h * 128:(h + 1) * 128], kd4[s_][:, h, c0:c0 + 128], qd4[s_][:, h, c0:c0 + 128],
                                           start=True, stop=True)
                        return ins
                    P.op("pe", sc, reads=[("kd", s_, h) for h in H4] + [("qd", s_, h) for h in H4], writes=[("ps", 4)])
                    P.op("dve", lambda e: e.tensor_tensor(sT4, psb[4].rearrange("p (a b) -> p a b", a=4),
                                                          mask2.unsqueeze(1).to_broadcast([128, 4, 128]), ALU.mult),
                         reads=[("ps", 4), "mask2"], writes=["sT4"])

                def pm(e):
                    ins = None
                    for h in H4:
                        ins = e.matmul(psb[4][:, h * 128:(h + 1) * 128], kltok4[s_][r0:r0 + 64, h, c0:c0 + 128],
                                       vtok[s_][r0:r0 + 64, ti, h * 128:(h + 1) * 128], start=True, stop=True)
                    return ins
                P.op("pe", pm, reads=[("kltok", s_, h) for h in H4] + [("vtok", s_, ti)], writes=[("ps", 4)])

                def io(e):
                    ins = None
                    for h in H4:
                        ob = OB[h // 2]
                        col = (h % 2) * U + q0
                        e.matmul(psb[ob][:, col:col + 64], vtok[s_][:, ti, h * 128:(h + 1) * 128], sT4[:, h, r0:r0 + 64],
                                 start=True, stop=False)
                        ins = e.matmul(psb[ob][:, col:col + 64], Sb[:, h, :], qd4[s_][:, h, q0:q0 + 64], start=False, stop=True)
                    return ins
                P.op("pe", io, reads=["sT4", ("vtok", s_, ti), "Sb"] + [("qd", s_, h) for h in H4],
                     writes=[("ps", OB[0]), ("ps", OB[1])])
                P.op("dve", lambda e: e.tensor_tensor(Sf, Sf, EL[s_][:, :, c:c + 1].to_broadcast([128, 4, 128]), ALU.mult),
                     reads=["Sf", ("EL", s_)], writes=["Sf"])
                P.op("dve", lambda e: e.tensor_tensor(Sf, Sf, psb[4].rearrange("p (a b) -> p a b", a=4), ALU.add),
                     reads=["Sf", ("ps", 4)], writes=["Sf"])
                P.op("act", lambda e: e.activation(Sb, Sf, AF.Identity), reads=["Sf"], writes=["Sb"])

            def Cnorm(u, hp):
                t0 = u * U
                tok = slice(t0, t0 + U)
                xk = RK("XT", ALLC, t0, U)
                ob = OB[hp]
                pv = psb[ob][:, :2 * U].rearrange("p (a b) -> p a b", a=2)
                P.op("act", lambda e: e.activation(osq2, pv, AF.Square), reads=[("ps", ob)], writes=["osq2"])
                P.op("pe", mmgroup(psb[4][:, :2 * U], [(onesV, osq2.rearrange("p a b -> p (a b)"))]),
                     reads=["osq2", "onesV"], writes=[("ps", 4)])
                P.op("act", lambda e: e.activation(rs2, psb[4][:, :2 * U].rearrange("p (a b) -> p a b", a=2), AF.Ln,
                                                   bias=RMS_EPS, scale=1.0),
                     reads=[("ps", 4)], writes=["rs2"])
                P.op("act", lambda e: e.activation(rs2, rs2, AF.Exp, scale=-0.5), reads=["rs2"], writes=["rs2"])
                P.op("dve", lambda e: e.tensor_tensor(rs2, pv, rs2, ALU.mult), reads=[("ps", ob), "rs2"], writes=["rs2"])
                b = 1
                for j in range(2):
                    h = 2 * hp + j
                    P.op("pe", mmgroup(psb[b][:, j * U:(j + 1) * U],
                                       [(wmi[:, k, 1536 + h * 128:1536 + (h + 1) * 128], XT[:, k, tok]) for k in ALLC]),
                         reads=xk + [("wmi", 3)], writes=[("ps", b)])
                gv = psb[b][:, :2 * U].rearrange("p (a b) -> p a b", a=2)
                P.op("act", lambda e, gv=gv: e.activation(sgg2, gv, AF.Exp, scale=-1.0), reads=[("ps", b)], writes=["sgg2"])
                P.op("act", lambda e: e.activation(sgg2, sgg2, AF.Ln, bias=1.0, scale=1.0), reads=["sgg2"], writes=["sgg2"])
                P.op("act", lambda e: e.activation(sgg2, sgg2, AF.Exp, scale=-1.0), reads=["sgg2"], writes=["sgg2"])
                P.op("dve", lambda e: e.scalar_tensor_tensor(rs2, rs2, cols[:, C_GN:C_GN + 1], sgg2, ALU.mult, ALU.mult),
                     reads=["rs2", "sgg2", "cols"], writes=["rs2"])
                P.op("dve", lambda e, hp=hp, gv=gv: e.tensor_tensor(MT[:, 2 * hp:2 * hp + 2, :], gv, rs2, ALU.mult),
                     reads=["rs2", ("ps", b)], writes=[("MT", 2 * hp), ("MT", 2 * hp + 1)])

            def Cpool(u, h):
                t0 = u * U
                tok = slice(t0, t0 + U)
                xk = RK("XT", ALLC, t0, U)
                w = POOL_W[h]
                b = nbC()
                if u == 0:
                    P.op("pe", mmgroup(psb[b][:, :U], [(wmi[:, k, 2048 + h * 128:2048 + (h + 1) * 128], XT[:, k, tok]) for k in ALLC]),
                         reads=xk + [("wmi", 4)], writes=[("ps", b)])
                    P.op("dve", lambda e: e.memset(VP[0][:, 0:16], 0.0), writes=[("VP", 0)])
                    P.op("act", lambda e, b=b: e.activation(VP[0][:, 16:16 + U], psb[b][:, :U], AF.Identity),
                         reads=[("ps", b)], writes=[("VP", 0)])
                else:
                    tokh = slice(t0 - 16, t0 + U)
                    P.op("pe", mmgroup(psb[b][:, :U + 16], [(wmi[:, k, 2048 + h * 128:2048 + (h + 1) * 128], XT[:, k, tokh]) for k in ALLC]),
                         reads=RK("XT", ALLC, t0 - 16, U + 16) + [("wmi", 4)], writes=[("ps", b)])
                    P.op("act", lambda e, b=b: e.activation(VP[0][:, 0:16 + U], psb[b][:, :U + 16], AF.Identity),
                         reads=[("ps", b)], writes=[("VP", 0)])
                src = 0
                step = 1
                while step < w:
                    dst = 1 if src != 1 else 2
                    P.op("dve", lambda e, src=src, dst=dst, step=step: e.tensor_tensor(
                        VP[dst][:, step:16 + U], VP[src][:, step:16 + U], VP[src][:, 0:16 + U - step], ALU.add),
                         reads=[("VP", src)], writes=[("VP", dst)])
                    src = dst
                    step *= 2
                P.op("dve", lambda e, src=src, w=w: e.scalar_tensor_tensor(
                    pooled, VP[src][:, 16:16 + U], 1.0 / w, VP[0][:, 16:16 + U], ALU.mult, ALU.subtract),
                     reads=[("VP", src), ("VP", 0)], writes=["pooled"])
                if u == 0:
                    P.op("dve", lambda e, src=src, h=h: e.tensor_tensor(pfix, VP[src][:, 16:32], rcfix[:, h, :], ALU.mult),
                         reads=[("VP", src), ("rcfix", h)], writes=["pfix"])
                    P.op("dve", lambda e: e.tensor_tensor(pooled[:, 0:16], pfix, VP[0][:, 16:32], ALU.subtract),
                         reads=["pfix", ("VP", 0), "pooled"], writes=["pooled"])
                b = nbC()
                P.op("pe", mmgroup(psb[b][:, :U], [(poolw[:, h * 128:(h + 1) * 128], pooled)]),
                     reads=["poolw", "pooled"], writes=[("ps", b)])
                P.op("act", lambda e, b=b, h=h: e.activation(MT[:, 4 + h, :], psb[b][:, :U], AF.Identity,
                                                             scale=cols[:, C_PSC + h:C_PSC + h + 1]),
                     reads=[("ps", b), "cols"], writes=[("MT", 4 + h)])

            def Dout(u):
                t0 = u * U
                tok = slice(t0, t0 + U)
                for m in range(KC):
                    b = (1, 4)[m % 2]
                    P.op("pe", mmgroup(psb[b][:, :U], [(wmo[:, k, m * 128:(m + 1) * 128], MT[:, k, :]) for k in ALLC]),
                         reads=[("MT", k) for k in ALLC] + [("wmo", m // 4)], writes=[("ps", b)])
                    rk = RK("RT", [m], t0, U)
                    P.op("dve", lambda e, m=m, tok=tok, b=b: e.scalar_tensor_tensor(
                        RT[:, m, tok], RT[:, m, tok], ALPHA, psb[b][:, :U], ALU.mult, ALU.add),
                         reads=[("ps", b)] + rk, writes=rk)

            P.interleave([pre, prologue])
            P.op("dve", lambda e: e.memset(pfix, 0.0),
                 writes=["mean", "m2", "rstd", "zsq", "sgt", "sgt2", "C2", "E2", "F2", "L2", "F_", "L_", "C_", "E_", "pfix", ("VP", 1)])
            pool_done = {}
            for i in range(NUU + 1):
                streams = []

                def s1(i=i):
                    if 1 <= i:
                        Cnorm(i - 1, 0)
                        Cnorm(i - 1, 1)
                    if i < NUU:
                        for c in range(NCH):
                            Bchunk(i, c)
                    if 1 <= i:
                        P.stream_wait(lambda i=i: pool_done.get(i, False))
                        Dout(i - 1)
                streams.append(s1)

                def s2(i=i):
                    if i + 1 < NUU:
                        V(i + 1)
                        A1any(i + 1, H4)
                if i + 1 < NUU:
                    streams.append(s2)

                def s3(i=i):
                    for h in H4:
                        Cpool(i - 1, h)
                    pool_done[i] = True
                    if i >= 2:
                        layernorm((i - 2) * U, U, gcol, bcol)
                if i >= 1:
                    streams.append(s3)
                P.interleave(streams)
            return lambda: layernorm((NUU - 1) * U, U, gcol, bcol, cast_eng="dve")

        if stop_after >= 2:
            tail = mixer(C_LN + 16, C_LN + 24, pre=tail)

        def attention(gcol, bcol, pre=None):
            P.new_phase()
            A.off = PH
            wq = A.bf16(KC, D)
            wo_ = A.bf16(KC, D)
            kT = A.bf16(KC, NMEM)
            vm = A.bf16(2, D)
            qT0 = A.bf16(KC, TG)
            R0 = A.off
            memT = A.bf16(KC, NMEM)
            mstage = A.f32(2, D)
            wk = A.bf16(KC, D)
            wv = A.bf16(KC, D)
            def prep():
                for b in range(2):
                    wload(f"wq{b}", wq[:, :, b * 512:(b + 1) * 512], wq_d[b].rearrange("p (k n) -> p k n", k=KC), ("wq", b))
                for b in range(2):
                    wload(f"wk{b}", wk[:, :, b * 512:(b + 1) * 512], wk_d[b].rearrange("p (k n) -> p k n", k=KC), ("wk", b))
                for b in range(2):
                    wload(f"wv{b}", wv[:, :, b * 512:(b + 1) * 512], wv_d[b].rearrange("p (k n) -> p k n", k=KC), ("wv", b))
                for b in range(2):
                    wload(f"wox{b}", wo_[:, :, b * 512:(b + 1) * 512], wo_d[b].rearrange("p (k n) -> p k n", k=KC), ("wo_", b))
                for m in range(KC):
                    b = nextbank()
                    P.op("pe", mmgroup(psb[b][:, :TG], [(wq[:, k, m * 128:(m + 1) * 128], XT[:, k, 0:TG]) for k in ALLC]),
                         reads=RK("XT", ALLC, 0, TG) + [("wq", m // 4)], writes=[("ps", b)])
                    P.op("act", lambda e, m=m, b=b: e.activation(qT0[:, m, :], psb[b][:, :TG], AF.Identity, scale=1.0 / 16.0),
                         reads=[("ps", b)], writes=[("qT", 0, m)])
                for ti in range(2):
                    P.dma("sp", f"mst{ti}", mstage[:, ti, :], mem_d[ti * 128:(ti + 1) * 128, :], writes=[("mst", ti)])
                    for hb in range(2):
                        b = nextbank()

                        def trm(e, ti=ti, hb=hb, b=b):
                            ins = None
                            for q in range(4):
                                c = hb * 4 + q
                                ins = e.transpose(psb[b][:, q * 128:(q + 1) * 128], mstage[:, ti, c * 128:(c + 1) * 128], identF)
                            return ins
                        P.op("pe", trm, reads=[("mst", ti), "identF"], writes=[("ps", b)])
                        P.op("act", lambda e, ti=ti, hb=hb, b=b: e.activation(
                            memT[:, hb * 4:hb * 4 + 4, ti * 128:(ti + 1) * 128], psb[b].rearrange("p (a b) -> p a b", a=4), AF.Identity),
                             reads=[("ps", b)], writes=[("memT", ti, hb)])
                memk = [("memT", ti, hb) for ti in range(2) for hb in range(2)]
                for dc in range(KC):
                    b = nextbank()
                    P.op("pe", mmgroup(psb[b][:, :NMEM], [(wk[:, k, dc * 128:(dc + 1) * 128], memT[:, k, :]) for k in ALLC]),
                         reads=memk + [("wk", dc // 4)], writes=[("ps", b)])
                    P.op("act", lambda e, dc=dc, b=b: e.activation(kT[:, dc, :], psb[b][:, :NMEM], AF.Identity),
                         reads=[("ps", b)], writes=[("kT", dc)])
                for mi in range(2):
                    for nh in range(2):
                        b = nextbank()
                        P.op("pe", mmgroup(psb[b], [(memT[:, k, mi * 128:(mi + 1) * 128], wv[:, k, nh * 512:(nh + 1) * 512]) for k in ALLC]),
                             reads=memk + [("wv", nh)], writes=[("ps", b)])
                        P.op("dve", lambda e, mi=mi, nh=nh, b=b: e.tensor_copy(vm[:, mi, nh * 512:(nh + 1) * 512], psb[b]),
                             reads=[("ps", b)], writes=[("vm", mi, nh)])

            P.interleave([pre, prep])
            P.new_phase()
            A.off = R0
            NTG = S // TG
            qT = [qT0, A.bf16(KC, TG)]
            oT = [A.bf16(KC, TG), A.bf16(KC, TG)]
            ET = [[A.bf16(TG), A.bf16(TG)] for _ in range(4)]
            rec = [A.f32(TG), A.f32(TG)]
            rq = [0]
            rp = [0]
            rs_ = [0]

            def nbQ():
                rq[0] ^= 1
                return (2, 7)[rq[0]]

            HN = TG // 2 if TG >= 512 else TG

            def lnq(tg):
                for t0 in range(tg * TG, (tg + 1) * TG, HN):
                    layernorm(t0, HN, gcol, bcol, cast_eng="dve")

            def nbP():
                rp[0] ^= 1
                return (0, 1)[rp[0]]

            def nbS():
                rs_[0] ^= 1
                return (4, 5)[rs_[0]]

            def Qp(tg):
                t0 = tg * TG
                tok = slice(t0, t0 + TG)
                xk = RK("XT", ALLC, t0, TG)
                q = qT[tg % 2]
                for m in range(KC):
                    b = nbQ()
                    P.op("pe", mmgroup(psb[b][:, :TG], [(wq[:, k, m * 128:(m + 1) * 128], XT[:, k, tok]) for k in ALLC]),
                         reads=xk + [("wq", m // 4)], writes=[("ps", b)])
                    P.op("act", lambda e, m=m, b=b, q=q: e.activation(q[:, m, :], psb[b][:, :TG], AF.Identity, scale=1.0 / 16.0),
                         reads=[("ps", b)], writes=[("qT", tg % 2, m)])

            def Hd(tg):
                q = qT[tg % 2]
                o = oT[tg % 2]
                for h in range(4):
                    for mi in range(2):
                        sb_ = nbS()
                        P.op("pe", mmgroup(psb[sb_][:, :TG], [(kT[:, 2 * h + dd, mi * 128:(mi + 1) * 128], q[:, 2 * h + dd, :]) for dd in range(2)]),
                             reads=[("kT", 2 * h), ("kT", 2 * h + 1), ("qT", tg % 2, 2 * h), ("qT", tg % 2, 2 * h + 1)], writes=[("ps", sb_)])
                        P.op("act", lambda e, h=h, mi=mi, sb_=sb_: e.activation(ET[h][mi], psb[sb_][:, :TG], AF.Exp),
                             reads=[("ps", sb_)], writes=[("ET", h, mi)])
                for h in range(4):
                    i = h % 2
                    P.op("pe", mmgroup(psb[3][:, :TG], [(ones1, ET[h][mi]) for mi in range(2)]),
                         reads=[("ET", h, 0), ("ET", h, 1), "ones1"], writes=[("ps", 3)])
                    P.op("act", lambda e, i=i: e.activation(rec[i], psb[3][:, :TG], AF.Ln), reads=[("ps", 3)], writes=[("rec", i)])
                    P.op("act", lambda e, i=i: e.activation(rec[i], rec[i], AF.Exp, scale=-1.0), reads=[("rec", i)], writes=[("rec", i)])
                    for dd in range(2):
                        dc = 2 * h + dd
                        b = nbP()
                        P.op("pe", mmgroup(psb[b][:, :TG], [(vm[:, mi, dc * 128:(dc + 1) * 128], ET[h][mi]) for mi in range(2)]),
                             reads=[("ET", h, 0), ("ET", h, 1), ("vm", 0, dc // 4), ("vm", 1, dc // 4)], writes=[("ps", b)])
                        P.op("dve", lambda e, dc=dc, b=b, i=i, o=o: e.tensor_tensor(o[:, dc, :], psb[b][:, :TG], rec[i], ALU.mult),
                             reads=[("ps", b), ("rec", i)], writes=[("oT", tg % 2, dc)])

            def Op(tg):
                t0 = tg * TG
                tok = slice(t0, t0 + TG)
                o = oT[tg % 2]
                for m in range(KC):
                    b = nbQ()
                    P.op("pe", mmgroup(psb[b][:, :TG], [(wo_[:, k, m * 128:(m + 1) * 128], o[:, k, :]) for k in ALLC]),
                         reads=[("oT", tg % 2, k) for k in ALLC] + [("wo_", m // 4)], writes=[("ps", b)])
                    rk = RK("RT", [m], t0, TG)
                    P.op("dve", lambda e, m=m, tok=tok, b=b: e.scalar_tensor_tensor(
                        RT[:, m, tok], RT[:, m, tok], ALPHA, psb[b][:, :TG], ALU.mult, ALU.add),
                         reads=[("ps", b)] + rk, writes=rk)

            for tg in range(NTG + 1):
                streams = []
                if tg < NTG:
                    streams.append(lambda tg=tg: Hd(tg))

                def qo(tg=tg):
                    if tg >= 1:
                        Op(tg - 1)
                    if tg + 1 < NTG:
                        Qp(tg + 1)
                if tg >= 1 or tg + 1 < NTG:
                    streams.append(qo)
                if tg >= 2:
                    streams.append(lambda tg=tg: lnq(tg - 2))
                P.interleave(streams)
            return lambda: lnq(NTG - 1)

        if stop_after >= 3:
            tail = attention(C_LN + 32, C_LN + 40, pre=tail)
        if stop_after >= 4:
            tail = ffn(1, C_LN + 48, C_LN + 56, pre=tail, newphase=True)

        P.new_phase()
        A.off = PH
        ostage = [A.f32(D), A.f32(D)]
        otoks = []
        if "O" in DBG:
            for tt in range(NT):
                otoks.append(P.dma("sp", f"out{tt % 2}", out_d[tt * 128:(tt + 1) * 128, :].rearrange("p (c t) -> p c t", c=8),
                                   RT[:, :, tt * 128:(tt + 1) * 128], reads=RK("RT", ALLC, tt * 128, 128)))
        def store_tiles(tiles):
          for tt in tiles:
              sl = tt % 2
              for hb in range(2):
                  b = nextbank()

                  def tro(e, tt=tt, hb=hb, b=b):
                      ins = None
                      for q in range(4):
                          c = hb * 4 + q
                          ins = e.transpose(psb[b][:, q * 128:(q + 1) * 128], RT[:, c, tt * 128:(tt + 1) * 128], identF)
                      return ins
                  P.op("pe", tro, reads=RK("RT", list(range(hb * 4, hb * 4 + 4)), tt * 128, 128) + ["identF"], writes=[("ps", b)])
                  if hb == 0:
                      P.op("dve", lambda e, sl=sl, b=b: e.tensor_copy(ostage[sl][:, 0:512], psb[b]),
                           reads=[("ps", b)], writes=[("ost", sl, 0)])
                  else:
                      P.op("act", lambda e, sl=sl, b=b: e.activation(ostage[sl][:, 512:1024], psb[b], AF.Identity),
                           reads=[("ps", b)], writes=[("ost", sl, 1)])
              otoks.append(P.dma("sp", f"out{sl}", out_d[tt * 128:(tt + 1) * 128, :], ostage[sl],
                                 reads=[("ost", sl, 0), ("ost", sl, 1)]))

        if stop_after >= 4 and HALF % 256 == 0 and HALF >= 512:
            g4, b4 = C_LN + 48, C_LN + 56
            pieces = list(range(HALF, S, 256))
            lnp = lambda t0: (lambda: layernorm(t0, 256, g4, b4))
            tl = lambda t0, n: range(t0 // 128, (t0 + n) // 128)
            P.interleave([lambda: [lnp(pieces[0])(), lnp(pieces[1])()], lambda: store_tiles(range(NT // 2))])
            stored, lndone = HALF, HALF + 512
            for k in range(2, len(pieces)):
                P.interleave([lnp(pieces[k]), lambda a=stored, b=lndone: store_tiles(range(a // 128, b // 128))])
                stored, lndone = lndone, lndone + 256
            store_tiles(range(stored // 128, NT))
        elif tail is not None:
            P.interleave([tail, lambda: store_tiles(range(NT // 2))], weights=[1, 1])
            store_tiles(range(NT // 2, NT))
        else:
            store_tiles(range(NT // 2))
            store_tiles(range(NT // 2, NT))
        P.final_wait("sp", otoks[-2:])
        P.run(st)
    nc._in_names = _names
    return nc


def _blk_cols(w, nblk, width):
    K, N = w.shape
    kc = K // 128
    a = w.reshape(kc, 128, nblk, width).transpose(2, 1, 0, 3)
    return np.ascontiguousarray(a.reshape(nblk, 128, kc * width))


def prep_weights(inp):
    f = lambda a: np.asarray(a, dtype=np.float32)
    out = {}
    for i, nm in ((1, "w_ffn1"), (2, "w_ffn2")):
        w_in = f(inp[nm + "_in"])[0]
        g = w_in[:, :DFF].reshape(KC, 128, JC // 2, 2, 128)
        u = w_in[:, DFF:].reshape(KC, 128, JC // 2, 2, 128)
        blk = np.concatenate([g, u], axis=3)
        blk = blk.transpose(2, 1, 0, 3, 4).reshape(JC // 2, 128, KC * 512)
        out[f"wf{i}_in"] = np.ascontiguousarray(blk)
        w_out = f(inp[nm + "_out"])[0]
        out[f"wf{i}_out"] = _blk_cols(w_out, 8, 128)
    out["wmi"] = _blk_cols(f(inp["w_mix_in"])[0], 5, 512)
    out["wmo"] = _blk_cols(f(inp["w_mix_out"])[0], 2, 512)
    for nm, k in (("wq", "xa_wq"), ("wk", "xa_wk"), ("wv", "xa_wv"), ("wo", "xa_wo")):
        out[nm] = _blk_cols(f(inp[k])[0], 2, 512)
    pw = f(inp["pool_w"])[0]
    out["poolw"] = np.ascontiguousarray(pw.transpose(1, 0, 2).reshape(128, 512))
    cols = np.zeros((128, NCOLS), np.float32)
    col8 = lambda v: f(v).reshape(-1, 128).T
    for i, nm in enumerate(["ln1_g", "ln1_b", "ln2_g", "ln2_b", "ln3_g", "ln3_b", "ln4_g", "ln4_b"]):
        cols[:, C_LN + 8 * i:C_LN + 8 * i + 8] = col8(inp[nm][0])
    cols[:, C_PSC:C_PSC + 4] = col8(inp["pool_scale"][0])
    cols[:, C_GN] = f(inp["hgrn_gnorm"])[0]
    lb = f(inp["hgrn_lb"])
    cols[:, C_LBA:C_LBA + 4] = col8(lb[0])
    cols[:, C_LBB:C_LBB + 4] = col8(lb[1])
    out["cols"] = cols
    return out


_NC_CACHE = {}


def kernel(**inputs):
    x = np.asarray(inputs["x"], dtype=np.float32)
    mem = np.asarray(inputs["mem"], dtype=np.float32)
    B, S, _ = x.shape
    w = prep_weights(inputs)
    key = (S,)
    if key not in _NC_CACHE:
        _NC_CACHE[key] = build(S=S)
    nc = _NC_CACHE[key]
    in_maps = []
    for b in range(B):
        m = dict(w)
        m["x"] = np.ascontiguousarray(x[b])
        m["mem"] = np.ascontiguousarray(mem[b])
        in_maps.append(m)
    res = run_bass_kernel_spmd(nc, in_maps, core_ids=list(range(B)))
    return np.stack([np.asarray(r["out"], dtype=np.float32) for r in res.results], axis=0)
```

```python
from contextlib import ExitStack
import numpy as np
import concourse.bass as bass
import concourse.mybir as mybir
from concourse.bass_utils import run_bass_kernel_spmd

F32 = mybir.dt.float32
BF16 = mybir.dt.bfloat16
I32 = mybir.dt.int32
AF = mybir.ActivationFunctionType
ALU = mybir.AluOpType

D = 1024
KC = 8
DFF = 2816
JC = 22
NMEM = 256
ALPHA = 2.0 ** 0.25
LN_EPS = 1e-5
RMS_EPS = 1e-6
POOL_W = (2, 4, 8, 16)
NCOLS = 80
C_LN = 0
C_PSC = 64
C_GN = 68
C_LBA = 69
C_LBB = 73


class Prog:
    ENGS = ("pe", "act", "dve", "pool", "sp")

    def __init__(self, nc):
        self.nc = nc
        self.ops = {e: [] for e in self.ENGS}
        self.count = {e: 0 for e in self.ENGS}
        self.sem = {}
        self.dsem = {}
        self.last_w = {}
        self.readers = {}
        self.waited = {e: {} for e in self.ENGS}
        self.guard = []
        self.seen = set()
        self.n_waits = 0
        self.fuse_waits = True

    def _yield(self):
        il = getattr(self, "_il", None)
        if il is None:
            return
        import threading
        i = il["tl"].__dict__.get("idx")
        if i is None:
            return
        il["left"][i] -= 1
        if il["left"][i] > 0:
            return
        il["main"].release()
        il["sems"][i].acquire()

    def stream_wait(self, cond):
        while not cond():
            il = getattr(self, "_il", None)
            i = il["tl"].__dict__.get("idx") if il is not None else None
            if i is None:
                raise RuntimeError("stream_wait outside an interleave / condition never set")
            il["main"].release()
            il["sems"][i].acquire()

    def interleave(self, fns, weights=None):
        import threading
        fns = [f for f in fns if f is not None]
        if not fns:
            return
        n = len(fns)
        weights = list(weights or [1] * n)
        il = dict(tl=threading.local(), sems=[threading.Semaphore(0) for _ in range(n)],
                  main=threading.Semaphore(0), left=[0] * n, alive=[True] * n, err=[])
        self._il = il

        def runner(i):
            il["tl"].idx = i
            il["sems"][i].acquire()
            try:
                fns[i]()
            except BaseException as ex:
                il["err"].append(ex)
            il["alive"][i] = False
            il["tl"].idx = None
            il["main"].release()

        ths = [threading.Thread(target=runner, args=(i,), daemon=True) for i in range(n)]
        for t in ths:
            t.start()
        while any(il["alive"]):
            for i in range(n):
                if il["alive"][i]:
                    il["left"][i] = weights[i]
                    il["sems"][i].release()
                    il["main"].acquire()
                    if il["err"]:
                        self._il = None
                        raise il["err"][0]
        for t in ths:
            t.join()
        self._il = None

    def _sem_for(self, tok):
        if tok[0] == "eng":
            return self.sem[tok[1]], tok[2]
        return self.dsem[tok[1]][0], tok[2]

    def new_phase(self):
        g = {}
        for d in (self.last_w,):
            for t in d.values():
                n = t[0] + ":" + t[1]
                if n not in g or g[n][2] < t[2]:
                    g[n] = t
        for lst in self.readers.values():
            for t in lst:
                n = t[0] + ":" + t[1]
                if n not in g or g[n][2] < t[2]:
                    g[n] = t
        self.guard = list(g.values())
        self.seen = set()

    def _collect(self, eng, reads, writes):
        deps = []
        for r in reads:
            t = self.last_w.get(r)
            if t is not None:
                deps.append(t)
        for w in writes:
            t = self.last_w.get(w)
            if t is not None:
                deps.append(t)
            deps.extend(self.readers.get(w, ()))
            if w not in self.seen:
                self.seen.add(w)
                deps.extend(self.guard)
        need = {}
        for t in deps:
            if t[0] == "eng" and t[1] == eng and eng == "pe":
                continue
            name = t[0] + ":" + t[1]
            v = t[2]
            if self.waited[eng].get(name, 0) >= v:
                continue
            if name not in need or need[name][2] < v:
                need[name] = t
        for name, t in need.items():
            self.waited[eng][name] = t[2]
        return list(need.values())

    def _commit(self, tok, reads, writes):
        for r in reads:
            self.readers.setdefault(r, []).append(tok)
        for w in writes:
            self.last_w[w] = tok
            self.readers[w] = []

    def op(self, eng, fn, reads=(), writes=()):
        waits = self._collect(eng, reads, writes)
        self.count[eng] += 1
        seq = self.count[eng]
        self.n_waits += len(waits)

        def emit(e, fn=fn, waits=waits, eng=eng):
            ws = [self._sem_for(t) for t in waits]
            fuse = bool(ws) and eng != "pe" and self.fuse_waits
            for sv in (ws[:-1] if fuse else ws):
                e.wait_ge(*sv)
            ins = fn(e)
            if fuse:
                ins._wait_ge(*ws[-1])
            ins.then_inc(self.sem[eng], 1)

        self.ops[eng].append(emit)
        tok = ("eng", eng, seq)
        self._commit(tok, reads, writes)
        self._yield()
        return tok

    def dma(self, eng, key, out, in_, reads=(), writes=(), **kw):
        waits = self._collect(eng, reads, writes)
        if key not in self.dsem:
            self.dsem[key] = [None, 0]
        self.dsem[key][1] += 16
        cnt = self.dsem[key][1]
        self.n_waits += len(waits)

        def emit(e, waits=waits, key=key, out=out, in_=in_, kw=kw):
            for t in waits:
                e.wait_ge(*self._sem_for(t))
            e.dma_start(out=out, in_=in_, **kw).then_inc(self.dsem[key][0], 16)

        self.ops[eng].append(emit)
        tok = ("dma", key, cnt)
        self._commit(tok, reads, writes)
        self._yield()
        return tok

    def final_wait(self, eng, toks):
        def emit(e, toks=list(toks)):
            for t in toks:
                e.wait_ge(*self._sem_for(t))
        self.ops[eng].append(emit)

    def run(self, stack):
        nc = self.nc
        for e in self.ENGS:
            self.sem[e] = stack.enter_context(nc.semaphore("prog_" + e))
        for k in self.dsem:
            self.dsem[k][0] = stack.enter_context(nc.semaphore("dma_" + k))
        block = stack.enter_context(nc.Block())

        @block.tensor
        def _(e):
            for f in self.ops["pe"]:
                f(e)

        @block.scalar
        def _(e):
            for f in self.ops["act"]:
                f(e)

        @block.vector
        def _(e):
            for f in self.ops["dve"]:
                f(e)

        @block.gpsimd
        def _(e):
            for f in self.ops["pool"]:
                f(e)

        @block.sync
        def _(e):
            for f in self.ops["sp"]:
                f(e)


def build(S=2048, TG=512, U=256, stop_after=4):
    assert S % (2 * TG) == 0 and TG % U == 0 and U % 128 == 0
    nc = bass.Bass("TRN2", target_bir_lowering=False)
    NT = S // 128
    HALF = S // 2
    NTGH = HALF // TG

    import os
    _u = "u" in os.environ.get("KDBG", "")
    _names = []

    def din(name, shape):
        if _u and stop_after == 0 and name not in ("x", "cols"):
            return None
        _names.append(name)
        return nc.dram_tensor(name, list(shape), F32, kind="ExternalInput").ap()

    x_d = din("x", [S, D])
    mem_d = din("mem", [NMEM, D])
    wf_in_d = [din("wf1_in", [11, 128, KC * 512]), din("wf2_in", [11, 128, KC * 512])]
    wf_out_d = [din("wf1_out", [8, 128, JC * 128]), din("wf2_out", [8, 128, JC * 128])]
    wmi_d = din("wmi", [5, 128, KC * 512])
    wmo_d = din("wmo", [2, 128, KC * 512])
    wq_d = din("wq", [2, 128, KC * 512])
    wk_d = din("wk", [2, 128, KC * 512])
    wv_d = din("wv", [2, 128, KC * 512])
    wo_d = din("wo", [2, 128, KC * 512])
    poolw_d = din("poolw", [128, 512])
    cols_d = din("cols", [128, NCOLS])
    out_d = nc.dram_tensor("out", [S, D], F32, kind="ExternalOutput").ap()

    st = ExitStack()
    with st:
        AW = 53200
        arena = st.enter_context(nc.sbuf_tensor("arena", [128, AW], F32))
        psb = [st.enter_context(nc.psum_tensor(f"ps{i}", [128, 512], F32))[:] for i in range(8)]
        P = Prog(nc)

        class Alloc:
            def __init__(self, base):
                self.off = base

            def f32(self, *shape):
                n = int(np.prod(shape))
                v = arena[:, self.off:self.off + n]
                self.off += n
                assert self.off <= AW, f"arena overflow {self.off}"
                if len(shape) == 2:
                    return v.rearrange("p (a b) -> p a b", a=shape[0])
                if len(shape) == 3:
                    return v.rearrange("p (a b c) -> p a b c", a=shape[0], b=shape[1])
                return v

            def bf16(self, *shape):
                n = int(np.prod(shape))
                assert n % 2 == 0
                v = arena[:, self.off:self.off + n // 2].bitcast(BF16)
                self.off += n // 2
                assert self.off <= AW, f"arena overflow {self.off}"
                if len(shape) == 2:
                    return v.rearrange("p (a b) -> p a b", a=shape[0])
                if len(shape) == 3:
                    return v.rearrange("p (a b c) -> p a b c", a=shape[0], b=shape[1])
                return v

        A = Alloc(0)
        RT = A.f32(KC, S)
        XT = A.bf16(KC, S)
        identF = A.f32(128)
        onesD = A.bf16(128)
        onesV = A.bf16(128)
        ones1 = A.bf16(128)
        mask2 = A.f32(128)
        cols = A.f32(NCOLS)
        lbc = A.f32(4)
        omlc = A.f32(4)
        rcfix = A.f32(4, 16)
        iot = A.f32(16)
        ioti = arena[:, A.off:A.off + 16].bitcast(I32)
        A.off += 16
        scanmask = A.f32(U)
        mean_off = A.off
        mean = A.f32(TG)
        m2 = A.f32(TG)
        rstd = A.f32(TG)
        zsq_off = A.off
        zsq = A.bf16(KC, TG)
        PH = A.off

        def RK(name, cs, t0, n):
            return [(name, c, tt) for c in cs for tt in range(t0 // 128, (t0 + n + 127) // 128)]

        ALLC = list(range(KC))
        bank_rr = [0]

        def nextbank():
            b = bank_rr[0]
            bank_rr[0] = (b + 1) % 4
            return b

        def mmgroup(out, pairs):
            def fn(e, out=out, pairs=pairs):
                n = len(pairs)
                ins = None
                for i, (l, r) in enumerate(pairs):
                    ins = e.matmul(out, l, r, start=(i == 0), stop=(i == n - 1))
                return ins
            return fn

        def wload(key, dst, src, reskey, rows=None):
            P.dma("pool", key, dst, src, writes=[reskey])

        P.dma("sp", "cols", cols, cols_d, writes=["cols"])
        P.op("pool", lambda e: e.memset(identF, 0.0), writes=["identF"])
        P.op("pool", lambda e: e.affine_select(out=identF, in_=identF, pattern=[[-1, 128]], compare_op=ALU.not_equal,
                                               fill=1.0, base=0, channel_multiplier=1),
             reads=["identF"], writes=["identF"])
        import os
        DBG = os.environ.get("KDBG", "")
        P.op("pool", lambda e: e.memset(onesD, 1.0 / 1024.0), writes=["onesD"])
        P.op("pool", lambda e: e.memset(onesV, 1.0 / 128.0), writes=["onesV"])
        P.op("pool", lambda e: e.memset(ones1, 1.0), writes=["ones1"])
        if "m" not in DBG:
            P.op("pool", lambda e: e.memset(mask2, 1.0), writes=["mask2"])
            P.op("pool", lambda e: e.affine_select(out=mask2, in_=mask2, pattern=[[1, 128]], compare_op=ALU.is_ge,
                                                   fill=0.0, base=0, channel_multiplier=-1),
                 reads=["mask2"], writes=["mask2"])
            P.op("pool", lambda e: e.memset(mask2[0:64, 64:128], 0.0), reads=["mask2"], writes=["mask2"])
        if "s" not in DBG:
            P.op("pool", lambda e: e.memset(scanmask, 1.0), writes=["scanmask"])
            P.op("pool", lambda e: e.memset(scanmask.rearrange("p (a b) -> p a b", b=64)[:, :, 0:1], 0.0),
                 reads=["scanmask"], writes=["scanmask"])
        if "i" not in DBG:
            P.op("pool", lambda e: e.iota(ioti, pattern=[[1, 16]], base=1, channel_multiplier=0), writes=["ioti"])
            P.op("pool", lambda e: e.tensor_copy(iot, ioti), reads=["ioti"], writes=["iot"])
            for g, w in enumerate(POOL_W):
                P.op("dve", lambda e, g=g, w=w: e.tensor_scalar_min(rcfix[:, g, :], iot, float(w)),
                     reads=["iot"], writes=[("rcfix", g)])
                P.op("dve", lambda e, g=g: e.reciprocal(rcfix[:, g, :], rcfix[:, g, :]),
                     reads=[("rcfix", g)], writes=[("rcfix", g)])
        if "l" not in DBG:
            P.op("dve", lambda e: e.tensor_tensor(lbc, cols[:, C_LBA:C_LBA + 4], cols[:, C_LBB:C_LBB + 4], ALU.subtract),
                 reads=["cols"], writes=["lbc"])
            P.op("act", lambda e: e.activation(lbc, lbc, AF.Sigmoid), reads=["lbc"], writes=["lbc"])
            P.op("dve", lambda e: e.tensor_scalar(omlc, lbc, -1.0, 1.0, ALU.mult, ALU.add), reads=["lbc"], writes=["omlc"])

        zsq_flat = zsq.rearrange("p c t -> p (c t)")

        def layernorm(t0, n, gcol, bcol, cast_eng="dve"):
            zsq = zsq_flat[:, :KC * n].rearrange("p (c t) -> p c t", c=KC)
            tok = slice(t0, t0 + n)
            rk = RK("RT", ALLC, t0, n)
            xk = RK("XT", ALLC, t0, n)
            P.op(cast_eng, lambda e: e.tensor_copy(XT[:, :, tok], RT[:, :, tok]), reads=rk, writes=xk)
            P.op("act", lambda e: e.activation(zsq[:, :, :n], RT[:, :, tok], AF.Square), reads=rk, writes=["zsq"])
            P.op("pe", mmgroup(psb[6][:, :n], [(onesD, XT[:, c, tok]) for c in ALLC]),
                 reads=xk + ["onesD"], writes=[("ps", 6)])
            if n <= 256:
                msq_ps, msq_key = psb[6][:, 256:256 + n], ("ps", 6)
            else:
                msq_ps, msq_key = psb[7][:, :n], ("ps", 7)
            P.op("pe", mmgroup(msq_ps, [(onesD, zsq[:, c, :n]) for c in ALLC]),
                 reads=["zsq", "onesD"], writes=[msq_key])
            P.op("act", lambda e: e.activation(mean[:, :n], psb[6][:, :n], AF.Identity),
                 reads=[("ps", 6)], writes=["mean"])
            P.op("dve", lambda e: e.tensor_tensor(m2[:, :n], mean[:, :n], mean[:, :n], ALU.mult),
                 reads=["mean"], writes=["m2"])
            P.op("dve", lambda e: e.tensor_tensor(m2[:, :n], msq_ps, m2[:, :n], ALU.subtract),
                 reads=[msq_key, "m2"], writes=["m2"])
            P.op("act", lambda e: e.activation(m2[:, :n], m2[:, :n], AF.Ln, bias=LN_EPS, scale=1.0),
                 reads=["m2"], writes=["m2"])
            P.op("act", lambda e: e.activation(rstd[:, :n], m2[:, :n], AF.Exp, scale=-0.5),
                 reads=["m2"], writes=["rstd"])
            P.op("dve", lambda e: e.tensor_tensor(RT[:, :, tok], RT[:, :, tok],
                                                  mean[:, :n].unsqueeze(1).to_broadcast([128, KC, n]), ALU.subtract),
                 reads=rk + ["mean"], writes=rk)
            P.op("dve", lambda e: e.tensor_tensor(RT[:, :, tok], RT[:, :, tok],
                                                  rstd[:, :n].unsqueeze(1).to_broadcast([128, KC, n]), ALU.mult),
                 reads=rk + ["rstd"], writes=rk)
            for c in ALLC:
                P.op("act", lambda e, c=c: e.activation(RT[:, c, tok], RT[:, c, tok], AF.Identity,
                                                        scale=cols[:, gcol + c:gcol + c + 1],
                                                        bias=cols[:, bcol + c:bcol + c + 1]),
                     reads=RK("RT", [c], t0, n) + ["cols"], writes=RK("RT", [c], t0, n))
            P.op(cast_eng, lambda e: e.tensor_copy(XT[:, :, tok], RT[:, :, tok]), reads=rk, writes=xk)

        A.off = PH
        xstage = [A.f32(D), A.f32(D)]

        def load_x(tiles):
          for tt in tiles:
              sl = tt % 2
              P.dma("sp", f"xs{sl}", xstage[sl], x_d[tt * 128:(tt + 1) * 128, :], writes=[("xs", sl)])
              for hb in range(2):
                  b = 6 + hb

                  def trf(e, sl=sl, hb=hb, b=b):
                      ins = None
                      for q in range(4):
                          c = hb * 4 + q
                          ins = e.transpose(psb[b][:, q * 128:(q + 1) * 128], xstage[sl][:, c * 128:(c + 1) * 128], identF)
                      return ins
                  P.op("pe", trf, reads=[("xs", sl), "identF"], writes=[("ps", b)])
                  cs = list(range(hb * 4, hb * 4 + 4))
                  pv = psb[b].rearrange("p (a b) -> p a b", a=4)
                  P.op("dve", lambda e, hb=hb, tt=tt, pv=pv: e.tensor_copy(RT[:, hb * 4:hb * 4 + 4, tt * 128:(tt + 1) * 128], pv),
                       reads=[("ps", b)], writes=RK("RT", cs, tt * 128, 128))
                  if "A" not in DBG:
                      P.op("act", lambda e, hb=hb, tt=tt: e.activation(XT[:, hb * 4:hb * 4 + 4, tt * 128:(tt + 1) * 128],
                                                                       RT[:, hb * 4:hb * 4 + 4, tt * 128:(tt + 1) * 128], AF.Identity),
                           reads=RK("RT", cs, tt * 128, 128), writes=RK("XT", cs, tt * 128, 128))


        load_x(range(NT // 2))

        def ffn(idx, gcol, bcol, pre=None, newphase=False):
            if newphase:
                P.new_phase()
            A.off = PH + 2 * D
            H = A.bf16(JC, HALF)
            wi = [A.bf16(KC, 512), A.bf16(KC, 512)]
            wo = [A.bf16(JC, 128), A.bf16(JC, 128)]
            sg = [A.f32(TG), A.f32(TG)]
            w_in_d, w_out_d = wf_in_d[idx], wf_out_d[idx]
            cnt = [0, 0, 0]

            def Aph(half):
                t0h = half * HALF
                for jj in range(JC // 2):
                    sl = cnt[0] % 2
                    cnt[0] += 1
                    wload(f"wi{sl}", wi[sl].rearrange("p k n -> p (k n)").rearrange("p (r e) -> p r e", e=2048),
                          w_in_d[jj].rearrange("p (r e) -> p r e", e=2048), ("wi", sl))
                    for jl in range(2):
                        j = 2 * jj + jl
                        for tg in range(NTGH):
                            t0 = t0h + tg * TG
                            tok = slice(t0, t0 + TG)
                            i = cnt[1] % 2
                            cnt[1] += 1
                            gb, ub = i, 2 + i
                            xk = RK("XT", ALLC, t0, TG)
                            P.op("pe", mmgroup(psb[gb][:, :TG], [(wi[sl][:, k, jl * 128:(jl + 1) * 128], XT[:, k, tok]) for k in ALLC]),
                                 reads=[("wi", sl)] + xk, writes=[("ps", gb)])
                            P.op("pe", mmgroup(psb[ub][:, :TG], [(wi[sl][:, k, 256 + jl * 128:256 + (jl + 1) * 128], XT[:, k, tok]) for k in ALLC]),
                                 reads=[("wi", sl)] + xk, writes=[("ps", ub)])
                            P.op("act", lambda e, i=i, gb=gb: e.activation(sg[i], psb[gb][:, :TG], AF.Silu),
                                 reads=[("ps", gb)], writes=[("sg", i)])
                            P.op("dve", lambda e, i=i, ub=ub, j=j, tg=tg: e.scalar_tensor_tensor(
                                H[:, j, tg * TG:(tg + 1) * TG], sg[i], 0.5, psb[ub][:, :TG], ALU.mult, ALU.mult),
                                 reads=[("sg", i), ("ps", ub)], writes=[("H", j, tg)])

            def Bph(half):
                t0h = half * HALF
                for m in range(KC):
                    sl = cnt[2] % 2
                    cnt[2] += 1
                    wload(f"wo{sl}", wo[sl].rearrange("p j n -> p (j n)").rearrange("p (r e) -> p r e", e=1408),
                          w_out_d[m].rearrange("p (r e) -> p r e", e=1408), ("wo", sl))
                    for tg in range(NTGH):
                        t0 = t0h + tg * TG
                        tok = slice(t0, t0 + TG)
                        ob = 4 + (m * NTGH + tg) % 2
                        P.op("pe", mmgroup(psb[ob][:, :TG], [(wo[sl][:, j, :], H[:, j, tg * TG:(tg + 1) * TG]) for j in range(JC)]),
                             reads=[("wo", sl)] + [("H", j, tg) for j in range(JC)], writes=[("ps", ob)])
                        rk = RK("RT", [m], t0, TG)
                        P.op("dve", lambda e, m=m, tok=tok, ob=ob: e.scalar_tensor_tensor(
                            RT[:, m, tok], RT[:, m, tok], ALPHA, psb[ob][:, :TG], ALU.mult, ALU.add),
                             reads=[("ps", ob)] + rk, writes=rk)

            def LNs(half):
                for tg in range(NTGH):
                    layernorm(half * HALF + tg * TG, TG, gcol, bcol)

            if pre is not None:
                P.interleave([pre, lambda: Aph(0)], weights=[1, 6])
            else:
                Aph(0)
            Bph(0)
            P.interleave([lambda: LNs(0), lambda: Aph(1)], weights=[1, 6])
            Bph(1)
            return lambda: LNs(1)

        tail = None
        if stop_after >= 1:
            tail = ffn(0, C_LN + 0, C_LN + 8, pre=lambda: load_x(range(NT // 2, NT)))
        else:
            load_x(range(NT // 2, NT))

        def mixer(gcol, bcol, pre=None):
            P.new_phase()
            A.off = PH
            NTU = U // 128
            NCH = U // 64
            NUU = S // U
            wmi = A.bf16(KC, 2560)
            wmo = A.bf16(KC, D)
            poolw = A.bf16(512)
            vtok = [A.bf16(NTU, 512), A.bf16(NTU, 512)]
            MT = A.bf16(KC, U)
            fl_off = A.off
            F_, L_, C_, E_ = A.f32(U), A.f32(U), A.f32(U), A.f32(U)
            PAIR = (TG >= 2 * U)
            if PAIR:
                F2 = arena[:, fl_off:fl_off + 2 * U].rearrange("p (a b) -> p a b", a=2)
                L2 = arena[:, fl_off + 2 * U:fl_off + 4 * U].rearrange("p (a b) -> p a b", a=2)
                zu = zsq_off + KC * U // 2
                C2_d = arena[:, zu:zu + 2 * U].rearrange("p (a b) -> p a b", a=2)
                E2_d = arena[:, zu + 2 * U:zu + 4 * U].rearrange("p (a b) -> p a b", a=2)
                sgt2_d = arena[:, mean_off:mean_off + 2 * TG].rearrange("p (a b) -> p a b", a=2)[:, :, U:2 * U]
            qd4 = [A.bf16(4, U)]
            qd1_off = A.off
            qd4.append(A.bf16(4, U))
            kd4 = [A.bf16(4, U)]
            kd1_off = A.off
            kd4.append(A.bf16(4, U))
            kltok4 = [A.bf16(4, U)]
            kl1_off = A.off
            kltok4.append(A.bf16(4, U))
            EL = [A.f32(4, NCH), A.f32(4, NCH)]
            sgt_default = mean[:, U:2 * U] if TG >= 2 * U else A.f32(U)
            Sf = A.f32(4, 128)
            Sb = A.bf16(4, 128)
            sT4 = A.bf16(4, 128)
            osq2 = A.bf16(2, U)
            rs2 = A.f32(2, U)
            sgg2 = A.f32(2, U)
            VP = [A.f32(16 + U), A.f32(16 + U), A.f32(16 + U)]
            halo = A.f32(4, 16)
            pooled = A.bf16(U)
            pfix = A.f32(16)
            OB = (3, 5)
            rrA = [0]
            rrC = [0]

            def nbA():
                return 0

            def nbC():
                rrC[0] ^= 1
                return (2, 7)[rrC[0]]

            def prologue():
                for b in (1, 0, 2, 3, 4):
                    wload(f"wmi{b}", wmi[:, :, b * 512:(b + 1) * 512], wmi_d[b].rearrange("p (k n) -> p k n", k=KC), ("wmi", b))
                for b in range(2):
                    wload(f"wmo{b}", wmo[:, :, b * 512:(b + 1) * 512], wmo_d[b].rearrange("p (k n) -> p k n", k=KC), ("wmo", b))
                wload("poolw", poolw, poolw_d, "poolw")
                P.op("dve", lambda e: e.memset(Sf, 0.0), writes=["Sf"])
                P.op("dve", lambda e: e.memset(Sb, 0.0), writes=["Sb"])
                P.op("dve", lambda e: e.memset(halo, 0.0), writes=[("halo", h) for h in range(4)])
                P.op("dve", lambda e: e.memset(VP[1], 0.0), writes=[("VP", 1)])
                P.op("dve", lambda e: e.memset(VP[2], 0.0), writes=[("VP", 2)])
                V(0)
                if PAIR:
                    f3 = lambda off: arena[:, off:off + 2 * U].rearrange("p (a b) -> p a b", a=2)
                    alt0 = (f3(kd1_off), f3(qd1_off), f3(kl1_off))
                    A1p(0, 0, alt=alt0)
                    A1p(0, 1, alt=alt0)
                else:
                    for h in range(4):
                        A1(0, h, sgt=VP[1][:, :U], sgk=("VP", 1))

            def V(u):
                vs = vtok[u % 2]
                for ti in range(NTU):
                    b = nbA()
                    ts0 = u * U + ti * 128
                    tsl = slice(ts0, ts0 + 128)
                    P.op("pe", mmgroup(psb[b], [(XT[:, k, tsl], wmi[:, k, 1024:1536]) for k in ALLC]),
                         reads=RK("XT", ALLC, ts0, 128) + [("wmi", 2)], writes=[("ps", b)])
                    P.op("act", lambda e, ti=ti, b=b, vs=vs: e.activation(vs[:, ti, :], psb[b], AF.Identity),
                         reads=[("ps", b)], writes=[("vtok", u % 2, ti)])

            def A1(u, h, sgt=None, sgk="sgt"):
                sgt = sgt_default if sgt is None else sgt
                s_ = u % 2
                t0 = u * U
                tok = slice(t0, t0 + U)
                xk = RK("XT", ALLC, t0, U)
                b = nbA()
                P.op("pe", mmgroup(psb[b][:, :U], [(wmi[:, k, 512 + h * 128:512 + (h + 1) * 128], XT[:, k, tok]) for k in ALLC]),
                     reads=xk + [("wmi", 1)], writes=[("ps", b)])
                P.op("act", lambda e, b=b: e.activation(F_, psb[b][:, :U], AF.Exp, scale=-1.0), reads=[("ps", b)], writes=["F_"])
                P.op("act", lambda e: e.activation(F_, F_, AF.Ln, bias=1.0, scale=1.0), reads=["F_"], writes=["F_"])
                P.op("act", lambda e: e.activation(F_, F_, AF.Exp, scale=-1.0), reads=["F_"], writes=["F_"])
                P.op("dve", lambda e, h=h: e.tensor_scalar(F_, F_, omlc[:, h:h + 1], lbc[:, h:h + 1], ALU.mult, ALU.add),
                     reads=["F_", "lbc", "omlc"], writes=["F_"])
                P.op("act", lambda e: e.activation(L_, F_, AF.Ln), reads=["F_"], writes=["L_"])
                P.op("dve", lambda e: e.tensor_scalar(F_, F_, -1.0, 1.0, ALU.mult, ALU.add), reads=["F_"], writes=["F_"])
                P.op("dve", lambda e: e.tensor_tensor_scan(C_, scanmask, L_, 0.0, ALU.mult, ALU.add),
                     reads=["L_", "scanmask"], writes=["C_"])
                P.op("act", lambda e: e.activation(E_, C_, AF.Exp), reads=["C_"], writes=["E_"])
                P.op("act", lambda e: e.activation(L_, C_, AF.Exp, scale=-1.0), reads=["C_"], writes=["L_"])
                P.op("dve", lambda e: e.tensor_tensor(L_, F_, L_, ALU.mult), reads=["F_", "L_"], writes=["L_"])
                P.op("pool", lambda e, s_=s_, h=h: e.tensor_copy(kd4[s_][:, h, :], L_),
                     reads=["L_"], writes=[("kd", s_, h)])
                P.op("pool", lambda e, s_=s_, h=h: e.tensor_copy(EL[s_][:, h, :], E_.rearrange("p (a b) -> p a b", b=64)[:, :, 63]),
                     reads=["E_"], writes=[("EL", s_)])
                P.op("dve", lambda e: e.tensor_tensor(
                    C_.rearrange("p (a b) -> p a b", b=64), L_.rearrange("p (a b) -> p a b", b=64),
                    E_.rearrange("p (a b) -> p a b", b=64)[:, :, 63:64].to_broadcast([128, NCH, 64]), ALU.mult),
                     reads=["L_", "E_"], writes=["C_"])
                b = nbA()
                P.op("pe", mmgroup(psb[b][:, :U], [(wmi[:, k, h * 128:(h + 1) * 128], XT[:, k, tok]) for k in ALLC]),
                     reads=xk + [("wmi", 0)], writes=[("ps", b)])
                P.op("act", lambda e, b=b: e.activation(sgt, psb[b][:, :U], AF.Exp, scale=-1.0), reads=[("ps", b)], writes=[sgk])
                P.op("act", lambda e: e.activation(sgt, sgt, AF.Ln, bias=1.0, scale=1.0), reads=[sgk], writes=[sgk])
                P.op("act", lambda e: e.activation(sgt, sgt, AF.Exp, scale=-1.0), reads=[sgk], writes=[sgk])
                P.op("dve", lambda e: e.tensor_tensor(sgt, sgt, E_, ALU.mult), reads=[sgk, "E_"], writes=[sgk])
                P.op("dve", lambda e, s_=s_, h=h, b=b: e.tensor_tensor(qd4[s_][:, h, :], psb[b][:, :U], sgt, ALU.mult),
                     reads=[sgk, ("ps", b)], writes=[("qd", s_, h)])
                b = nbA()

                def trk(e, b=b):
                    ins = None
                    for ti in range(NTU):
                        ins = e.transpose(psb[b][:, ti * 128:(ti + 1) * 128], C_[:, ti * 128:(ti + 1) * 128], identF)
                    return ins
                P.op("pe", trk, reads=["C_", "identF"], writes=[("ps", b)])
                P.op("act", lambda e, b=b, s_=s_, h=h: e.activation(kltok4[s_][:, h, :], psb[b][:, :U], AF.Identity),
                     reads=[("ps", b)], writes=[("kltok", s_, h)])

            def A1p(u, hp, alt=None):
                s_ = u % 2
                t0 = u * U
                tok = slice(t0, t0 + U)
                xk = RK("XT", ALLC, t0, U)
                hh = (2 * hp, 2 * hp + 1)
                if alt is None:
                    C2, E2, sgt2, kC, kE, kS = C2_d, E2_d, sgt2_d, "C2", "E2", "sgt2"
                else:
                    C2, E2, sgt2 = alt
                    kC, kE, kS = "C2alt", "E2alt", "sgt2alt"
                b = nbA()
                for j, h in enumerate(hh):
                    P.op("pe", mmgroup(psb[b][:, j * U:(j + 1) * U], [(wmi[:, k, 512 + h * 128:512 + (h + 1) * 128], XT[:, k, tok]) for k in ALLC]),
                         reads=xk + [("wmi", 1)], writes=[("ps", b)])
                pf = psb[b][:, :2 * U].rearrange("p (a b) -> p a b", a=2)
                P.op("act", lambda e, pf=pf: e.activation(F2, pf, AF.Exp, scale=-1.0), reads=[("ps", b)], writes=["F2"])
                P.op("act", lambda e: e.activation(F2, F2, AF.Ln, bias=1.0, scale=1.0), reads=["F2"], writes=["F2"])
                P.op("act", lambda e: e.activation(F2, F2, AF.Exp, scale=-1.0), reads=["F2"], writes=["F2"])
                for j, h in enumerate(hh):
                    P.op("act", lambda e, j=j, h=h: e.activation(F2[:, j, :], F2[:, j, :], AF.Identity,
                                                                 scale=omlc[:, h:h + 1], bias=lbc[:, h:h + 1]),
                         reads=["F2", "lbc", "omlc"], writes=["F2"])
                P.op("act", lambda e: e.activation(L2, F2, AF.Ln), reads=["F2"], writes=["L2"])
                P.op("act", lambda e: e.activation(F2, F2, AF.Identity, scale=-1.0, bias=1.0), reads=["F2"], writes=["F2"])
                for j in range(2):
                    P.op("dve", lambda e, j=j: e.tensor_tensor_scan(C2[:, j, :], scanmask, L2[:, j, :], 0.0, ALU.mult, ALU.add),
                         reads=["L2", "scanmask"], writes=[kC])
                P.op("act", lambda e: e.activation(E2, C2, AF.Exp), reads=[kC], writes=[kE])
                P.op("act", lambda e: e.activation(L2, C2, AF.Exp, scale=-1.0), reads=[kC], writes=["L2"])
                P.op("dve", lambda e: e.tensor_tensor(L2, F2, L2, ALU.mult), reads=["F2", "L2"], writes=["L2"])
                P.op("pool", lambda e, s_=s_, hp=hp: e.tensor_copy(kd4[s_][:, 2 * hp:2 * hp + 2, :], L2),
                     reads=["L2"], writes=[("kd", s_, 2 * hp), ("kd", s_, 2 * hp + 1)])
                E4 = E2.rearrange("p a (c b) -> p a c b", b=64)
                P.op("pool", lambda e, s_=s_, hp=hp: e.tensor_copy(EL[s_][:, 2 * hp:2 * hp + 2, :], E4[:, :, :, 63]),
                     reads=[kE], writes=[("EL", s_)])
                P.op("dve", lambda e: e.tensor_tensor(
                    C2.rearrange("p a (c b) -> p (a c) b", b=64), L2.rearrange("p a (c b) -> p (a c) b", b=64),
                    E2.rearrange("p a (c b) -> p (a c) b", b=64)[:, :, 63:64].to_broadcast([128, 2 * NCH, 64]), ALU.mult),
                     reads=["L2", kE], writes=[kC])
                b = nbA()
                for j, h in enumerate(hh):
                    P.op("pe", mmgroup(psb[b][:, j * U:(j + 1) * U], [(wmi[:, k, h * 128:(h + 1) * 128], XT[:, k, tok]) for k in ALLC]),
                         reads=xk + [("wmi", 0)], writes=[("ps", b)])
                pq = psb[b][:, :2 * U].rearrange("p (a b) -> p a b", a=2)
                P.op("act", lambda e, pq=pq: e.activation(sgt2, pq, AF.Exp, scale=-1.0), reads=[("ps", b)], writes=[kS])
                P.op("act", lambda e: e.activation(sgt2, sgt2, AF.Ln, bias=1.0, scale=1.0), reads=[kS], writes=[kS])
                P.op("act", lambda e: e.activation(sgt2, sgt2, AF.Exp, scale=-1.0), reads=[kS], writes=[kS])
                P.op("dve", lambda e: e.tensor_tensor(sgt2, sgt2, E2, ALU.mult), reads=[kS, kE], writes=[kS])
                P.op("dve", lambda e, s_=s_, hp=hp, pq=pq: e.tensor_tensor(qd4[s_][:, 2 * hp:2 * hp + 2, :], pq, sgt2, ALU.mult),
                     reads=[kS, ("ps", b)], writes=[("qd", s_, 2 * hp), ("qd", s_, 2 * hp + 1)])
                b = nbA()

                def trk(e, b=b):
                    ins = None
                    for j in range(2):
                        for ti in range(NTU):
                            ins = e.transpose(psb[b][:, j * U + ti * 128:j * U + (ti + 1) * 128], C2[:, j, ti * 128:(ti + 1) * 128], identF)
                    return ins
                P.op("pe", trk, reads=[kC, "identF"], writes=[("ps", b)])
                P.op("act", lambda e, b=b, s_=s_, hp=hp: e.activation(
                    kltok4[s_][:, 2 * hp:2 * hp + 2, :], psb[b][:, :2 * U].rearrange("p (a b) -> p a b", a=2), AF.Identity),
                     reads=[("ps", b)], writes=[("kltok", s_, 2 * hp), ("kltok", s_, 2 * hp + 1)])

            def A1any(u, hlist):
                hlist = list(hlist)
                if PAIR and len(hlist) % 2 == 0:
                    for hp in sorted(set(h // 2 for h in hlist)):
                        A1p(u, hp)
                else:
                    for h in hlist:
                        A1(u, h)

            H4 = list(range(4))

            def Bchunk(u, c):
                s_ = u % 2
                ti, cc = c // 2, c % 2
                c0 = ti * 128
                q0 = c * 64
                r0 = cc * 64
                if cc == 0:
                    def sc(e):
                        ins = None
                        for h in H4:
                            ins = e.matmul(psb[4][:, h * 128:(h + 1) * 128], kd4[s_][:, h, c0:c0 + 128], qd4[s_][:, h, c0:c0 + 128],
                                           start=True, stop=True)
                        return ins
                    P.op("pe", sc, reads=[("kd", s_, h) for h in H4] + [("qd", s_, h) for h in H4], writes=[("ps", 4)])
                    P.op("dve", lambda e: e.tensor_tensor(sT4, psb[4].rearrange("p (a b) -> p a b", a=4),
                                                          mask2.unsqueeze(1).to_broadcast([128, 4, 128]), ALU.mult),
                         reads=[("ps", 4), "mask2"], writes=["sT4"])

                def pm(e):
                    ins = None
                    for h in H4:
                        ins = e.matmul(psb[4][:, h * 128:(h + 1) * 128], kltok4[s_][r0:r0 + 64, h, c0:c0 + 128],
                                       vtok[s_][r0:r0 + 64, ti, h * 128:(h + 1) * 128], start=True, stop=True)
                    return ins
                P.op("pe", pm, reads=[("kltok", s_, h) for h in H4] + [("vtok", s_, ti)], writes=[("ps", 4)])

                def io(e):
                    ins = None
                    for h in H4:
                        ob = OB[h // 2]
                        col = (h % 2) * U + q0
                        e.matmul(psb[ob][:, col:col + 64], vtok[s_][:, ti, h * 128:(h + 1) * 128], sT4[:, h, r0:r0 + 64],
                                 start=True, stop=False)
                        ins = e.matmul(psb[ob][:, col:col + 64], Sb[:, h, :], qd4[s_][:, h, q0:q0 + 64], start=False, stop=True)
                    return ins
                P.op("pe", io, reads=["sT4", ("vtok", s_, ti), "Sb"] + [("qd", s_, h) for h in H4],
                     writes=[("ps", OB[0]), ("ps", OB[1])])
                P.op("dve", lambda e: e.tensor_tensor(Sf, Sf, EL[s_][:, :, c:c + 1].to_broadcast([128, 4, 128]), ALU.mult),
                     reads=["Sf", ("EL", s_)], writes=["Sf"])
                P.op("dve", lambda e: e.tensor_tensor(Sf, Sf, psb[4].rearrange("p (a b) -> p a b", a=4), ALU.add),
                     reads=["Sf", ("ps", 4)], writes=["Sf"])
                P.op("act", lambda e: e.activation(Sb, Sf, AF.Identity), reads=["Sf"], writes=["Sb"])

            def Cnorm(u, hp):
                t0 = u * U
                tok = slice(t0, t0 + U)
                xk = RK("XT", ALLC, t0, U)
                ob = OB[hp]
                pv = psb[ob][:, :2 * U].rearrange("p (a b) -> p a b", a=2)
                P.op("act", lambda e: e.activation(osq2, pv, AF.Square), reads=[("ps", ob)], writes=["osq2"])
                P.op("pe", mmgroup(psb[4][:, :2 * U], [(onesV, osq2.rearrange("p a b -> p (a b)"))]),
                     reads=["osq2", "onesV"], writes=[("ps", 4)])
                P.op("act", lambda e: e.activation(rs2, psb[4][:, :2 * U].rearrange("p (a b) -> p a b", a=2), AF.Ln,
                                                   bias=RMS_EPS, scale=1.0),
                     reads=[("ps", 4)], writes=["rs2"])
                P.op("act", lambda e: e.activation(rs2, rs2, AF.Exp, scale=-0.5), reads=["rs2"], writes=["rs2"])
                P.op("dve", lambda e: e.tensor_tensor(rs2, pv, rs2, ALU.mult), reads=[("ps", ob), "rs2"], writes=["rs2"])
                b = 1
                for j in range(2):
                    h = 2 * hp + j
                    P.op("pe", mmgroup(psb[b][:, j * U:(j + 1) * U],
                                       [(wmi[:, k, 1536 + h * 128:1536 + (h + 1) * 128], XT[:, k, tok]) for k in ALLC]),
                         reads=xk + [("wmi", 3)], writes=[("ps", b)])
                gv = psb[b][:, :2 * U].rearrange("p (a b) -> p a b", a=2)
                P.op("act", lambda e, gv=gv: e.activation(sgg2, gv, AF.Exp, scale=-1.0), reads=[("ps", b)], writes=["sgg2"])
                P.op("act", lambda e: e.activation(sgg2, sgg2, AF.Ln, bias=1.0, scale=1.0), reads=["sgg2"], writes=["sgg2"])
                P.op("act", lambda e: e.activation(sgg2, sgg2, AF.Exp, scale=-1.0), reads=["sgg2"], writes=["sgg2"])
                P.op("dve", lambda e: e.scalar_tensor_tensor(rs2, rs2, cols[:, C_GN:C_GN + 1], sgg2, ALU.mult, ALU.mult),
                     reads=["rs2", "sgg2", "cols"], writes=["rs2"])
                P.op("dve", lambda e, hp=hp, gv=gv: e.tensor_tensor(MT[:, 2 * hp:2 * hp + 2, :], gv, rs2, ALU.mult),
                     reads=["rs2", ("ps", b)], writes=[("MT", 2 * hp), ("MT", 2 * hp + 1)])

            def Cpool(u, h):
                t0 = u * U
                tok = slice(t0, t0 + U)
                xk = RK("XT", ALLC, t0, U)
                w = POOL_W[h]
                b = nbC()
                if u == 0:
                    P.op("pe", mmgroup(psb[b][:, :U], [(wmi[:, k, 2048 + h * 128:2048 + (h + 1) * 128], XT[:, k, tok]) for k in ALLC]),
                         reads=xk + [("wmi", 4)], writes=[("ps", b)])
                    P.op("dve", lambda e: e.memset(VP[0][:, 0:16], 0.0), writes=[("VP", 0)])
                    P.op("act", lambda e, b=b: e.activation(VP[0][:, 16:16 + U], psb[b][:, :U], AF.Identity),
                         reads=[("ps", b)], writes=[("VP", 0)])
                else:
                    tokh = slice(t0 - 16, t0 + U)
                    P.op("pe", mmgroup(psb[b][:, :U + 16], [(wmi[:, k, 2048 + h * 128:2048 + (h + 1) * 128], XT[:, k, tokh]) for k in ALLC]),
                         reads=RK("XT", ALLC, t0 - 16, U + 16) + [("wmi", 4)], writes=[("ps", b)])
                    P.op("act", lambda e, b=b: e.activation(VP[0][:, 0:16 + U], psb[b][:, :U + 16], AF.Identity),
                         reads=[("ps", b)], writes=[("VP", 0)])
                src = 0
                step = 1
                while step < w:
                    dst = 1 if src != 1 else 2
                    P.op("dve", lambda e, src=src, dst=dst, step=step: e.tensor_tensor(
                        VP[dst][:, step:16 + U], VP[src][:, step:16 + U], VP[src][:, 0:16 + U - step], ALU.add),
                         reads=[("VP", src)], writes=[("VP", dst)])
                    src = dst
                    step *= 2
                P.op("dve", lambda e, src=src, w=w: e.scalar_tensor_tensor(
                    pooled, VP[src][:, 16:16 + U], 1.0 / w, VP[0][:, 16:16 + U], ALU.mult, ALU.subtract),
                     reads=[("VP", src), ("VP", 0)], writes=["pooled"])
                if u == 0:
                    P.op("dve", lambda e, src=src, h=h: e.tensor_tensor(pfix, VP[src][:, 16:32], rcfix[:, h, :], ALU.mult),
                         reads=[("VP", src), ("rcfix", h)], writes=["pfix"])
                    P.op("dve", lambda e: e.tensor_tensor(pooled[:, 0:16], pfix, VP[0][:, 16:32], ALU.subtract),
                         reads=["pfix", ("VP", 0), "pooled"], writes=["pooled"])
                b = nbC()
                P.op("pe", mmgroup(psb[b][:, :U], [(poolw[:, h * 128:(h + 1) * 128], pooled)]),
                     reads=["poolw", "pooled"], writes=[("ps", b)])
                P.op("act", lambda e, b=b, h=h: e.activation(MT[:, 4 + h, :], psb[b][:, :U], AF.Identity,
                                                             scale=cols[:, C_PSC + h:C_PSC + h + 1]),
                     reads=[("ps", b), "cols"], writes=[("MT", 4 + h)])

            def Dout(u):
                t0 = u * U
                tok = slice(t0, t0 + U)
                for m in range(KC):
                    b = (1, 4)[m % 2]
                    P.op("pe", mmgroup(psb[b][:, :U], [(wmo[:, k, m * 128:(m + 1) * 128], MT[:, k, :]) for k in ALLC]),
                         reads=[("MT", k) for k in ALLC] + [("wmo", m // 4)], writes=[("ps", b)])
                    rk = RK("RT", [m], t0, U)
                    P.op("dve", lambda e, m=m, tok=tok, b=b: e.scalar_tensor_tensor(
                        RT[:, m, tok], RT[:, m, tok], ALPHA, psb[b][:, :U], ALU.mult, ALU.add),
                         reads=[("ps", b)] + rk, writes=rk)

            P.interleave([pre, prologue])
            P.op("dve", lambda e: e.memset(pfix, 0.0),
                 writes=["mean", "m2", "rstd", "zsq", "sgt", "sgt2", "C2", "E2", "F2", "L2", "F_", "L_", "C_", "E_", "pfix", ("VP", 1),
                         "C2alt", "E2alt", "sgt2alt"] + [(nm, 1, h) for nm in ("qd", "kd", "kltok") for h in range(4)])
            pool_done = {}
            for i in range(NUU + 1):
                streams = []

                def s1(i=i):
                    if 1 <= i:
                        Cnorm(i - 1, 0)
                        Cnorm(i - 1, 1)
                    if i < NUU:
                        for c in range(NCH):
                            Bchunk(i, c)
                    if 1 <= i:
                        P.stream_wait(lambda i=i: pool_done.get(i, False))
                        Dout(i - 1)
                streams.append(s1)

                def s2(i=i):
                    if i + 1 < NUU:
                        V(i + 1)
                        A1any(i + 1, H4)
                if i + 1 < NUU:
                    streams.append(s2)

                def s3(i=i):
                    for h in H4:
                        Cpool(i - 1, h)
                    pool_done[i] = True
                    if i >= 2:
                        layernorm((i - 2) * U, U, gcol, bcol)
                if i >= 1:
                    streams.append(s3)
                P.interleave(streams)
            return lambda: layernorm((NUU - 1) * U, U, gcol, bcol, cast_eng="dve")

        if stop_after >= 2:
            tail = mixer(C_LN + 16, C_LN + 24, pre=tail)

        def attention(gcol, bcol, pre=None):
            P.new_phase()
            A.off = PH
            wq = A.bf16(KC, D)
            wo_ = A.bf16(KC, D)
            kT = A.bf16(KC, NMEM)
            vm = A.bf16(2, D)
            qT0 = A.bf16(KC, TG)
            R0 = A.off
            memT = A.bf16(KC, NMEM)
            mstage = A.f32(2, D)
            wk = A.bf16(KC, D)
            wv = A.bf16(KC, D)
            def prep():
                for b in range(2):
                    wload(f"wq{b}", wq[:, :, b * 512:(b + 1) * 512], wq_d[b].rearrange("p (k n) -> p k n", k=KC), ("wq", b))
                for b in range(2):
                    wload(f"wk{b}", wk[:, :, b * 512:(b + 1) * 512], wk_d[b].rearrange("p (k n) -> p k n", k=KC), ("wk", b))
                for b in range(2):
                    wload(f"wv{b}", wv[:, :, b * 512:(b + 1) * 512], wv_d[b].rearrange("p (k n) -> p k n", k=KC), ("wv", b))
                for b in range(2):
                    wload(f"wox{b}", wo_[:, :, b * 512:(b + 1) * 512], wo_d[b].rearrange("p (k n) -> p k n", k=KC), ("wo_", b))
                for m in range(KC):
                    b = nextbank()
                    P.op("pe", mmgroup(psb[b][:, :TG], [(wq[:, k, m * 128:(m + 1) * 128], XT[:, k, 0:TG]) for k in ALLC]),
                         reads=RK("XT", ALLC, 0, TG) + [("wq", m // 4)], writes=[("ps", b)])
                    P.op("act", lambda e, m=m, b=b: e.activation(qT0[:, m, :], psb[b][:, :TG], AF.Identity, scale=1.0 / 16.0),
                         reads=[("ps", b)], writes=[("qT", 0, m)])
                for ti in range(2):
                    P.dma("sp", f"mst{ti}", mstage[:, ti, :], mem_d[ti * 128:(ti + 1) * 128, :], writes=[("mst", ti)])
                    for hb in range(2):
                        b = nextbank()

                        def trm(e, ti=ti, hb=hb, b=b):
                            ins = None
                            for q in range(4):
                                c = hb * 4 + q
                                ins = e.transpose(psb[b][:, q * 128:(q + 1) * 128], mstage[:, ti, c * 128:(c + 1) * 128], identF)
                            return ins
                        P.op("pe", trm, reads=[("mst", ti), "identF"], writes=[("ps", b)])
                        P.op("act", lambda e, ti=ti, hb=hb, b=b: e.activation(
                            memT[:, hb * 4:hb * 4 + 4, ti * 128:(ti + 1) * 128], psb[b].rearrange("p (a b) -> p a b", a=4), AF.Identity),
                             reads=[("ps", b)], writes=[("memT", ti, hb)])
                memk = [("memT", ti, hb) for ti in range(2) for hb in range(2)]
                for dc in range(KC):
                    b = nextbank()
                    P.op("pe", mmgroup(psb[b][:, :NMEM], [(wk[:, k, dc * 128:(dc + 1) * 128], memT[:, k, :]) for k in ALLC]),
                         reads=memk + [("wk", dc // 4)], writes=[("ps", b)])
                    P.op("act", lambda e, dc=dc, b=b: e.activation(kT[:, dc, :], psb[b][:, :NMEM], AF.Identity),
                         reads=[("ps", b)], writes=[("kT", dc)])
                for mi in range(2):
                    for nh in range(2):
                        b = nextbank()
                        P.op("pe", mmgroup(psb[b], [(memT[:, k, mi * 128:(mi + 1) * 128], wv[:, k, nh * 512:(nh + 1) * 512]) for k in ALLC]),
                             reads=memk + [("wv", nh)], writes=[("ps", b)])
                        P.op("dve", lambda e, mi=mi, nh=nh, b=b: e.tensor_copy(vm[:, mi, nh * 512:(nh + 1) * 512], psb[b]),
                             reads=[("ps", b)], writes=[("vm", mi, nh)])

            P.interleave([pre, prep])
            P.new_phase()
            A.off = R0
            NTG = S // TG
            qT = [qT0, A.bf16(KC, TG)]
            oT = [A.bf16(KC, TG), A.bf16(KC, TG)]
            ET = [[A.bf16(TG), A.bf16(TG)] for _ in range(4)]
            rec = [A.f32(TG), A.f32(TG)]
            rq = [0]
            rp = [0]
            rs_ = [0]

            def nbQ():
                rq[0] ^= 1
                return (2, 7)[rq[0]]

            HN = TG // 2 if TG >= 512 else TG

            def lnq(tg):
                for t0 in range(tg * TG, (tg + 1) * TG, HN):
                    layernorm(t0, HN, gcol, bcol, cast_eng="dve")

            def nbP():
                rp[0] ^= 1
                return (0, 1)[rp[0]]

            def nbS():
                rs_[0] ^= 1
                return (4, 5)[rs_[0]]

            def Qp(tg):
                t0 = tg * TG
                tok = slice(t0, t0 + TG)
                xk = RK("XT", ALLC, t0, TG)
                q = qT[tg % 2]
                for m in range(KC):
                    b = nbQ()
                    P.op("pe", mmgroup(psb[b][:, :TG], [(wq[:, k, m * 128:(m + 1) * 128], XT[:, k, tok]) for k in ALLC]),
                         reads=xk + [("wq", m // 4)], writes=[("ps", b)])
                    P.op("act", lambda e, m=m, b=b, q=q: e.activation(q[:, m, :], psb[b][:, :TG], AF.Identity, scale=1.0 / 16.0),
                         reads=[("ps", b)], writes=[("qT", tg % 2, m)])

            def Hd(tg):
                q = qT[tg % 2]
                o = oT[tg % 2]
                for h in range(4):
                    for mi in range(2):
                        sb_ = nbS()
                        P.op("pe", mmgroup(psb[sb_][:, :TG], [(kT[:, 2 * h + dd, mi * 128:(mi + 1) * 128], q[:, 2 * h + dd, :]) for dd in range(2)]),
                             reads=[("kT", 2 * h), ("kT", 2 * h + 1), ("qT", tg % 2, 2 * h), ("qT", tg % 2, 2 * h + 1)], writes=[("ps", sb_)])
                        P.op("act", lambda e, h=h, mi=mi, sb_=sb_: e.activation(ET[h][mi], psb[sb_][:, :TG], AF.Exp),
                             reads=[("ps", sb_)], writes=[("ET", h, mi)])
                for h in range(4):
                    i = h % 2
                    P.op("pe", mmgroup(psb[3][:, :TG], [(ones1, ET[h][mi]) for mi in range(2)]),
                         reads=[("ET", h, 0), ("ET", h, 1), "ones1"], writes=[("ps", 3)])
                    P.op("act", lambda e, i=i: e.activation(rec[i], psb[3][:, :TG], AF.Ln), reads=[("ps", 3)], writes=[("rec", i)])
                    P.op("act", lambda e, i=i: e.activation(rec[i], rec[i], AF.Exp, scale=-1.0), reads=[("rec", i)], writes=[("rec", i)])
                    for dd in range(2):
                        dc = 2 * h + dd
                        b = nbP()
                        P.op("pe", mmgroup(psb[b][:, :TG], [(vm[:, mi, dc * 128:(dc + 1) * 128], ET[h][mi]) for mi in range(2)]),
                             reads=[("ET", h, 0), ("ET", h, 1), ("vm", 0, dc // 4), ("vm", 1, dc // 4)], writes=[("ps", b)])
                        P.op("dve", lambda e, dc=dc, b=b, i=i, o=o: e.tensor_tensor(o[:, dc, :], psb[b][:, :TG], rec[i], ALU.mult),
                             reads=[("ps", b), ("rec", i)], writes=[("oT", tg % 2, dc)])

            def Op(tg):
                t0 = tg * TG
                tok = slice(t0, t0 + TG)
                o = oT[tg % 2]
                for m in range(KC):
                    b = nbQ()
                    P.op("pe", mmgroup(psb[b][:, :TG], [(wo_[:, k, m * 128:(m + 1) * 128], o[:, k, :]) for k in ALLC]),
                         reads=[("oT", tg % 2, k) for k in ALLC] + [("wo_", m // 4)], writes=[("ps", b)])
                    rk = RK("RT", [m], t0, TG)
                    P.op("dve", lambda e, m=m, tok=tok, b=b: e.scalar_tensor_tensor(
                        RT[:, m, tok], RT[:, m, tok], ALPHA, psb[b][:, :TG], ALU.mult, ALU.add),
                         reads=[("ps", b)] + rk, writes=rk)

            for tg in range(NTG + 1):
                streams = []
                if tg < NTG:
                    streams.append(lambda tg=tg: Hd(tg))

                def qo(tg=tg):
                    if tg >= 1:
                        Op(tg - 1)
                    if tg + 1 < NTG:
                        Qp(tg + 1)
                if tg >= 1 or tg + 1 < NTG:
                    streams.append(qo)
                if tg >= 2:
                    streams.append(lambda tg=tg: lnq(tg - 2))
                P.interleave(streams)
            return lambda: lnq(NTG - 1)

        if stop_after >= 3:
            tail = attention(C_LN + 32, C_LN + 40, pre=tail)
        if stop_after >= 4:
            tail = ffn(1, C_LN + 48, C_LN + 56, pre=tail, newphase=True)

        P.new_phase()
        A.off = PH
        ostage = [A.f32(D), A.f32(D)]
        otoks = []
        if "O" in DBG:
            for tt in range(NT):
                otoks.append(P.dma("sp", f"out{tt % 2}", out_d[tt * 128:(tt + 1) * 128, :].rearrange("p (c t) -> p c t", c=8),
                                   RT[:, :, tt * 128:(tt + 1) * 128], reads=RK("RT", ALLC, tt * 128, 128)))
        def store_tiles(tiles):
          for tt in tiles:
              sl = tt % 2
              for hb in range(2):
                  b = nextbank()

                  def tro(e, tt=tt, hb=hb, b=b):
                      ins = None
                      for q in range(4):
                          c = hb * 4 + q
                          ins = e.transpose(psb[b][:, q * 128:(q + 1) * 128], RT[:, c, tt * 128:(tt + 1) * 128], identF)
                      return ins
                  P.op("pe", tro, reads=RK("RT", list(range(hb * 4, hb * 4 + 4)), tt * 128, 128) + ["identF"], writes=[("ps", b)])
                  if hb == 0:
                      P.op("dve", lambda e, sl=sl, b=b: e.tensor_copy(ostage[sl][:, 0:512], psb[b]),
                           reads=[("ps", b)], writes=[("ost", sl, 0)])
                  else:
                      P.op("act", lambda e, sl=sl, b=b: e.activation(ostage[sl][:, 512:1024], psb[b], AF.Identity),
                           reads=[("ps", b)], writes=[("ost", sl, 1)])
              otoks.append(P.dma("sp", f"out{sl}", out_d[tt * 128:(tt + 1) * 128, :], ostage[sl],
                                 reads=[("ost", sl, 0), ("ost", sl, 1)]))

        if stop_after >= 4 and HALF % 256 == 0 and HALF >= 512:
            g4, b4 = C_LN + 48, C_LN + 56
            pieces = list(range(HALF, S, 256))
            lnp = lambda t0: (lambda: layernorm(t0, 256, g4, b4))
            tl = lambda t0, n: range(t0 // 128, (t0 + n) // 128)
            P.interleave([lambda: [lnp(pieces[0])(), lnp(pieces[1])()], lambda: store_tiles(range(NT // 2))])
            stored, lndone = HALF, HALF + 512
            for k in range(2, len(pieces)):
                P.interleave([lnp(pieces[k]), lambda a=stored, b=lndone: store_tiles(range(a // 128, b // 128))])
                stored, lndone = lndone, lndone + 256
            store_tiles(range(stored // 128, NT))
        elif tail is not None:
            P.interleave([tail, lambda: store_tiles(range(NT // 2))], weights=[1, 1])
            store_tiles(range(NT // 2, NT))
        else:
            store_tiles(range(NT // 2))
            store_tiles(range(NT // 2, NT))
        P.final_wait("sp", otoks[-2:])
        P.run(st)
    nc._in_names = _names
    return nc


def _blk_cols(w, nblk, width):
    K, N = w.shape
    kc = K // 128
    a = w.reshape(kc, 128, nblk, width).transpose(2, 1, 0, 3)
    return np.ascontiguousarray(a.reshape(nblk, 128, kc * width))


def prep_weights(inp):
    f = lambda a: np.asarray(a, dtype=np.float32)
    out = {}
    for i, nm in ((1, "w_ffn1"), (2, "w_ffn2")):
        w_in = f(inp[nm + "_in"])[0]
        g = w_in[:, :DFF].reshape(KC, 128, JC // 2, 2, 128)
        u = w_in[:, DFF:].reshape(KC, 128, JC // 2, 2, 128)
        blk = np.concatenate([g, u], axis=3)
        blk = blk.transpose(2, 1, 0, 3, 4).reshape(JC // 2, 128, KC * 512)
        out[f"wf{i}_in"] = np.ascontiguousarray(blk)
        w_out = f(inp[nm + "_out"])[0]
        out[f"wf{i}_out"] = _blk_cols(w_out, 8, 128)
    out["wmi"] = _blk_cols(f(inp["w_mix_in"])[0], 5, 512)
    out["wmo"] = _blk_cols(f(inp["w_mix_out"])[0], 2, 512)
    for nm, k in (("wq", "xa_wq"), ("wk", "xa_wk"), ("wv", "xa_wv"), ("wo", "xa_wo")):
        out[nm] = _blk_cols(f(inp[k])[0], 2, 512)
    pw = f(inp["pool_w"])[0]
    out["poolw"] = np.ascontiguousarray(pw.transpose(1, 0, 2).reshape(128, 512))
    cols = np.zeros((128, NCOLS), np.float32)
    col8 = lambda v: f(v).reshape(-1, 128).T
    for i, nm in enumerate(["ln1_g", "ln1_b", "ln2_g", "ln2_b", "ln3_g", "ln3_b", "ln4_g", "ln4_b"]):
        cols[:, C_LN + 8 * i:C_LN + 8 * i + 8] = col8(inp[nm][0])
    cols[:, C_PSC:C_PSC + 4] = col8(inp["pool_scale"][0])
    cols[:, C_GN] = f(inp["hgrn_gnorm"])[0]
    lb = f(inp["hgrn_lb"])
    cols[:, C_LBA:C_LBA + 4] = col8(lb[0])
    cols[:, C_LBB:C_LBB + 4] = col8(lb[1])
    out["cols"] = cols
    return out


_NC_CACHE = {}


def kernel(**inputs):
    x = np.asarray(inputs["x"], dtype=np.float32)
    mem = np.asarray(inputs["mem"], dtype=np.float32)
    B, S, _ = x.shape
    w = prep_weights(inputs)
    key = (S,)
    if key not in _NC_CACHE:
        _NC_CACHE[key] = build(S=S)
    nc = _NC_CACHE[key]
    in_maps = []
    for b in range(B):
        m = dict(w)
        m["x"] = np.ascontiguousarray(x[b])
        m["mem"] = np.ascontiguousarray(mem[b])
        in_maps.append(m)
    res = run_bass_kernel_spmd(nc, in_maps, core_ids=list(range(B)))
    return np.stack([np.asarray(r["out"], dtype=np.float32) for r in res.results], axis=0)
```
